# Optimizing a Trainium2 kernel written in Bass

```python
import math
import jax, jax.numpy as jnp
from jax import lax
import numpy as np

D_MODEL = 2048
BATCH = 4
SEQ = 4096
DEPTH = 4

DIFF_HEADS = 4
DIFF_DK = 64
DIFF_DV = 2 * DIFF_DK
MLA_HEADS = 4
MLA_Q_RANK = 512
MLA_KV_RANK = 256
MLA_NOPE = 128
MLA_ROPE = 64
MLA_DV = 128
ROPE_THETA = 10000.0
SWA_Q_HEADS = 8
SWA_KV_HEADS = 2
SWA_DH = 64
SWA_WINDOW = 128
SWA_BLOCK = 128
MOBA_HEADS = 4
MOBA_DH = 128
MOBA_BLOCK = 256
MOBA_TOPK = 3
MOBA_QCHUNK = 64
Q_BLOCK = 128
REL_BUCKETS = 32
REL_MAX_DIST = 128
BIAS_A0, BIAS_A1 = 0, DIFF_HEADS
BIAS_C0, BIAS_C1 = DIFF_HEADS, DIFF_HEADS + SWA_Q_HEADS
BIAS_D0, BIAS_D1 = DIFF_HEADS + SWA_Q_HEADS, DIFF_HEADS + SWA_Q_HEADS + MOBA_HEADS
BIAS_HEADS = BIAS_D1
N_BRANCH = 4
BRANCH_W = 512
FFN_DIM = 5632
CONV_W = 3
EPS = 1e-6
NEG_INF = -1e30

IN_SPLITS = (
    DIFF_HEADS * 2 * DIFF_DK, DIFF_HEADS * 2 * DIFF_DK, DIFF_HEADS * DIFF_DV,
    MLA_Q_RANK, MLA_KV_RANK, MLA_ROPE,
    SWA_Q_HEADS * SWA_DH, SWA_KV_HEADS * SWA_DH, SWA_KV_HEADS * SWA_DH,
    MOBA_HEADS * MOBA_DH, MOBA_HEADS * MOBA_DH, MOBA_HEADS * MOBA_DH,
    N_BRANCH * D_MODEL,
)
IN_COLS = sum(IN_SPLITS)

kernel_name = 'hybrid_gated_diff_mla_swa_moba_convglu'

F32 = jnp.float32


def rms_norm(x, g):
    xf = x.astype(F32)
    y = xf * lax.rsqrt(jnp.mean(xf * xf, axis=-1, keepdims=True) + EPS)
    return (y * g.astype(F32)).astype(x.dtype)


def t5_bucket(rel):
    n = jnp.maximum(rel, 0)
    max_exact = REL_BUCKETS // 2
    nf = jnp.maximum(n, 1).astype(F32)
    large = max_exact + (jnp.log(nf / max_exact) / math.log(REL_MAX_DIST / max_exact)
                         * (REL_BUCKETS - max_exact)).astype(jnp.int32)
    return jnp.where(n < max_exact, n, jnp.minimum(large, REL_BUCKETS - 1))


def rope(x, positions):
    dr = x.shape[-1]
    inv = ROPE_THETA ** (-jnp.arange(0, dr, 2, dtype=F32) / dr)
    ang = positions.astype(F32)[:, None] * inv[None, :]
    cos, sin = jnp.cos(ang), jnp.sin(ang)
    xf = x.astype(F32)
    x1, x2 = xf[..., : dr // 2], xf[..., dr // 2:]
    return jnp.concatenate([x1 * cos - x2 * sin, x1 * sin + x2 * cos], axis=-1).astype(x.dtype)


def dense_causal_attention(q, k, v, scale, positions, bias_tab):
    B, H, S, _ = q.shape
    kidx = jnp.arange(S)

    def block(i):
        start = i * Q_BLOCK
        qb = lax.dynamic_slice_in_dim(q, start, Q_BLOCK, axis=2)
        s = jnp.einsum('bhqd,bhkd->bhqk', qb, k, preferred_element_type=F32) * scale
        qidx = start + jnp.arange(Q_BLOCK)
        if bias_tab is not None:
            qpos = lax.dynamic_slice_in_dim(positions, start, Q_BLOCK)
            bkt = t5_bucket(qpos[:, None] - positions[None, :])
            s = s + jnp.moveaxis(bias_tab[bkt].astype(F32), -1, 0)[None]
        s = jnp.where(kidx[None, :] <= qidx[:, None], s, NEG_INF)
        p = jax.nn.softmax(s, axis=-1)
        return jnp.einsum('bhqk,bhkd->bhqd', p.astype(v.dtype), v)

    out = lax.map(block, jnp.arange(S // Q_BLOCK))
    return jnp.moveaxis(out, 0, 2).reshape(B, H, S, v.shape[-1])


def diff_attention(zq, zk, zv, lam_vecs, subln_g, lam_init, positions, bias_tab):
    B, S, _ = zq.shape
    H = DIFF_HEADS
    q = zq.reshape(B, S, H * 2, DIFF_DK).transpose(0, 2, 1, 3)
    k = zk.reshape(B, S, H * 2, DIFF_DK).transpose(0, 2, 1, 3)
    v = zv.reshape(B, S, H, DIFF_DV).transpose(0, 2, 1, 3)
    v = jnp.repeat(v, 2, axis=1)
    o = dense_causal_attention(q, k, v, DIFF_DK ** -0.5, positions, jnp.repeat(bias_tab, 2, axis=1))
    o = o.reshape(B, H, 2, S, DIFF_DV).astype(F32)
    lv = lam_vecs.astype(F32)
    lam = jnp.exp(jnp.sum(lv[0] * lv[1])) - jnp.exp(jnp.sum(lv[2] * lv[3])) + lam_init
    o = o[:, :, 0] - lam * o[:, :, 1]
    o = rms_norm(o, subln_g) * (1.0 - lam_init)
    return o.transpose(0, 2, 1, 3).reshape(B, S, H * DIFF_DV).astype(zq.dtype)


def mla_attention(zcq, zckv, zkr, gq, w_uq, gkv, w_ukv, positions):
    B, S, _ = zcq.shape
    H = MLA_HEADS
    cq = rms_norm(zcq, gq)
    q = (cq @ w_uq).reshape(B, S, H, MLA_NOPE + MLA_ROPE).transpose(0, 2, 1, 3)
    q = jnp.concatenate([q[..., :MLA_NOPE], rope(q[..., MLA_NOPE:], positions)], axis=-1)
    ckv = rms_norm(zckv, gkv)
    kv = (ckv @ w_ukv).reshape(B, S, H, MLA_NOPE + MLA_DV).transpose(0, 2, 1, 3)
    k_rope = jnp.broadcast_to(rope(zkr, positions)[:, None], (B, H, S, MLA_ROPE))
    k = jnp.concatenate([kv[..., :MLA_NOPE], k_rope], axis=-1)
    v = kv[..., MLA_NOPE:]
    o = dense_causal_attention(q, k, v, (MLA_NOPE + MLA_ROPE) ** -0.5, positions, None)
    return o.transpose(0, 2, 1, 3).reshape(B, S, H * MLA_DV)


def swa_sink_attention(zq, zk, zv, sinks, positions, bias_tab):
    B, S, _ = zq.shape
    KVH, G, L, d = SWA_KV_HEADS, SWA_Q_HEADS // SWA_KV_HEADS, SWA_BLOCK, SWA_DH
    nb = S // L
    q = zq.reshape(B, nb, L, KVH, G, d)

    def band(t):
        tb = jnp.pad(t, ((0, 0), (L, 0), (0, 0), (0, 0))).reshape(B, nb + 1, L, KVH, d)
        return jnp.concatenate([tb[:, :-1], tb[:, 1:]], axis=2)

    k = band(zk.reshape(B, S, KVH, d))
    v = band(zv.reshape(B, S, KVH, d))
    s = jnp.einsum('bnqhgd,bnkhd->bhgnqk', q, k, preferred_element_type=F32) * d ** -0.5
    qidx = jnp.arange(nb)[:, None] * L + jnp.arange(L)[None, :]
    kidx = jnp.arange(nb)[:, None] * L - L + jnp.arange(2 * L)[None, :]
    dist = qidx[:, :, None] - kidx[:, None, :]
    valid = (dist >= 0) & (dist < SWA_WINDOW) & (kidx[:, None, :] >= 0)
    pos_p = jnp.pad(positions, (L, 0))
    bkt = t5_bucket(positions[qidx][:, :, None] - pos_p[kidx + L][:, None, :])
    bias = bias_tab[bkt].astype(F32).reshape(nb, L, 2 * L, KVH, G).transpose(3, 4, 0, 1, 2)
    s = jnp.where(valid, s + bias, NEG_INF)
    sink = sinks.astype(F32).reshape(KVH, G)[:, :, None, None, None]
    m = jnp.maximum(jnp.max(s, axis=-1, keepdims=True), sink)
    p = jnp.exp(s - m)
    p = p / (jnp.sum(p, axis=-1, keepdims=True) + jnp.exp(sink - m))
    o = jnp.einsum('bhgnqk,bnkhd->bnqhgd', p.astype(v.dtype), v)
    return o.reshape(B, S, SWA_Q_HEADS * d)


def moba_attention(zq, zk, zv, positions, bias_tab):
    B, S, _ = zq.shape
    H, d, L, QC = MOBA_HEADS, MOBA_DH, MOBA_BLOCK, MOBA_QCHUNK
    nblk = -(-S // L)
    Sp = nblk * L
    n_sel = min(MOBA_TOPK, nblk - 1)

    def to_heads(t):
        return t.reshape(B, S, H, d).transpose(0, 2, 1, 3)

    pad = ((0, 0), (0, 0), (0, Sp - S), (0, 0))
    q = to_heads(zq)
    k = jnp.pad(to_heads(zk), pad)
    v = jnp.pad(to_heads(zv), pad)
    pos_p = jnp.pad(positions, (0, Sp - S))
    k_blocks = k.reshape(B, H, nblk, L, d)
    v_blocks = v.reshape(B, H, nblk, L, d)
    k_mean = jnp.mean(k_blocks.astype(F32), axis=3)
    bias_T = bias_tab.astype(F32).T
    b_ix = jnp.arange(B)[:, None, None, None]
    h_ix = jnp.arange(H)[None, :, None, None]
    scale = d ** -0.5

    def chunk(c):
        start = c * QC
        qc = lax.dynamic_slice_in_dim(q, start, QC, axis=2)
        qidx = start + jnp.arange(QC)
        qpos = lax.dynamic_slice_in_dim(positions, start, QC)
        blk = start // L
        ks = lax.dynamic_slice_in_dim(k, blk * L, L, axis=2)
        vs = lax.dynamic_slice_in_dim(v, blk * L, L, axis=2)
        kidx = blk * L + jnp.arange(L)
        s_own = jnp.einsum('bhqd,bhkd->bhqk', qc, ks, preferred_element_type=F32) * scale
        s_own = s_own + bias_T[:, t5_bucket(qpos[:, None] - pos_p[kidx][None, :])][None]
        s_own = jnp.where(kidx[None, :] <= qidx[:, None], s_own, NEG_INF)
        if n_sel == 0:
            p = jax.nn.softmax(s_own, axis=-1)
            return jnp.einsum('bhqk,bhkd->bhqd', p.astype(vs.dtype), vs)
        gate = jnp.einsum('bhqd,bhnd->bhqn', qc.astype(F32), k_mean)
        gate = jnp.where(jnp.arange(nblk) < blk, gate, NEG_INF)
        _, sel = lax.top_k(gate, n_sel)
        sel_ok = jnp.repeat(jnp.arange(n_sel) < blk, L)
        k_sel = k_blocks[b_ix, h_ix, sel].reshape(B, H, QC, n_sel * L, d)
        v_sel = v_blocks[b_ix, h_ix, sel].reshape(B, H, QC, n_sel * L, d)
        s_sel = jnp.einsum('bhqd,bhqkd->bhqk', qc, k_sel, preferred_element_type=F32) * scale
        kidx_sel = (sel[..., None] * L + jnp.arange(L)).reshape(B, H, QC, n_sel * L)
        bkt = t5_bucket(qpos[:, None] - pos_p[kidx_sel])
        s_sel = jnp.where(sel_ok, s_sel + bias_T[jnp.arange(H)[:, None, None], bkt], NEG_INF)
        p = jax.nn.softmax(jnp.concatenate([s_sel, s_own], axis=-1), axis=-1)
        p_sel = p[..., : n_sel * L].astype(v_sel.dtype)
        p_own = p[..., n_sel * L:].astype(vs.dtype)
        return (jnp.einsum('bhqk,bhqkd->bhqd', p_sel, v_sel)
                + jnp.einsum('bhqk,bhkd->bhqd', p_own, vs))

    out = lax.map(chunk, jnp.arange(S // QC))
    out = jnp.moveaxis(out, 0, 2).reshape(B, H, S, d)
    return out.transpose(0, 2, 1, 3).reshape(B, S, H * d)


def conv_glu(h, w_up, conv_w, conv_b, w_down):
    S = h.shape[1]
    u = h @ w_up
    a, val = u[..., :FFN_DIM], u[..., FFN_DIM:]
    ap = jnp.pad(a, ((0, 0), (CONV_W - 1, 0), (0, 0)))
    a = conv_b + conv_w[0] * ap[:, 0:S]
    for j in range(1, CONV_W):
        a = a + conv_w[j] * ap[:, j:j + S]
    return (jax.nn.gelu(a, approximate=False) * val) @ w_down


def setup_inputs(seed: int = 0) -> dict:
    key = jax.random.key(seed)
    ks = jax.random.split(key, 20)

    def nrm(k, shape, scale):
        return jax.random.normal(k, shape, F32) * scale

    def gain(k, shape):
        return 1.0 + 0.02 * jax.random.normal(k, shape, F32)

    return {
        'x': nrm(ks[0], (BATCH, SEQ, D_MODEL), 1.0),
        'positions': jnp.arange(SEQ, dtype=jnp.int32),
        'rel_bias': nrm(ks[1], (REL_BUCKETS, BIAS_HEADS), 0.5),
        'norm1_g': gain(ks[2], (DEPTH, D_MODEL)),
        'w_in': nrm(ks[3], (DEPTH, D_MODEL, IN_COLS), D_MODEL ** -0.5),
        'diff_lambda': nrm(ks[4], (DEPTH, 4, DIFF_DK), 0.1),
        'diff_subln_g': gain(ks[5], (DEPTH, DIFF_DV)),
        'mla_q_norm_g': gain(ks[6], (DEPTH, MLA_Q_RANK)),
        'mla_w_uq': nrm(ks[7], (DEPTH, MLA_Q_RANK, MLA_HEADS * (MLA_NOPE + MLA_ROPE)), MLA_Q_RANK ** -0.5),
        'mla_kv_norm_g': gain(ks[8], (DEPTH, MLA_KV_RANK)),
        'mla_w_ukv': nrm(ks[9], (DEPTH, MLA_KV_RANK, MLA_HEADS * (MLA_NOPE + MLA_DV)), MLA_KV_RANK ** -0.5),
        'swa_sinks': nrm(ks[10], (DEPTH, SWA_Q_HEADS), 0.5),
        'w_branch': nrm(ks[11], (DEPTH, N_BRANCH, BRANCH_W, D_MODEL), BRANCH_W ** -0.5),
        'w_out': nrm(ks[12], (DEPTH, D_MODEL, D_MODEL), D_MODEL ** -0.5),
        'norm2_g': gain(ks[13], (DEPTH, D_MODEL)),
        'ffn_w_up': nrm(ks[14], (DEPTH, D_MODEL, 2 * FFN_DIM), D_MODEL ** -0.5),
        'ffn_conv_w': nrm(ks[15], (DEPTH, CONV_W, FFN_DIM), CONV_W ** -0.5),
        'ffn_conv_b': nrm(ks[16], (DEPTH, FFN_DIM), 0.01),
        'ffn_w_down': nrm(ks[17], (DEPTH, FFN_DIM, D_MODEL), FFN_DIM ** -0.5),
        'final_norm_g': gain(ks[18], (D_MODEL,)),
    }


def reference(x, positions, rel_bias, norm1_g, w_in, diff_lambda, diff_subln_g,
              mla_q_norm_g, mla_w_uq, mla_kv_norm_g, mla_w_ukv, swa_sinks, w_branch,
              w_out, norm2_g, ffn_w_up, ffn_conv_w, ffn_conv_b, ffn_w_down, final_norm_g):
    B, S, D = x.shape
    split_points = [int(i) for i in np.cumsum(IN_SPLITS)[:-1]]
    bias_a = rel_bias[:, BIAS_A0:BIAS_A1]
    bias_c = rel_bias[:, BIAS_C0:BIAS_C1]
    bias_d = rel_bias[:, BIAS_D0:BIAS_D1]
    for l in range(DEPTH):
        lam_init = 0.8 - 0.6 * math.exp(-0.3 * l)
        h = rms_norm(x, norm1_g[l])
        z = h @ w_in[l]
        (aq, ak, av, cq, ckv, kr, sq, sk, sv, mq, mk, mv, zg) = jnp.split(z, split_points, axis=-1)
        branches = (
            diff_attention(aq, ak, av, diff_lambda[l], diff_subln_g[l], lam_init, positions, bias_a),
            mla_attention(cq, ckv, kr, mla_q_norm_g[l], mla_w_uq[l], mla_kv_norm_g[l], mla_w_ukv[l], positions),
            swa_sink_attention(sq, sk, sv, swa_sinks[l], positions, bias_c),
            moba_attention(mq, mk, mv, positions, bias_d),
        )
        gates = jax.nn.sigmoid(zg.astype(F32)).astype(x.dtype).reshape(B, S, N_BRANCH, D)
        merged = gates[:, :, 0] * (branches[0] @ w_branch[l, 0])
        for i in range(1, N_BRANCH):
            merged = merged + gates[:, :, i] * (branches[i] @ w_branch[l, i])
        x = x + merged @ w_out[l]
        x = x + conv_glu(rms_norm(x, norm2_g[l]), ffn_w_up[l], ffn_conv_w[l], ffn_conv_b[l], ffn_w_down[l])
    return rms_norm(x, final_norm_g)
```

```python
import contextlib
import math
import numpy as np
import ml_dtypes
import concourse.bass as bass
import concourse.mybir as mybir
from concourse.bass_utils import run_bass_kernel_spmd

F32 = mybir.dt.float32
BF16 = mybir.dt.bfloat16
I32 = mybir.dt.int32
AF = mybir.ActivationFunctionType
ALU = mybir.AluOpType
AX = mybir.AxisListType
ENGS = ("pe", "act", "dve", "pool", "sp")

D = 2048
INC = 12864
FF = 5632
EPS = 1e-6
NEG = -30000.0
AQ, AK, AV, CQ, CKV, KR, SQ, SK, SV, MQ, MK, MV, ZG = (0, 512, 1024, 1536, 2048, 2304, 2368, 2880, 3008,
                                                        3136, 3648, 4160, 4672)
Q_DIFF, Q_MN, Q_MR, Q_SWA, Q_MOBA, QROWS = 0, 512, 1024, 1280, 1792, 2304
K_DIFF, K_MN, K_MR, K_SWA, K_MOBA, KROWS = 0, 512, 1024, 1088, 1216, 1728


class Op:
    __slots__ = ("eng", "fn", "deps", "signal", "tok", "is_dma", "semkey", "ninc", "idx", "epoch")

    def __init__(self, eng, fn, is_dma=False, semkey=None):
        self.eng, self.fn, self.is_dma, self.semkey = eng, fn, is_dma, semkey
        self.deps, self.signal, self.tok, self.ninc, self.idx, self.epoch = [], False, None, 0, -1, 0


class Prog:
    def __init__(self, nc):
        self.nc = nc
        self.ops = {e: [] for e in ENGS}
        self.last_w = {}
        self.readers = {}
        self.n = 0
        self.epoch = 0
        self.bar = []
        self.last_eng = {}
        self.last_dma = {}

    def barrier(self):
        b = [o for o in self.last_eng.values()] + [o for o in self.last_dma.values()]
        for o in b:
            o.signal = True
        self.bar = b
        self.last_w.clear()
        self.readers.clear()
        self.epoch += 1

    def _add(self, op, reads, writes):
        deps = set(self.bar)
        for k in reads:
            w = self.last_w.get(k)
            if w is not None:
                deps.add(w)
        for k in writes:
            w = self.last_w.get(k)
            if w is not None:
                deps.add(w)
            deps.update(self.readers.get(k, ()))
        deps.discard(op)
        keep = []
        for d in deps:
            if d.eng == op.eng and not d.is_dma and not op.is_dma and op.eng == "pe":
                continue
            keep.append(d)
            d.signal = True
        op.deps = keep
        op.idx = self.n
        op.epoch = self.epoch
        self.n += 1
        self.ops[op.eng].append(op)
        if op.is_dma:
            self.last_dma[op.semkey] = op
        else:
            self.last_eng[op.eng] = op
        for k in reads:
            self.readers.setdefault(k, []).append(op)
        for k in writes:
            self.last_w[k] = op
            self.readers[k] = []
        return op

    def op(self, eng, fn, r=(), w=()):
        return self._add(Op(eng, fn), r, w)

    def dma(self, eng, fn, semkey, r=(), w=(), n=1):
        o = Op(eng, fn, True, semkey)
        o.signal = True
        o.ninc = n
        return self._add(o, r, w)

    def emit(self, finals):
        nc = self.nc
        with contextlib.ExitStack() as es:
            eng_sems, eng_cnt, dma_sems, dma_cnt = {}, {}, {}, {}
            for e in ENGS:
                for o in self.ops[e]:
                    if o.is_dma or not o.signal:
                        continue
                    key = (e, o.epoch // 7)
                    if key not in eng_sems:
                        eng_sems[key] = es.enter_context(nc.semaphore(f"s_{e}_{key[1]}"))
                        eng_cnt[key] = 0
                    eng_cnt[key] += 1
                    o.tok = (eng_sems[key], eng_cnt[key])
            alld = sorted((o for e in ENGS for o in self.ops[e] if o.is_dma), key=lambda o: o.idx)
            for o in alld:
                k = o.semkey
                if k not in dma_sems:
                    dma_sems[k] = es.enter_context(nc.semaphore(f"d_{k}"))
                    dma_cnt[k] = 0
                dma_cnt[k] += 16 * o.ninc
                o.tok = (dma_sems[k], dma_cnt[k])
            self.nsems = len(eng_sems) + len(dma_sems)
            with nc.Block() as block:
                def run(engname, eh):
                    waited = {}
                    for o in self.ops[engname]:
                        for d in o.deps:
                            sem, cnt = d.tok
                            if waited.get(id(sem), 0) >= cnt:
                                continue
                            waited[id(sem)] = cnt
                            eh.wait_ge(sem, cnt)
                        res = o.fn(eh)
                        if o.is_dma:
                            if not isinstance(res, (list, tuple)):
                                res = [res]
                            assert len(res) == o.ninc
                            for ins in res:
                                ins.then_inc(o.tok[0], 16)
                        elif o.signal:
                            res.then_inc(o.tok[0], 1)
                    if engname == "sp":
                        for d in finals:
                            eh.wait_ge(d.tok[0], d.tok[1])

                @block.tensor
                def _(e):
                    run("pe", e)

                @block.scalar
                def _(e):
                    run("act", e)

                @block.vector
                def _(e):
                    run("dve", e)

                @block.gpsimd
                def _(e):
                    run("pool", e)

                @block.sync
                def _(e):
                    run("sp", e)


class B:
    def __init__(self, nc, P, arena, ps):
        self.nc, self.P, self.arena, self.ps = nc, P, arena, ps
        self.off = 0
        self.ring = 0
        self.wslot = 0

    def mark(self):
        return self.off

    def release(self, m):
        self.off = m

    def alloc(self, free, dt, parts=128):
        n = int(np.prod(free))
        sz = n * (4 if dt in (F32, I32) else 2)
        off = (self.off + 63) // 64 * 64
        self.off = off + sz
        assert self.off <= self.arena_bytes, ("SBUF arena overflow", self.off)
        v = self.arena[:, off // 2:(off + sz) // 2]
        if dt != BF16:
            v = v.bitcast(dt)
        if len(free) == 2:
            v = v.rearrange("p (a b) -> p a b", a=free[0])
        elif len(free) == 3:
            v = v.rearrange("p (a b c) -> p a b c", a=free[0], b=free[1])
        return v

    ring_set = list(range(8))

    def bank(self):
        self.ring = (self.ring + 1) % len(self.ring_set)
        return self.ring_set[self.ring]

    def pb(self, b, n=512, off=0):
        return self.ps[:, b * 512 + off:b * 512 + off + n]

    def mm(self, out, lhsT, rhs, start, stop, r, w):
        return self.P.op("pe", lambda e: e.matmul(out, lhsT=lhsT, rhs=rhs, start=start, stop=stop,
                                                  skip_group_check=True), r=r, w=w)

    def tr(self, out, in_, ident, r, w):
        return self.P.op("pe", lambda e: e.transpose(out, in_, ident), r=r, w=w)

    def act(self, out, in_, func, r, w, bias=None, scale=None):
        kw = {}
        if bias is not None:
            kw["bias"] = bias
        if scale is not None:
            kw["scale"] = scale
        return self.P.op("act", lambda e: e.activation(out=out, in_=in_, func=func, **kw), r=r, w=w)

    def ts(self, out, in0, s1, s2, op0, op1, r, w, eng="dve"):
        if op1 is None:
            return self.P.op(eng, lambda e: e.tensor_scalar(out=out, in0=in0, scalar1=s1, scalar2=None, op0=op0),
                             r=r, w=w)
        return self.P.op(eng, lambda e: e.tensor_scalar(out=out, in0=in0, scalar1=s1, scalar2=s2, op0=op0,
                                                        op1=op1), r=r, w=w)

    def tt(self, out, in0, in1, op, r, w, eng="dve"):
        return self.P.op(eng, lambda e: e.tensor_tensor(out=out, in0=in0, in1=in1, op=op), r=r, w=w)

    def stt(self, out, in0, scalar, in1, op0, op1, r, w, eng="dve"):
        return self.P.op(eng, lambda e: e.scalar_tensor_tensor(out=out, in0=in0, scalar=scalar, in1=in1,
                                                               op0=op0, op1=op1), r=r, w=w)

    def cp(self, out, in_, r, w, eng="dve"):
        return self.P.op(eng, lambda e: e.tensor_copy(out=out, in_=in_), r=r, w=w)

    def red(self, out, in_, op, r, w):
        return self.P.op("dve", lambda e: e.tensor_reduce(out=out, in_=in_, axis=AX.X, op=op), r=r, w=w)

    def recip(self, out, in_, r, w):
        return self.P.op("dve", lambda e: e.reciprocal(out=out, in_=in_), r=r, w=w)

    def memset(self, ap, val, w):
        return self.P.op("dve", lambda e: e.memset(ap, val), r=(), w=w)

    def ld(self, out, in_, key, r=(), eng="sp"):
        return self.P.dma(eng, lambda e: e.dma_start(out=out, in_=in_), key, r=r, w=[key])

    def st(self, out, in_, key, dkey, eng="sp"):
        return self.P.dma(eng, lambda e: e.dma_start(out=out, in_=in_), key, r=[key], w=[dkey])

    def wload(self, src, shape):
        s = self.wslot
        self.wslot ^= 1
        key = f"wbuf{s}"
        n = int(np.prod(shape))
        assert n * 2 <= self.wbytes
        v = self.wbufs[s][:, 0:n]
        if len(shape) == 2:
            v = v.rearrange("p (a b) -> p a b", a=shape[0])
        elif len(shape) == 3:
            v = v.rearrange("p (a b c) -> p a b c", a=shape[0], b=shape[1])
        if isinstance(src, (list, tuple)):
            self.P.dma("pool", lambda e: [e.dma_start(out=v[:, :, i, :], in_=sr) for i, sr in enumerate(src)],
                       key, r=(), w=[key], n=len(src))
        else:
            self.P.dma("pool", lambda e: e.dma_start(out=v, in_=src), key, r=(), w=[key])
        return v, key

    def rstd(self, out, in_, mul, r, w):
        self.ts(out, in_, float(mul), EPS, ALU.mult, ALU.add, r=r, w=w)
        self.act(out, out, AF.Sqrt, r=w, w=w)
        self.recip(out, out, r=w, w=w)


class _Stop(Exception):
    pass


def build(NT, DEPTH, stop=99):
    NB = NT // 256
    NKB = NT // 128
    G1 = 1024
    GM = 1024
    GF = 512
    nc = bass.Bass("TRN2", target_bir_lowering=False)

    def din(name, shape, dt=F32):
        return nc.dram_tensor(name, list(shape), dt, kind="ExternalInput").ap()

    x_in = din("x", [NT, D])
    pos_in = din("pos", [64, NT], I32)
    w_in = din("w_in", [DEPTH, D, INC])
    w_krsw = din("w_krsw", [DEPTH, D, 64])
    w_uq = din("w_uq", [DEPTH, 512, 1024])
    w_ukv = din("w_ukv", [DEPTH, 256, 1024])
    w_br = din("w_br", [DEPTH, 4, 512, D])
    w_out = din("w_out", [DEPTH, D, D])
    w_up = din("w_up", [DEPTH, D, 2 * FF])
    w_dn = din("w_dn", [DEPTH, FF, D])
    g1_in = din("g1", [DEPTH, 128, D])
    g2_in = din("g2", [DEPTH, 128, D])
    gf_in = din("gf", [128, D])
    gq_in = din("gq", [DEPTH, 128, 4])
    gkv_in = din("gkv", [DEPTH, 128, 2])
    sub_in = din("subg", [DEPTH, 128, 128])
    lam_in = din("lam", [DEPTH, 128, 256])
    sink_in = din("sinks", [DEPTH, 128, 8])
    cw_in = din("convw", [DEPTH, 128, 3 * 44])
    cb_in = din("convb", [DEPTH, 128, 44])
    td_in = din("td", [128, 16 * 128])
    tsb_in = din("tsb", [128, 16 * 128])
    c31_in = din("c31", [128, 16])
    ident_in = din("ident", [128, 128], BF16)
    ones_in = din("ones", [128, 128], BF16)
    mdiag_in = din("mdiag", [128, 128])
    mswa_in = din("mswa", [128, 128])
    esel_in = din("esel", [16, 16 * 128], BF16)
    elig_in = din("elig", [128, NB * NB])
    keep_in = din("keep", [128, NB * NB])
    invf_in = din("invf", [64, 2])
    out_d = nc.dram_tensor("out", [NT, D], F32, kind="ExternalOutput").ap()

    import os as _os
    _dbg = {"kind": "ExternalOutput"} if _os.environ.get("MKDBG") else {}
    XA = nc.dram_tensor("XA", [NT, D], F32, **_dbg).ap()
    XB = nc.dram_tensor("XB", [NT, D], F32).ap()
    QT = nc.dram_tensor("QT", [QROWS, NT], BF16).ap()
    KT = nc.dram_tensor("KT", [KROWS, NT], BF16).ap()
    VD = nc.dram_tensor("VD", [NT, 4 * 132], BF16).ap()
    VM = nc.dram_tensor("VM", [NT, 4 * 132], BF16).ap()
    VB = nc.dram_tensor("VB", [NT, 4 * 132], BF16).ap()
    VS = nc.dram_tensor("VS", [NT, 2 * 68], BF16).ap()
    GT = nc.dram_tensor("GT", [4 * D, NT], BF16).ap()
    BRT = nc.dram_tensor("BRT", [D, NT], BF16, **_dbg).ap()

    P = Prog(nc)
    with contextlib.ExitStack() as es:
        ARENA = 211000
        arena = es.enter_context(nc.sbuf_tensor("arena", [128, ARENA // 2], BF16))
        ps = es.enter_context(nc.psum_tensor("ps", [128, 4096], F32))
        b = B(nc, P, arena, ps)
        b.arena_bytes = ARENA
        b.wbytes = 22528
        b.wbufs = [b.alloc([b.wbytes // 2], BF16) for _ in range(2)]

        ident = b.alloc([128], BF16)
        ones = b.alloc([128], BF16)
        mdiag = b.alloc([128], F32)
        mswa = b.alloc([128], F32)
        esel = b.alloc([16, 128], BF16)
        elig = b.alloc([NB, NB], F32)
        keep = b.alloc([NB, NB], F32)
        invf = b.alloc([2], F32)
        c31 = b.alloc([16], F32)
        zcol = b.alloc([1], F32)
        exd = b.alloc([16, 128], BF16)
        exs = b.alloc([16, 128], BF16)
        exdm = b.alloc([128], BF16)
        cos2 = b.alloc([NT], BF16)
        sinS = b.alloc([NT], BF16)
        b.ld(ident, ident_in, "ident")
        b.ld(ones, ones_in, "ones")
        b.ld(mdiag, mdiag_in, "mdiag")
        b.ld(mswa, mswa_in, "mswa")
        b.ld(esel[0:16], esel_in.rearrange("p (a b) -> p a b", a=16), "esel")
        b.ld(elig, elig_in.rearrange("p (a b) -> p a b", a=NB), "elig")
        b.ld(keep, keep_in.rearrange("p (a b) -> p a b", a=NB), "keep")
        b.ld(invf[0:64], invf_in, "invf")
        b.ld(c31, c31_in, "c31")
        b.memset(zcol, 0.0, w=["zcol"])
        m0 = b.mark()
        td = b.alloc([16, 128], F32)
        tsb = b.alloc([16, 128], F32)
        tmpb = b.alloc([128], F32)
        b.ld(td, td_in.rearrange("p (a b) -> p a b", a=16), "td")
        b.ld(tsb, tsb_in.rearrange("p (a b) -> p a b", a=16), "tsb")
        SC_D, SC_M, SC_S, SC_B = 64 ** -0.5, 192 ** -0.5, 64 ** -0.5, 128 ** -0.5
        for h in range(16):
            swa = 4 <= h < 12
            sc = SC_D if h < 4 else (SC_S if swa else SC_B)
            ccol = zcol[:, 0:1] if swa else c31[:, h:h + 1]
            b.ts(tmpb, td[:, h, :], ccol, 1.0 / sc, ALU.subtract, ALU.mult, r=["td", "c31", "zcol"], w=["tmpb"])
            b.tt(exd[:, h, :], tmpb, mdiag, ALU.add, r=["tmpb", "mdiag"], w=["exd"])
            b.ts(tmpb, tsb[:, h, :], ccol, 1.0 / sc, ALU.subtract, ALU.mult, r=["tsb", "c31", "zcol"], w=["tmpb"])
            if swa:
                b.tt(exs[:, h, :], tmpb, mswa, ALU.add, r=["tmpb", "mswa"], w=["exs"])
            else:
                b.cp(exs[:, h, :], tmpb, r=["tmpb"], w=["exs"])
        b.cp(exdm, mdiag, r=["mdiag"], w=["exdm"])
        P.barrier()
        b.release(m0)
        m0 = b.mark()
        posi = b.alloc([NT], I32)
        ang = b.alloc([NT], F32)
        kf = b.alloc([NT], F32)
        ki = b.alloc([NT], I32)
        TWO_PI = float(2 * np.pi)
        b.ld(posi[0:64], pos_in, "posi")
        for which in range(2):
            A, KF, KI = ang[0:64], kf[0:64], ki[0:64]
            b.cp(A, posi[0:64], r=["posi"], w=["ang"])
            b.ts(A, A, invf[0:64, 0:1], float(np.pi / 2) if which == 0 else 0.0, ALU.mult, ALU.add,
                 r=["ang", "invf"], w=["ang"])
            b.ts(KF, A, 1.0 / TWO_PI, None, ALU.mult, None, r=["ang"], w=["kf"])
            b.cp(KI, KF, r=["kf"], w=["ki"])
            b.cp(KF, KI, r=["ki"], w=["kf"])
            b.stt(A, KF, -TWO_PI, A, ALU.mult, ALU.add, r=["kf", "ang"], w=["ang"])
            b.ts(KF, A, float(np.pi), None, ALU.is_gt, None, r=["ang"], w=["kf"])
            b.stt(A, KF, -TWO_PI, A, ALU.mult, ALU.add, r=["kf", "ang"], w=["ang"])
            b.ts(KF, A, float(-np.pi), None, ALU.is_lt, None, r=["ang"], w=["kf"])
            b.stt(A, KF, TWO_PI, A, ALU.mult, ALU.add, r=["kf", "ang"], w=["ang"])
            if which == 0:
                b.act(cos2[0:64], A, AF.Sin, r=["ang"], w=["cos2"])
            else:
                b.act(KF, A, AF.Sin, r=["ang"], w=["kf"])
                b.ts(sinS[0:64], KF, invf[0:64, 1:2], None, ALU.mult, None, r=["kf", "invf"], w=["sinS"])
        b.release(m0)
        P.barrier()
        persist = b.mark()

        def norm_tile(xsrc_rows, gb, hT, col0, xt, hj, ssq, xkey, hkey, gkey):
            b.ld(xt, xsrc_rows, xkey)
            b.tt(hj, xt, xt, ALU.mult, r=[xkey], w=[hkey])
            b.red(ssq, hj, ALU.add, r=[hkey], w=["ssq"])
            b.rstd(ssq, ssq, 1.0 / D, r=["ssq"], w=["ssq"])
            b.stt(hj, xt, ssq[:, 0:1], gb, ALU.mult, ALU.mult, r=[xkey, "ssq", gkey], w=[hkey])
            for k4 in range(4):
                bk = b.bank()
                tp = b.pb(bk).bitcast(BF16)
                for j in range(4):
                    kc = k4 * 4 + j
                    b.tr(tp[:, j * 128:(j + 1) * 128], hj[:, kc * 128:(kc + 1) * 128], ident,
                         r=[hkey, "ident"], w=[f"B{bk}"])
                b.act(hT[:, k4 * 4:k4 * 4 + 4, col0:col0 + 128],
                      tp[:, 0:512].rearrange("p (a b) -> p a b", a=4), AF.Copy, r=[f"B{bk}"], w=["hT"])

        def _chk(k):
            if stop <= k:
                raise _Stop()

        try:
          _chk(0)
          for l in range(DEPTH):
            xsrc = x_in if l == 0 else XB
            lam_init = 0.8 - 0.6 * math.exp(-0.3 * l)
            b.release(persist)
            gq = b.alloc([4], F32)
            gkv = b.alloc([2], F32)
            subg = b.alloc([128], F32)
            lamt = b.alloc([256], F32)
            lamc = b.alloc([4], F32)
            esink = b.alloc([8], F32)
            cw = b.alloc([3, 44], F32)
            cbias = b.alloc([44], F32)
            wuq = b.alloc([4, 1024], BF16)
            wukv = b.alloc([2, 1024], BF16)
            b.ld(gq, gq_in[l], "gq")
            b.ld(gkv, gkv_in[l], "gkv")
            b.ld(subg, sub_in[l], "subg")
            b.ld(lamt, lam_in[l], "lamt")
            b.ld(esink, sink_in[l], "esink")
            b.ld(cw, cw_in[l].rearrange("p (a b) -> p a b", a=3), "cw")
            b.ld(cbias, cb_in[l], "cbias")
            P.dma("pool", lambda e, l=l: e.dma_start(out=wuq, in_=w_uq[l].rearrange("(k p) n -> p k n", p=128)),
                  "wuq", w=["wuq"])
            P.dma("pool", lambda e, l=l: e.dma_start(out=wukv, in_=w_ukv[l].rearrange("(k p) n -> p k n", p=128)),
                  "wukv", w=["wukv"])
            b.act(esink, esink, AF.Exp, r=["esink"], w=["esink"])
            b.ts(subg, subg, 1.0 - lam_init, None, ALU.mult, None, r=["subg"], w=["subg"])
            b.tt(lamt[:, 0:64], lamt[:, 0:64], lamt[:, 64:128], ALU.mult, r=["lamt"], w=["lamt"])
            b.tt(lamt[:, 128:192], lamt[:, 128:192], lamt[:, 192:256], ALU.mult, r=["lamt"], w=["lamt"])
            b.red(lamc[:, 0:1], lamt[:, 0:64], ALU.add, r=["lamt"], w=["lamc"])
            b.red(lamc[:, 1:2], lamt[:, 128:192], ALU.add, r=["lamt"], w=["lamc"])
            b.act(lamc[:, 0:2], lamc[:, 0:2], AF.Exp, r=["lamc"], w=["lamc"])
            b.tt(lamc[:, 2:3], lamc[:, 1:2], lamc[:, 0:1], ALU.subtract, r=["lamc"], w=["lamc"])
            b.ts(lamc[:, 2:3], lamc[:, 2:3], -lam_init, None, ALU.add, None, r=["lamc"], w=["lamc"])
            layer_mark = b.mark()

            for g in range(NT // G1):
                t0 = g * G1
                b.release(layer_mark)
                hT = b.alloc([16, G1], BF16)
                g1b = b.alloc([D], F32)
                b.ld(g1b, g1_in[l], "g1b")
                xt = b.alloc([D], F32)
                hj = b.alloc([D], BF16)
                ssq = b.alloc([1], F32)
                stg = [b.alloc([G1], BF16) for _ in range(2)]
                vst = [b.alloc([4, 132], BF16) for _ in range(2)]
                cqraw = b.alloc([4, G1], BF16)
                ckvraw = b.alloc([2, G1], BF16)
                krA = b.alloc([G1], F32)
                krB = b.alloc([G1], F32)
                for i in range(2):
                    b.memset(vst[i], 1.0, w=[f"vst{i}"])
                for t in range(G1 // 128):
                    norm_tile(xsrc[t0 + t * 128:t0 + (t + 1) * 128, :], g1b, hT, t * 128, xt, hj, ssq,
                              "xt", "hj", "g1b")
                sti = [0]

                def fm_block(wv, wkey, c0, nrows, evac):
                    for q in range(G1 // 512):
                        bk = b.bank()
                        for kc in range(16):
                            b.mm(b.pb(bk)[0:nrows], wv[:, kc, c0:c0 + nrows], hT[:, kc, q * 512:(q + 1) * 512],
                                 kc == 0, kc == 15, r=[wkey, "hT"], w=[f"B{bk}"])
                        evac(q, b.pb(bk)[0:nrows], f"B{bk}")

                def to_dram(dst, row0, func=AF.Copy):
                    s = sti[0] % 2
                    sti[0] += 1
                    key = f"stg{s}"

                    def ev(q, pap, bkey):
                        b.act(stg[s][0:pap.shape[0], q * 512:(q + 1) * 512], pap, func, r=[bkey], w=[key])
                        if q == G1 // 512 - 1:
                            n = pap.shape[0]
                            b.st(dst[row0:row0 + n, t0:t0 + G1], stg[s][0:n, :], key, "dram")
                    return ev

                def tm_chunk(wv, wkey, c0, H, dv, dst, padw):
                    for t in range(G1 // 128):
                        bk = b.bank()
                        for kc in range(16):
                            b.mm(b.pb(bk, H * dv), hT[:, kc, t * 128:(t + 1) * 128], wv[:, kc, c0:c0 + H * dv],
                                 kc == 0, kc == 15, r=[wkey, "hT"], w=[f"B{bk}"])
                        s = sti[0] % 2
                        sti[0] += 1
                        vv = vst[s][:, 0:H * padw // 132, :] if padw == 132 else \
                            vst[s].rearrange("p a b -> p (a b)")[:, 0:H * padw].rearrange("p (a b) -> p a b", a=H)
                        b.act(vv[:, :, 0:dv], b.pb(bk, H * dv).rearrange("p (a b) -> p a b", a=H), AF.Copy,
                              r=[f"B{bk}"], w=[f"vst{s}"])
                        b.st(dst[t0 + t * 128:t0 + (t + 1) * 128, :],
                             vv.rearrange("p a b -> p (a b)"), f"vst{s}", "dram")

                wl = w_in[l].rearrange("(k p) n -> p k n", p=128)
                for (c0, dst, r0) in ((AQ, QT, Q_DIFF), (AK, KT, K_DIFF), (SQ, QT, Q_SWA), (MQ, QT, Q_MOBA),
                                      (MK, KT, K_MOBA)):
                    wv, wk = b.wload(wl[:, :, c0:c0 + 512], [16, 512])
                    for blk in range(4):
                        fm_block(wv, wk, blk * 128, 128, to_dram(dst, r0 + blk * 128))
                for (c0, dst) in ((AV, VD), (MV, VB)):
                    wv, wk = b.wload(wl[:, :, c0:c0 + 512], [16, 512])
                    tm_chunk(wv, wk, 0, 4, 128, dst, 132)
                for i in range(2):
                    b.memset(vst[i], 1.0, w=[f"vst{i}"])
                wv, wk = b.wload(wl[:, :, SK:SK + 256], [16, 256])
                fm_block(wv, wk, 0, 128, to_dram(KT, K_SWA))
                tm_chunk(wv, wk, 128, 2, 64, VS, 68)
                for i in range(2):
                    b.memset(vst[i], 1.0, w=[f"vst{i}"])
                wv, wk = b.wload(wl[:, :, CQ:CQ + 512], [16, 512])
                for blk in range(4):
                    def ev(q, pap, bkey, blk=blk):
                        b.act(cqraw[:, blk, q * 512:(q + 1) * 512], pap, AF.Copy, r=[bkey], w=["cqraw"])
                    fm_block(wv, wk, blk * 128, 128, ev)
                wv, wk = b.wload(wl[:, :, CKV:CKV + 320], [16, 320])
                for blk in range(2):
                    def ev(q, pap, bkey, blk=blk):
                        b.act(ckvraw[:, blk, q * 512:(q + 1) * 512], pap, AF.Copy, r=[bkey], w=["ckvraw"])
                    fm_block(wv, wk, blk * 128, 128, ev)

                def evA(q, pap, bkey):
                    b.act(krA[0:64, q * 512:(q + 1) * 512], pap, AF.Copy, r=[bkey], w=["krA"])
                fm_block(wv, wk, 256, 64, evA)
                wv, wk = b.wload(w_krsw[l].rearrange("(k p) n -> p k n", p=128), [16, 64])

                def evB(q, pap, bkey):
                    b.act(krB[0:64, q * 512:(q + 1) * 512], pap, AF.Copy, r=[bkey], w=["krB"])
                fm_block(wv, wk, 0, 64, evB)
                for ch in range(16):
                    wv, wk = b.wload(wl[:, :, ZG + ch * 512:ZG + (ch + 1) * 512], [16, 512])
                    for blk in range(4):
                        fm_block(wv, wk, blk * 128, 128, to_dram(GT, ch * 512 + blk * 128, AF.Sigmoid))

                mm0 = b.mark()
                sq = b.alloc([4, G1], BF16)
                rq = b.alloc([G1], F32)
                rkv = b.alloc([G1], F32)
                rtok = b.alloc([G1 // 128], F32)
                t1 = b.alloc([G1], F32)
                t2 = b.alloc([G1], F32)
                csl = slice(t0, t0 + G1)

                def ssq_bc(raw, nblk, rout, rkey, mul):
                    b.tt(sq[:, 0:nblk, :], raw, raw, ALU.mult, r=[rkey], w=["sq"])
                    for q in range(G1 // 512):
                        bk = b.bank()
                        for kb in range(nblk):
                            b.mm(b.pb(bk), ones, sq[:, kb, q * 512:(q + 1) * 512], kb == 0, kb == nblk - 1,
                                 r=["sq", "ones"], w=[f"B{bk}"])
                        b.ts(rout[:, q * 512:(q + 1) * 512], b.pb(bk), mul, EPS, ALU.mult, ALU.add,
                             r=[f"B{bk}"], w=["rr"])
                    b.act(rout, rout, AF.Sqrt, r=["rr"], w=["rr"])
                    b.recip(rout, rout, r=["rr"], w=["rr"])

                ssq_bc(cqraw, 4, rq, "cqraw", 1.0 / 512)
                for kb in range(4):
                    b.ts(cqraw[:, kb, :], cqraw[:, kb, :], gq[:, kb:kb + 1], None, ALU.mult, None,
                         r=["cqraw", "gq", "sq"], w=["cqraw"])
                for h in range(4):
                    s = sti[0] % 2
                    sti[0] += 1
                    for q in range(G1 // 512):
                        bk = b.bank()
                        for kc in range(4):
                            b.mm(b.pb(bk), wuq[:, kc, h * 192:h * 192 + 128], cqraw[:, kc, q * 512:(q + 1) * 512],
                                 kc == 0, kc == 3, r=["wuq", "cqraw"], w=[f"B{bk}"])
                        b.tt(stg[s][:, q * 512:(q + 1) * 512], b.pb(bk), rq[:, q * 512:(q + 1) * 512], ALU.mult,
                             r=[f"B{bk}", "rr"], w=[f"stg{s}"])
                    b.st(QT[Q_MN + h * 128:Q_MN + (h + 1) * 128, csl], stg[s], f"stg{s}", "dram")
                    s = sti[0] % 2
                    sti[0] += 1
                    for q in range(G1 // 512):
                        qs = slice(q * 512, (q + 1) * 512)
                        bka, bkb = b.bank(), b.bank()
                        for kc in range(4):
                            b.mm(b.pb(bka)[0:64], wuq[:, kc, h * 192 + 128:h * 192 + 192], cqraw[:, kc, qs],
                                 kc == 0, kc == 3, r=["wuq", "cqraw"], w=[f"B{bka}"])
                        for kc in range(4):
                            b.mm(b.pb(bkb)[0:64], wuq[:, kc, 768 + h * 64:768 + (h + 1) * 64], cqraw[:, kc, qs],
                                 kc == 0, kc == 3, r=["wuq", "cqraw"], w=[f"B{bkb}"])
                        gs = slice(t0 + q * 512, t0 + (q + 1) * 512)
                        b.tt(t1[0:64, qs], b.pb(bka)[0:64], cos2[0:64, gs], ALU.mult, r=[f"B{bka}", "cos2"], w=["t1"])
                        b.tt(t2[0:64, qs], b.pb(bkb)[0:64], sinS[0:64, gs], ALU.mult, r=[f"B{bkb}", "sinS"], w=["t2"])
                        b.tt(t1[0:64, qs], t1[0:64, qs], t2[0:64, qs], ALU.add, r=["t1", "t2"], w=["t1"])
                        b.tt(stg[s][0:64, qs], t1[0:64, qs], rq[0:64, qs], ALU.mult, r=["t1", "rr"], w=[f"stg{s}"])
                    b.st(QT[Q_MR + h * 64:Q_MR + (h + 1) * 64, csl], stg[s][0:64, :], f"stg{s}", "dram")
                ssq_bc(ckvraw, 2, rkv, "ckvraw", 1.0 / 256)
                bk = b.bank()
                for t in range(G1 // 128):
                    for kb in range(2):
                        b.mm(b.pb(bk)[:, t:t + 1], sq[:, kb, t * 128:(t + 1) * 128], ones[:, 0:1], kb == 0, kb == 1,
                             r=["sq", "ones"], w=[f"B{bk}"])
                b.rstd(rtok, b.pb(bk)[:, 0:G1 // 128], 1.0 / 256, r=[f"B{bk}"], w=["rtok"])
                for kb in range(2):
                    b.ts(ckvraw[:, kb, :], ckvraw[:, kb, :], gkv[:, kb:kb + 1], None, ALU.mult, None,
                         r=["ckvraw", "gkv", "sq"], w=["ckvraw"])
                for h in range(4):
                    s = sti[0] % 2
                    sti[0] += 1
                    for q in range(G1 // 512):
                        bk = b.bank()
                        for kc in range(2):
                            b.mm(b.pb(bk), wukv[:, kc, h * 256:h * 256 + 128], ckvraw[:, kc, q * 512:(q + 1) * 512],
                                 kc == 0, kc == 1, r=["wukv", "ckvraw"], w=[f"B{bk}"])
                        b.tt(stg[s][:, q * 512:(q + 1) * 512], b.pb(bk), rkv[:, q * 512:(q + 1) * 512], ALU.mult,
                             r=[f"B{bk}", "rr"], w=[f"stg{s}"])
                    b.st(KT[K_MN + h * 128:K_MN + (h + 1) * 128, csl], stg[s], f"stg{s}", "dram")
                wv_v = wukv.rearrange("p k (h two c) -> p k h two c", h=4, two=2)
                for t in range(G1 // 128):
                    bk = b.bank()
                    for kc in range(2):
                        b.mm(b.pb(bk).rearrange("p (a b) -> p a b", a=4), ckvraw[:, kc, t * 128:(t + 1) * 128],
                             wv_v[:, kc, :, 1, :], kc == 0, kc == 1, r=["wukv", "ckvraw"], w=[f"B{bk}"])
                    s = sti[0] % 2
                    sti[0] += 1
                    b.ts(vst[s][:, :, 0:128], b.pb(bk).rearrange("p (a b) -> p a b", a=4), rtok[:, t:t + 1], None,
                         ALU.mult, None, r=[f"B{bk}", "rtok"], w=[f"vst{s}"])
                    b.st(VM[t0 + t * 128:t0 + (t + 1) * 128, :], vst[s].rearrange("p a b -> p (a b)"),
                         f"vst{s}", "dram")
                s = sti[0] % 2
                sti[0] += 1
                gsl = slice(t0, t0 + G1)
                b.tt(t1[0:64], krA[0:64], cos2[0:64, gsl], ALU.mult, r=["krA", "cos2"], w=["t1"])
                b.tt(t2[0:64], krB[0:64], sinS[0:64, gsl], ALU.mult, r=["krB", "sinS"], w=["t2"])
                b.tt(stg[s][0:64], t1[0:64], t2[0:64], ALU.add, r=["t1", "t2"], w=[f"stg{s}"])
                b.st(KT[K_MR:K_MR + 64, csl], stg[s][0:64, :], f"stg{s}", "dram")
                b.release(mm0)
                P.barrier()
            _chk(1)

            b.release(layer_mark)
            NQT = NT // 512
            pt = [b.alloc([512], BF16) for _ in range(3)]
            osb = b.alloc([4, 128], F32)
            obf = b.alloc([4, 128], BF16)
            rs = b.alloc([8], F32)
            brs = [b.alloc([512], BF16) for _ in range(2)]
            vsb = b.alloc([NKB, 4 * 132], BF16)
            kts = [b.alloc([NT], BF16) for _ in range(2)]
            qts = [b.alloc([NT], BF16) for _ in range(2)]
            ktr = b.alloc([NT], BF16)
            qtr = b.alloc([NT], BF16)
            negT = b.alloc([NT], BF16)
            brc = [0]
            b.ring_set = [0, 1, 2]
            ocnt = [0]

            def obanks():
                ocnt[0] += 1
                return (3, 4) if ocnt[0] % 2 else (5, 6)

            def dense_pass(T, chunks, hb, sc, ccol, exd_t, exs_t, vcol, obank, moba=False):
                nk = 4 * T + 4
                for j in range(nk):
                    smin = max(0, j - 4 * T)
                    q0 = smin * 128
                    sb_ = b.bank()
                    S = b.pb(sb_)
                    skey = f"B{sb_}"
                    ext = []
                    for s in range(smin, 4):
                        d = 4 * T + s - j
                        if d == 0:
                            ext.append((s, exd_t))
                        elif d == 1 and exs_t is not None:
                            ext.append((s, exs_t))
                    nmm = len(chunks) + len(ext) + (1 if moba else 0)
                    i = 0
                    for (kt_, qt_, rows, kkey, qkey) in chunks:
                        b.mm(S[:, q0:512], kt_[0:rows, j * 128:(j + 1) * 128], qt_[0:rows, T * 512 + q0:T * 512 + 512],
                             i == 0, i == nmm - 1, r=[kkey, qkey], w=[skey])
                        i += 1
                    for (s, ex) in ext:
                        b.mm(S[:, s * 128:(s + 1) * 128], ident, ex, False, i == nmm - 1, r=["ident", "exd", "exs", "exdm"],
                             w=[skey])
                        i += 1
                    if moba:
                        n = j // 2
                        b.mm(S[:, q0:512], esel[0:16, n, :], negT[0:16, T * 512 + q0:T * 512 + 512], False, True,
                             r=["esel", "negT"], w=[skey])
                    pi_ = j % 3
                    b.act(pt[pi_][:, q0:512], S[:, q0:512], AF.Exp, r=[skey, "c31", "zcol"], w=[f"pt{pi_}"],
                          bias=ccol, scale=float(sc))
                    for s in range(smin, 4):
                        ob = obank[s // 2]
                        b.mm(b.pb(ob, 129, (s % 2) * 256), pt[pi_][:, s * 128:(s + 1) * 128],
                             vsb[:, j, vcol:vcol + 129], j == 0 and s % 2 == 0, j == 4 * T + s, r=[f"pt{pi_}", "vsb"], w=[f"B{ob}"])

            def emit_branch(T, row0):
                bk = 7
                tp = b.pb(bk).bitcast(BF16)
                for s in range(4):
                    b.tr(tp[:, s * 128:(s + 1) * 128], obf[:, s, :], ident, r=["obf", "ident"], w=[f"B{bk}"])
                k = brc[0] % 2
                brc[0] += 1
                b.act(brs[k], tp[:, 0:512], AF.Copy, r=[f"B{bk}"], w=[f"brs{k}"])
                b.st(BRT[row0:row0 + 128, T * 512:(T + 1) * 512], brs[k], f"brs{k}", "dram")

            def load_rows(dst, src, row0, rows, key):
                b.ld(dst[0:rows, :], src[row0:row0 + rows, :], key)

            b.ld(vsb, VD.rearrange("(j p) c -> p j c", p=128), "vsb")
            for h in range(4):
                for m in range(2):
                    load_rows(kts[m], KT, K_DIFF + (2 * h + m) * 64, 64, f"kts{m}")
                    load_rows(qts[m], QT, Q_DIFF + (2 * h + m) * 64, 64, f"qts{m}")
                for T in range(NQT):
                    for m in range(2):
                        ob = obanks()
                        dense_pass(T, [(kts[m], qts[m], 64, f"kts{m}", f"qts{m}")], h, SC_D, c31[:, h:h + 1],
                                   exd[:, h, :], exs[:, h, :], h * 132, ob)
                        for s in range(4):
                            O = b.pb(ob[s // 2], 129, (s % 2) * 256)
                            okey = f"B{ob[s // 2]}"
                            b.recip(rs[:, s:s + 1], O[:, 128:129], r=[okey], w=["rs"])
                            if m == 0:
                                b.ts(osb[:, s, :], O[:, 0:128], rs[:, s:s + 1], None, ALU.mult, None,
                                     r=[okey, "rs", "obf"], w=["osb"])
                            else:
                                b.tt(rs[:, s:s + 1], rs[:, s:s + 1], lamc[:, 2:3], ALU.mult, r=["rs", "lamc"], w=["rs"])
                                b.stt(osb[:, s, :], O[:, 0:128], rs[:, s:s + 1], osb[:, s, :], ALU.mult, ALU.add,
                                      r=[okey, "rs", "osb"], w=["osb"])
                    b.tt(obf, osb, osb, ALU.mult, r=["osb"], w=["obf"])
                    b.red(rs[:, 4:8], obf, ALU.add, r=["obf"], w=["rs"])
                    b.rstd(rs[:, 4:8], rs[:, 4:8], 1.0 / 128, r=["rs"], w=["rs"])
                    for s in range(4):
                        b.stt(obf[:, s, :], osb[:, s, :], rs[:, 4 + s:5 + s], subg, ALU.mult, ALU.mult,
                              r=["osb", "rs", "subg"], w=["obf"])
                    emit_branch(T, 0 * 512 + h * 128)

            def simple_post(ob, extra_col=None):
                for s in range(4):
                    O = b.pb(ob[s // 2], 129, (s % 2) * 256)
                    okey = f"B{ob[s // 2]}"
                    b.recip(rs[:, s:s + 1], O[:, 128:129], r=[okey], w=["rs"])
                    b.ts(obf[:, s, :], O[:, 0:128], rs[:, s:s + 1], None, ALU.mult, None, r=[okey, "rs"], w=["obf"])

            _chk(1.25)
            b.ld(vsb, VM.rearrange("(j p) c -> p j c", p=128), "vsb")
            load_rows(ktr, KT, K_MR, 64, "ktr")
            for h in range(4):
                load_rows(kts[0], KT, K_MN + h * 128, 128, "kts0")
                load_rows(qts[0], QT, Q_MN + h * 128, 128, "qts0")
                load_rows(qtr, QT, Q_MR + h * 64, 64, "qtr")
                for T in range(NQT):
                    ob = obanks()
                    dense_pass(T, [(kts[0], qts[0], 128, "kts0", "qts0"), (ktr, qtr, 64, "ktr", "qtr")], 0, SC_M,
                               zcol[:, 0:1], exdm, None, h * 132, ob)
                    simple_post(ob)
                    emit_branch(T, 1 * 512 + h * 128)

            _chk(1.5)
            b.ld(vsb, VB.rearrange("(j p) c -> p j c", p=128), "vsb")
            kmf = b.alloc([NB], F32)
            kmT = b.alloc([NB], BF16)
            gm = b.alloc([NB], F32)
            gm2 = b.alloc([NB], F32)
            gmk = b.alloc([NB], F32)
            nbf = b.alloc([NB], BF16)
            mx = b.alloc([1], F32)
            for h in range(4):
                load_rows(kts[0], KT, K_MOBA + h * 128, 128, "kts0")
                load_rows(qts[0], QT, Q_MOBA + h * 128, 128, "qts0")
                b.red(kmf, kts[0].rearrange("p (n k) -> p n k", k=256), ALU.add, r=["kts0"], w=["kmf"])
                b.ts(kmT, kmf, 1.0 / 256, None, ALU.mult, None, r=["kmf"], w=["kmT"])
                for i in range(NKB):
                    blk = i // 2
                    bk = 7
                    b.mm(b.pb(bk, NB), qts[0][:, i * 128:(i + 1) * 128], kmT, True, True, r=["qts0", "kmT"],
                         w=[f"B{bk}"])
                    b.tt(gm, b.pb(bk, NB), elig[:, blk, :], ALU.add, r=[f"B{bk}", "elig"], w=["gm"])
                    b.cp(gm2, gm, r=["gm"], w=["gm2"])
                    for it in range(3):
                        b.red(mx, gm2, ALU.max, r=["gm2"], w=["mx"])
                        if it < 2:
                            b.ts(gmk, gm2, mx[:, 0:1], NEG * 1e20, ALU.is_ge, ALU.mult, r=["gm2", "mx"], w=["gmk"])
                            b.tt(gm2, gm2, gmk, ALU.add, r=["gm2", "gmk"], w=["gm2"])
                    b.ts(gmk, gm, mx[:, 0:1], NEG, ALU.is_lt, ALU.mult, r=["gm", "mx"], w=["gmk"])
                    b.tt(nbf, gmk, keep[:, blk, :], ALU.mult, r=["gmk", "keep"], w=["nbf"])
                    bk2 = 7
                    tp = b.pb(bk2).bitcast(BF16)
                    b.tr(tp[0:NB, 0:128], nbf, ident, r=["nbf", "ident"], w=[f"B{bk2}"])
                    b.act(negT[0:NB, i * 128:(i + 1) * 128], tp[0:NB, 0:128], AF.Copy, r=[f"B{bk2}"], w=["negT"])
                for T in range(NQT):
                    ob = obanks()
                    hb = 12 + h
                    dense_pass(T, [(kts[0], qts[0], 128, "kts0", "qts0")], hb, SC_B, c31[:, hb:hb + 1],
                               exd[:, hb, :], exs[:, hb, :], h * 132, ob, moba=True)
                    simple_post(ob)
                    emit_branch(T, 3 * 512 + h * 128)

            _chk(1.75)
            vss = vsb.rearrange("p j c -> p (j c)")[:, 0:NKB * 136].rearrange("p (j c) -> p j c", c=136)
            b.ld(vss, VS.rearrange("(j p) c -> p j c", p=128), "vsb")
            for pr in range(4):
                kvh = (2 * pr) // 4
                load_rows(kts[0], KT, K_SWA + kvh * 64, 64, "kts0")
                for hh in range(2):
                    load_rows(qts[hh], QT, Q_SWA + (2 * pr + hh) * 64, 64, f"qts{hh}")
                for i in range(NKB):
                    for hh in range(2):
                        hq = 2 * pr + hh
                        hb = 4 + hq
                        sb_ = b.bank()
                        S = b.pb(sb_)
                        skey = f"B{sb_}"
                        qsl = qts[hh][0:64, i * 128:(i + 1) * 128]
                        lo = 0 if i > 0 else 1
                        for kk in range(lo, 2):
                            kb = i - 1 + kk
                            b.mm(S[:, kk * 128:(kk + 1) * 128], kts[0][0:64, kb * 128:(kb + 1) * 128], qsl, True, False,
                                 r=["kts0", f"qts{hh}"], w=[skey])
                            b.mm(S[:, kk * 128:(kk + 1) * 128], ident, (exs if kk == 0 else exd)[:, hb, :], False, True,
                                 r=["ident", "exd", "exs"], w=[skey])
                        pi_ = (i * 2 + hh) % 3
                        b.act(pt[pi_][:, lo * 128:256], S[:, lo * 128:256], AF.Exp, r=[skey, "zcol"], w=[f"pt{pi_}"],
                              bias=zcol[:, 0:1], scale=float(SC_S))
                        ob = 3 + (i * 2 + hh) % 4
                        O = b.pb(ob, 65)
                        for kk in range(lo, 2):
                            kb = i - 1 + kk
                            b.mm(O, pt[pi_][:, kk * 128:(kk + 1) * 128], vss[:, kb, kvh * 68:kvh * 68 + 65], kk == lo,
                                 kk == 1, r=[f"pt{pi_}", "vsb"], w=[f"B{ob}"])
                        b.tt(rs[:, hh:hh + 1], O[:, 64:65], esink[:, hq:hq + 1], ALU.add, r=[f"B{ob}", "esink"], w=["rs"])
                        b.recip(rs[:, hh:hh + 1], rs[:, hh:hh + 1], r=["rs"], w=["rs"])
                        b.ts(obf[:, i % 4, hh * 64:(hh + 1) * 64], O[:, 0:64], rs[:, hh:hh + 1], None, ALU.mult, None,
                             r=[f"B{ob}", "rs"], w=["obf"])
                    if i % 4 == 3:
                        emit_branch(i // 4, 2 * 512 + pr * 128)
            P.barrier()
            b.ring_set = list(range(8))
            _chk(2)

            b.release(layer_mark)
            brT = b.alloc([16, GM], BF16)
            mT = b.alloc([16, GM], BF16)
            gts = [b.alloc([4, GM], BF16) for _ in range(2)]
            acc = b.alloc([512], F32)
            tmp = b.alloc([512], F32)
            xts = [b.alloc([512], F32) for _ in range(2)]
            for g in range(NT // GM):
                t0 = g * GM
                b.ld(brT, BRT[:, t0:t0 + GM].rearrange("(k p) t -> p k t", p=128), "brT")
                gi = 0
                for cc in range(4):
                    wv, wk = b.wload(w_br[l].rearrange("i (k p) n -> p (i k) n", p=128)[:, :, cc * 512:(cc + 1) * 512],
                                     [16, 512])
                    for cb in range(4):
                        c = cc * 4 + cb
                        gsel = gi % 2
                        gi += 1
                        gt_ = gts[gsel]
                        gkey = f"gts{gsel}"
                        P.dma("sp", lambda e, gt_=gt_, c=c, t0=t0: e.dma_start(
                            out=gt_, in_=GT.rearrange("(i r) t -> r i t", i=4)[c * 128:(c + 1) * 128, :, t0:t0 + GM]),
                            gkey, w=[gkey])
                        for hf in range(GM // 512):
                            hs = slice(hf * 512, (hf + 1) * 512)
                            bks = [b.bank() for _ in range(4)]
                            for i in range(4):
                                for kc in range(4):
                                    b.mm(b.pb(bks[i]), wv[:, i * 4 + kc, cb * 128:(cb + 1) * 128], brT[:, i * 4 + kc, hs],
                                         kc == 0, kc == 3, r=[wk, "brT"], w=[f"B{bks[i]}"])
                            b.tt(acc, b.pb(bks[0]), gt_[:, 0, hs], ALU.mult, r=[f"B{bks[0]}", gkey], w=["acc"])
                            for i in range(1, 4):
                                b.tt(tmp, b.pb(bks[i]), gt_[:, i, hs], ALU.mult, r=[f"B{bks[i]}", gkey], w=["tmp"])
                                if i < 3:
                                    b.tt(acc, acc, tmp, ALU.add, r=["acc", "tmp"], w=["acc"])
                                else:
                                    b.tt(mT[:, c, hs], acc, tmp, ALU.add, r=["acc", "tmp"], w=["mT"])
                xi = 0
                for cc in range(4):
                    cs = slice(cc * 512, (cc + 1) * 512)
                    wv, wk = b.wload(w_out[l].rearrange("(k p) n -> p k n", p=128)[:, :, cs], [16, 512])
                    for t in range(GM // 128):
                        rows = slice(t0 + t * 128, t0 + (t + 1) * 128)
                        k = xi % 2
                        xi += 1
                        b.ld(xts[k], xsrc[rows, cs], f"xts{k}", r=["dramx"])
                        bk = b.bank()
                        for kc in range(16):
                            b.mm(b.pb(bk), mT[:, kc, t * 128:(t + 1) * 128], wv[:, kc, :], kc == 0, kc == 15,
                                 r=[wk, "mT"], w=[f"B{bk}"])
                        b.tt(xts[k], xts[k], b.pb(bk), ALU.add, r=[f"xts{k}", f"B{bk}"], w=[f"xts{k}"])
                        b.st(XA[rows, cs], xts[k], f"xts{k}", "dramxa")
            P.barrier()
            _chk(3)

            b.release(layer_mark)
            h2T = b.alloc([16, GF + 2], BF16)
            g2b = b.alloc([D], F32)
            b.ld(g2b, g2_in[l], "g2b")
            gT = b.alloc([44, GF], BF16)
            xt = b.alloc([D], F32)
            hj = b.alloc([D], BF16)
            ssq = b.alloc([1], F32)
            asb = b.alloc([GF + 2], F32)
            c0t = b.alloc([GF], F32)
            c1t = b.alloc([GF], F32)
            xts = [b.alloc([256], F32) for _ in range(2)]
            for g in range(NT // GF):
                t0 = g * GF
                if g == 0:
                    b.memset(h2T[:, :, 0:2], 0.0, w=["hT"])
                else:
                    b.cp(h2T[:, :, 0:2], h2T[:, :, GF:GF + 2], r=["hT"], w=["hT"])
                for t in range(GF // 128):
                    norm_tile(XA[t0 + t * 128:t0 + (t + 1) * 128, :], g2b, h2T, 2 + t * 128, xt, hj, ssq,
                              "xt", "hj", "g2b")
                wu = w_up[l].rearrange("(k p) (two f) -> p k two f", p=128, two=2)
                for cp_ in range(22):
                    wv, wk = b.wload([wu[:, :, 0, cp_ * 256:(cp_ + 1) * 256], wu[:, :, 1, cp_ * 256:(cp_ + 1) * 256]],
                                     [16, 2, 256])
                    for c2 in range(2):
                        cb = cp_ * 2 + c2
                        cs = slice(c2 * 128, (c2 + 1) * 128)
                        ba, bh, bv = b.bank(), b.bank(), b.bank()
                        for kc in range(16):
                            b.mm(b.pb(ba), wv[:, kc, 0, cs], h2T[:, kc, 2:GF + 2], kc == 0, kc == 15, r=[wk, "hT"],
                                 w=[f"B{ba}"])
                        for kc in range(16):
                            b.mm(b.pb(bh, 2), wv[:, kc, 0, cs], h2T[:, kc, 0:2], kc == 0, kc == 15, r=[wk, "hT"],
                                 w=[f"B{bh}"])
                        for kc in range(16):
                            b.mm(b.pb(bv), wv[:, kc, 1, cs], h2T[:, kc, 2:GF + 2], kc == 0, kc == 15, r=[wk, "hT"],
                                 w=[f"B{bv}"])
                        b.act(asb[:, 0:2], b.pb(bh, 2), AF.Copy, r=[f"B{bh}"], w=["asb"])
                        b.act(asb[:, 2:GF + 2], b.pb(ba), AF.Copy, r=[f"B{ba}"], w=["asb"])
                        b.act(c0t, asb[:, 2:GF + 2], AF.Identity, r=["asb", "cw", "cbias"], w=["c0t"],
                              bias=cbias[:, cb:cb + 1], scale=cw[:, 2, cb:cb + 1])
                        b.stt(c1t, asb[:, 1:GF + 1], cw[:, 1, cb:cb + 1], c0t, ALU.mult, ALU.add,
                              r=["asb", "cw", "c0t"], w=["c1t"])
                        b.stt(c0t, asb[:, 0:GF], cw[:, 0, cb:cb + 1], c1t, ALU.mult, ALU.add,
                              r=["asb", "cw", "c1t"], w=["c0t"])
                        b.act(c1t, c0t, AF.Gelu, r=["c0t"], w=["c1t"])
                        b.tt(gT[:, cb, :], c1t, b.pb(bv), ALU.mult, r=["c1t", f"B{bv}"], w=["gT"])
                xi = 0
                for cc in range(8):
                    cs = slice(cc * 256, (cc + 1) * 256)
                    wv, wk = b.wload(w_dn[l].rearrange("(k p) n -> p k n", p=128)[:, :, cs], [44, 256])
                    for t in range(GF // 128):
                        rows = slice(t0 + t * 128, t0 + (t + 1) * 128)
                        k = xi % 2
                        xi += 1
                        b.ld(xts[k], XA[rows, cs], f"xts{k}")
                        bk = b.bank()
                        for kc in range(44):
                            b.mm(b.pb(bk, 256), gT[:, kc, t * 128:(t + 1) * 128], wv[:, kc, :], kc == 0, kc == 43,
                                 r=[wk, "gT"], w=[f"B{bk}"])
                        b.tt(xts[k], xts[k], b.pb(bk, 256), ALU.add, r=[f"xts{k}", f"B{bk}"], w=[f"xts{k}"])
                        b.st(XB[rows, cs], xts[k], f"xts{k}", "dramxb")
            P.barrier()

        except _Stop:
            pass
        b.release(persist)
        gfb = b.alloc([D], F32)
        xtf = [b.alloc([D], F32) for _ in range(2)]
        hjf = b.alloc([D], F32)
        ssq = b.alloc([1], F32)
        b.ld(gfb, gf_in, "gfb")
        finals = []
        for t in range(NT // 128):
            k = t % 2
            rows = slice(t * 128, (t + 1) * 128)
            b.ld(xtf[k], XB[rows, :], f"xtf{k}")
            b.tt(hjf, xtf[k], xtf[k], ALU.mult, r=[f"xtf{k}"], w=["hjf"])
            b.red(ssq, hjf, ALU.add, r=["hjf"], w=["ssq"])
            b.rstd(ssq, ssq, 1.0 / D, r=["ssq"], w=["ssq"])
            b.stt(xtf[k], xtf[k], ssq[:, 0:1], gfb, ALU.mult, ALU.mult, r=[f"xtf{k}", "ssq", "gfb"], w=[f"xtf{k}"])
            finals.append(b.st(out_d[rows, :], xtf[k], f"xtf{k}", "dramout"))
        P.emit(finals[-2:] if stop >= 99 else list(P.last_dma.values()))
    return nc


def _t5_bucket(n):
    n = np.maximum(n, 0)
    nf = np.maximum(n, 1).astype(np.float32)
    large = 16 + (np.log(nf / np.float32(16)) / np.float32(math.log(8)) * np.float32(16)).astype(np.int32)
    return np.where(n < 16, n, np.minimum(large, 31))


def host_consts(NT):
    NB = NT // 256
    kk = np.arange(128)[:, None]
    qq = np.arange(128)[None, :]
    c = {}
    c["ident"] = np.eye(128, dtype=np.float32).astype(ml_dtypes.bfloat16)
    c["ones"] = np.ones((128, 128), dtype=np.float32).astype(ml_dtypes.bfloat16)
    c["mdiag"] = np.where(qq >= kk, 0.0, NEG * 8).astype(np.float32)
    c["mswa"] = np.where(qq < kk, 0.0, NEG * 8).astype(np.float32)
    es = np.zeros((16, 16, 128), dtype=np.float32)
    for n in range(16):
        es[n, n, :] = 1.0
    c["esel"] = es.reshape(16, 16 * 128).astype(ml_dtypes.bfloat16)
    el = np.zeros((128, NB, NB), dtype=np.float32)
    kp = np.zeros((128, NB, NB), dtype=np.float32)
    for bq in range(NB):
        el[:, bq, bq:] = -1e30
        kp[:, bq, :bq] = 1.0
    c["elig"] = el.reshape(128, NB * NB)
    c["keep"] = kp.reshape(128, NB * NB)
    inv = (10000.0 ** (-np.arange(0, 64, 2, dtype=np.float32) / np.float32(64))).astype(np.float32)
    invf = np.zeros((64, 2), dtype=np.float32)
    invf[:, 0] = np.concatenate([inv, inv])
    invf[:, 1] = np.concatenate([-np.ones(32), np.ones(32)])
    c["invf"] = invf
    c["_bd"] = _t5_bucket(qq - kk)
    c["_bs"] = _t5_bucket(128 + qq - kk)
    return c


def prep_inputs(inp, NT, DEPTH, b):
    c = host_consts(NT)
    rb = np.asarray(inp["rel_bias"], dtype=np.float32)
    m = {}
    m["x"] = np.ascontiguousarray(inp["x"][b])
    m["pos"] = np.ascontiguousarray(np.broadcast_to(np.asarray(inp["positions"], dtype=np.int32)[None, :NT], (64, NT)))
    m["td"] = np.ascontiguousarray(rb[c["_bd"]].transpose(0, 2, 1)).reshape(128, 16 * 128)
    m["tsb"] = np.ascontiguousarray(rb[c["_bs"]].transpose(0, 2, 1)).reshape(128, 16 * 128)
    m["c31"] = np.ascontiguousarray(np.broadcast_to(rb[31:32, :], (128, 16)))
    for k in ("ident", "ones", "mdiag", "mswa", "esel", "elig", "keep", "invf"):
        m[k] = c[k]
    return m


_SHARED = {}


def rep128(a):
    a = np.asarray(a, dtype=np.float32)
    return np.ascontiguousarray(np.broadcast_to(a[:, None, :], (a.shape[0], 128, a.shape[1])))


def shared_inputs(inp, DEPTH):
    w_in = np.asarray(inp["w_in"], dtype=np.float32)
    s = {}
    s["w_in"] = w_in
    s["w_krsw"] = np.ascontiguousarray(np.concatenate([w_in[:, :, KR + 32:KR + 64], w_in[:, :, KR:KR + 32]], axis=2))
    wuq = np.asarray(inp["mla_w_uq"], dtype=np.float32)
    sw = [np.concatenate([wuq[:, :, h * 192 + 160:h * 192 + 192], wuq[:, :, h * 192 + 128:h * 192 + 160]], axis=2)
          for h in range(4)]
    s["w_uq"] = np.ascontiguousarray(np.concatenate([wuq] + sw, axis=2))
    s["w_ukv"] = np.asarray(inp["mla_w_ukv"], dtype=np.float32)
    s["w_br"] = np.asarray(inp["w_branch"], dtype=np.float32)
    s["w_out"] = np.asarray(inp["w_out"], dtype=np.float32)
    s["w_up"] = np.asarray(inp["ffn_w_up"], dtype=np.float32)
    s["w_dn"] = np.asarray(inp["ffn_w_down"], dtype=np.float32)
    s["g1"] = rep128(inp["norm1_g"])
    s["g2"] = rep128(inp["norm2_g"])
    s["gf"] = np.ascontiguousarray(np.broadcast_to(np.asarray(inp["final_norm_g"], dtype=np.float32)[None, :], (128, D)))
    s["gq"] = np.ascontiguousarray(np.asarray(inp["mla_q_norm_g"], dtype=np.float32).reshape(DEPTH, 4, 128).transpose(0, 2, 1))
    s["gkv"] = np.ascontiguousarray(np.asarray(inp["mla_kv_norm_g"], dtype=np.float32).reshape(DEPTH, 2, 128).transpose(0, 2, 1))
    s["subg"] = rep128(inp["diff_subln_g"])
    s["lam"] = rep128(np.asarray(inp["diff_lambda"], dtype=np.float32).reshape(DEPTH, 256))
    s["sinks"] = rep128(inp["swa_sinks"])
    cw = np.asarray(inp["ffn_conv_w"], dtype=np.float32).reshape(DEPTH, 3, 44, 128)
    s["convw"] = np.ascontiguousarray(cw.transpose(0, 3, 1, 2)).reshape(DEPTH, 128, 3 * 44)
    cb = np.asarray(inp["ffn_conv_b"], dtype=np.float32).reshape(DEPTH, 44, 128)
    s["convb"] = np.ascontiguousarray(cb.transpose(0, 2, 1))
    return s


def run(inp, NT, DEPTH, BATCH, n_cores=8, stop=99):
    nc = build(NT, DEPTH, stop)
    sh = shared_inputs(inp, DEPTH)
    in_maps = []
    for c in range(n_cores):
        m = dict(sh)
        m.update(prep_inputs(inp, NT, DEPTH, c % BATCH))
        in_maps.append(m)
    res = run_bass_kernel_spmd(nc, in_maps, core_ids=list(range(n_cores)))
    global LAST
    LAST = res.results
    return np.stack([res.results[bi]["out"] for bi in range(BATCH)], axis=0)


def kernel(**inputs):
    inp = {k: np.asarray(v) for k, v in inputs.items()}
    return run(inp, 4096, 4, 4).astype(np.float32)
```

```python
import contextlib
import math
import numpy as np
import ml_dtypes
import concourse.bass as bass
import concourse.mybir as mybir
from concourse.bass_utils import run_bass_kernel_spmd

F32 = mybir.dt.float32
BF16 = mybir.dt.bfloat16
I32 = mybir.dt.int32
AF = mybir.ActivationFunctionType
ALU = mybir.AluOpType
AX = mybir.AxisListType
ENGS = ("pe", "act", "dve", "pool", "sp")

D = 2048
INC = 12864
FF = 5632
EPS = 1e-6
NEG = -30000.0
AQ, AK, AV, CQ, CKV, KR, SQ, SK, SV, MQ, MK, MV, ZG = (0, 512, 1024, 1536, 2048, 2304, 2368, 2880, 3008,
                                                        3136, 3648, 4160, 4672)
Q_DIFF, Q_MN, Q_MR, Q_SWA, Q_MOBA, QROWS = 0, 512, 1024, 1280, 1792, 2304
K_DIFF, K_MN, K_MR, K_SWA, K_MOBA, KROWS = 0, 512, 1024, 1088, 1216, 1728


class Op:
    __slots__ = ("eng", "fn", "deps", "signal", "tok", "is_dma", "semkey", "ninc", "idx", "epoch")

    def __init__(self, eng, fn, is_dma=False, semkey=None):
        self.eng, self.fn, self.is_dma, self.semkey = eng, fn, is_dma, semkey
        self.deps, self.signal, self.tok, self.ninc, self.idx, self.epoch = [], False, None, 0, -1, 0


class Prog:
    def __init__(self, nc):
        self.nc = nc
        self.ops = {e: [] for e in ENGS}
        self.last_w = {}
        self.readers = {}
        self.n = 0
        self.epoch = 0
        self.bar = []
        self.last_eng = {}
        self.last_dma = {}

    def barrier(self):
        b = [o for o in self.last_eng.values()] + [o for o in self.last_dma.values()]
        for o in b:
            o.signal = True
        self.bar = b
        self.last_w.clear()
        self.readers.clear()
        self.epoch += 1

    def _add(self, op, reads, writes):
        deps = set(self.bar)
        for k in reads:
            w = self.last_w.get(k)
            if w is not None:
                deps.add(w)
        for k in writes:
            w = self.last_w.get(k)
            if w is not None:
                deps.add(w)
            deps.update(self.readers.get(k, ()))
        deps.discard(op)
        keep = []
        for d in deps:
            if d.eng == op.eng and not d.is_dma and not op.is_dma and op.eng == "pe":
                continue
            keep.append(d)
            d.signal = True
        op.deps = keep
        op.idx = self.n
        op.epoch = self.epoch
        self.n += 1
        self.ops[op.eng].append(op)
        if op.is_dma:
            self.last_dma[op.semkey] = op
        else:
            self.last_eng[op.eng] = op
        for k in reads:
            lst = self.readers.setdefault(k, [])
            if not op.is_dma:
                for i_, o_ in enumerate(lst):
                    if o_.eng == op.eng and not o_.is_dma:
                        lst[i_] = op
                        break
                else:
                    lst.append(op)
            else:
                lst.append(op)
        for k in writes:
            self.last_w[k] = op
            self.readers[k] = []
        return op

    def op(self, eng, fn, r=(), w=()):
        return self._add(Op(eng, fn), r, w)

    def dma(self, eng, fn, semkey, r=(), w=(), n=1):
        o = Op(eng, fn, True, semkey)
        o.signal = True
        o.ninc = n
        return self._add(o, r, w)

    def emit(self, finals):
        nc = self.nc
        with contextlib.ExitStack() as es:
            eng_sems, eng_cnt, dma_sems, dma_cnt = {}, {}, {}, {}
            for e in ENGS:
                for o in self.ops[e]:
                    if o.is_dma or not o.signal:
                        continue
                    key = (e, o.epoch // 7)
                    if key not in eng_sems:
                        eng_sems[key] = es.enter_context(nc.semaphore(f"s_{e}_{key[1]}"))
                        eng_cnt[key] = 0
                    eng_cnt[key] += 1
                    o.tok = (eng_sems[key], eng_cnt[key])
            alld = sorted((o for e in ENGS for o in self.ops[e] if o.is_dma), key=lambda o: o.idx)
            for o in alld:
                k = o.semkey
                if k not in dma_sems:
                    dma_sems[k] = es.enter_context(nc.semaphore(f"d_{k}"))
                    dma_cnt[k] = 0
                dma_cnt[k] += 16 * o.ninc
                o.tok = (dma_sems[k], dma_cnt[k])
            self.nsems = len(eng_sems) + len(dma_sems)
            with nc.Block() as block:
                def run(engname, eh):
                    waited = {}
                    for o in self.ops[engname]:
                        for d in o.deps:
                            sem, cnt = d.tok
                            if waited.get(id(sem), 0) >= cnt:
                                continue
                            waited[id(sem)] = cnt
                            eh.wait_ge(sem, cnt)
                        res = o.fn(eh)
                        if o.is_dma:
                            if not isinstance(res, (list, tuple)):
                                res = [res]
                            assert len(res) == o.ninc
                            for ins in res:
                                ins.then_inc(o.tok[0], 16)
                        elif o.signal:
                            res.then_inc(o.tok[0], 1)
                    if engname == "sp":
                        for d in finals:
                            eh.wait_ge(d.tok[0], d.tok[1])

                @block.tensor
                def _(e):
                    run("pe", e)

                @block.scalar
                def _(e):
                    run("act", e)

                @block.vector
                def _(e):
                    run("dve", e)

                @block.gpsimd
                def _(e):
                    run("pool", e)

                @block.sync
                def _(e):
                    run("sp", e)


class B:
    def __init__(self, nc, P, arena, ps):
        self.nc, self.P, self.arena, self.ps = nc, P, arena, ps
        self.off = 0
        self.ring = 0
        self.wslot = 0

    def mark(self):
        return self.off

    def release(self, m):
        self.off = m

    def alloc(self, free, dt, parts=128):
        n = int(np.prod(free))
        sz = n * (4 if dt in (F32, I32) else 2)
        off = (self.off + 63) // 64 * 64
        self.off = off + sz
        assert self.off <= self.arena_bytes, ("SBUF arena overflow", self.off)
        v = self.arena[:, off // 2:(off + sz) // 2]
        if dt != BF16:
            v = v.bitcast(dt)
        if len(free) == 2:
            v = v.rearrange("p (a b) -> p a b", a=free[0])
        elif len(free) == 3:
            v = v.rearrange("p (a b c) -> p a b c", a=free[0], b=free[1])
        return v

    ring_set = list(range(8))

    def bank(self):
        self.ring = (self.ring + 1) % len(self.ring_set)
        return self.ring_set[self.ring]

    def pb(self, b, n=512, off=0):
        return self.ps[:, b * 512 + off:b * 512 + off + n]

    def mm(self, out, lhsT, rhs, start, stop, r, w):
        return self.P.op("pe", lambda e: e.matmul(out, lhsT=lhsT, rhs=rhs, start=start, stop=stop,
                                                  skip_group_check=True), r=r, w=w)

    def tr(self, out, in_, ident, r, w):
        return self.P.op("pe", lambda e: e.transpose(out, in_, ident), r=r, w=w)

    def act(self, out, in_, func, r, w, bias=None, scale=None):
        kw = {}
        if bias is not None:
            kw["bias"] = bias
        if scale is not None:
            kw["scale"] = scale
        return self.P.op("act", lambda e: e.activation(out=out, in_=in_, func=func, **kw), r=r, w=w)

    def ts(self, out, in0, s1, s2, op0, op1, r, w, eng="dve"):
        if op1 is None:
            return self.P.op(eng, lambda e: e.tensor_scalar(out=out, in0=in0, scalar1=s1, scalar2=None, op0=op0),
                             r=r, w=w)
        return self.P.op(eng, lambda e: e.tensor_scalar(out=out, in0=in0, scalar1=s1, scalar2=s2, op0=op0,
                                                        op1=op1), r=r, w=w)

    def tt(self, out, in0, in1, op, r, w, eng="dve"):
        return self.P.op(eng, lambda e: e.tensor_tensor(out=out, in0=in0, in1=in1, op=op), r=r, w=w)

    def stt(self, out, in0, scalar, in1, op0, op1, r, w, eng="dve"):
        return self.P.op(eng, lambda e: e.scalar_tensor_tensor(out=out, in0=in0, scalar=scalar, in1=in1,
                                                               op0=op0, op1=op1), r=r, w=w)

    def cp(self, out, in_, r, w, eng="dve"):
        return self.P.op(eng, lambda e: e.tensor_copy(out=out, in_=in_), r=r, w=w)

    def red(self, out, in_, op, r, w):
        return self.P.op("dve", lambda e: e.tensor_reduce(out=out, in_=in_, axis=AX.X, op=op), r=r, w=w)

    def recip(self, out, in_, r, w):
        return self.P.op("dve", lambda e: e.reciprocal(out=out, in_=in_), r=r, w=w)

    def memset(self, ap, val, w):
        return self.P.op("dve", lambda e: e.memset(ap, val), r=(), w=w)

    def ld(self, out, in_, key, r=(), eng="sp"):
        return self.P.dma(eng, lambda e: e.dma_start(out=out, in_=in_), key, r=r, w=[key])

    def st(self, out, in_, key, dkey, eng="sp"):
        return self.P.dma(eng, lambda e: e.dma_start(out=out, in_=in_), key, r=[key], w=[dkey])

    def wload(self, src, shape):
        s = self.wslot
        self.wslot ^= 1
        key = f"wbuf{s}"
        n = int(np.prod(shape))
        assert n * 2 <= self.wbytes
        v = self.wbufs[s][:, 0:n]
        if len(shape) == 2:
            v = v.rearrange("p (a b) -> p a b", a=shape[0])
        elif len(shape) == 3:
            v = v.rearrange("p (a b c) -> p a b c", a=shape[0], b=shape[1])
        if isinstance(src, (list, tuple)):
            self.P.dma("pool", lambda e: [e.dma_start(out=v[:, :, i, :], in_=sr) for i, sr in enumerate(src)],
                       key, r=(), w=[key], n=len(src))
        else:
            self.P.dma("pool", lambda e: e.dma_start(out=v, in_=src), key, r=(), w=[key])
        return v, key

    def rstd(self, out, in_, mul, r, w):
        self.ts(out, in_, float(mul), EPS, ALU.mult, ALU.add, r=r, w=w)
        self.act(out, out, AF.Sqrt, r=w, w=w)
        self.recip(out, out, r=w, w=w)


class _Stop(Exception):
    pass


def build(NT, DEPTH, stop=99):
    NB = NT // 256
    NKB = NT // 128
    G1 = 1024
    GM = 1024
    GF = 512
    nc = bass.Bass("TRN2", target_bir_lowering=False)

    def din(name, shape, dt=F32):
        return nc.dram_tensor(name, list(shape), dt, kind="ExternalInput").ap()

    x_in = din("x", [NT, D])
    pos_in = din("pos", [64, NT], I32)
    w_in = din("w_in", [DEPTH, D, INC])
    w_krsw = din("w_krsw", [DEPTH, D, 64])
    w_uq = din("w_uq", [DEPTH, 512, 1024])
    w_ukv = din("w_ukv", [DEPTH, 256, 1024])
    w_br = din("w_br", [DEPTH, 4, 512, D])
    w_out = din("w_out", [DEPTH, D, D])
    w_up = din("w_up", [DEPTH, D, 2 * FF])
    w_dn = din("w_dn", [DEPTH, FF, D])
    g1_in = din("g1", [DEPTH, 128, D])
    g2_in = din("g2", [DEPTH, 128, D])
    gf_in = din("gf", [128, D])
    gq_in = din("gq", [DEPTH, 128, 4])
    gkv_in = din("gkv", [DEPTH, 128, 2])
    sub_in = din("subg", [DEPTH, 128, 128])
    lam_in = din("lam", [DEPTH, 128, 256])
    sink_in = din("sinks", [DEPTH, 128, 8])
    cw_in = din("convw", [DEPTH, 128, 3 * 44])
    cb_in = din("convb", [DEPTH, 128, 44])
    td_in = din("td", [128, 16 * 128])
    tsb_in = din("tsb", [128, 16 * 128])
    c31_in = din("c31", [128, 16])
    ident_in = din("ident", [128, 128], BF16)
    ones_in = din("ones", [128, 128], BF16)
    mdiag_in = din("mdiag", [128, 128])
    mswa_in = din("mswa", [128, 128])
    esel_in = din("esel", [16, 16 * 128], BF16)
    elig_in = din("elig", [128, NB * NB])
    keep_in = din("keep", [128, NB * NB])
    invf_in = din("invf", [64, 2])
    out_d = nc.dram_tensor("out", [NT, D], F32, kind="ExternalOutput").ap()

    import os as _os
    _dbg = {"kind": "ExternalOutput"} if _os.environ.get("MKDBG") else {}
    XA = nc.dram_tensor("XA", [NT, D], F32, **_dbg).ap()
    XB = nc.dram_tensor("XB", [NT, D], F32).ap()
    QT = nc.dram_tensor("QT", [QROWS, NT], BF16).ap()
    KT = nc.dram_tensor("KT", [KROWS, NT], BF16).ap()
    VD = nc.dram_tensor("VD", [NT, 4 * 132], BF16).ap()
    VM = nc.dram_tensor("VM", [NT, 4 * 132], BF16).ap()
    VB = nc.dram_tensor("VB", [NT, 4 * 132], BF16).ap()
    VS = nc.dram_tensor("VS", [NT, 2 * 68], BF16).ap()
    GT = nc.dram_tensor("GT", [4 * D, NT], BF16).ap()
    BRT = nc.dram_tensor("BRT", [D, NT], BF16, **_dbg).ap()

    P = Prog(nc)
    with contextlib.ExitStack() as es:
        ARENA = 211000
        arena = es.enter_context(nc.sbuf_tensor("arena", [128, ARENA // 2], BF16))
        ps = es.enter_context(nc.psum_tensor("ps", [128, 4096], F32))
        b = B(nc, P, arena, ps)
        b.arena_bytes = ARENA
        b.wbytes = 22528
        b.wbufs = [b.alloc([b.wbytes // 2], BF16) for _ in range(2)]

        ident = b.alloc([128], BF16)
        ones = b.alloc([128], BF16)
        mdiag = b.alloc([128], F32)
        mswa = b.alloc([128], F32)
        esel = b.alloc([16, 128], BF16)
        elig = b.alloc([NB, NB], F32)
        keep = b.alloc([NB, NB], F32)
        invf = b.alloc([2], F32)
        c31 = b.alloc([16], F32)
        zcol = b.alloc([1], F32)
        exd = b.alloc([16, 128], BF16)
        exs = b.alloc([16, 128], BF16)
        exdm = b.alloc([128], BF16)
        cos2 = b.alloc([NT], BF16)
        sinS = b.alloc([NT], BF16)
        b.ld(ident, ident_in, "ident")
        b.ld(ones, ones_in, "ones")
        b.ld(mdiag, mdiag_in, "mdiag")
        b.ld(mswa, mswa_in, "mswa")
        b.ld(esel[0:16], esel_in.rearrange("p (a b) -> p a b", a=16), "esel")
        b.ld(elig, elig_in.rearrange("p (a b) -> p a b", a=NB), "elig")
        b.ld(keep, keep_in.rearrange("p (a b) -> p a b", a=NB), "keep")
        b.ld(invf[0:64], invf_in, "invf")
        b.ld(c31, c31_in, "c31")
        b.memset(zcol, 0.0, w=["zcol"])
        m0 = b.mark()
        td = b.alloc([16, 128], F32)
        tsb = b.alloc([16, 128], F32)
        tmpb = b.alloc([128], F32)
        b.ld(td, td_in.rearrange("p (a b) -> p a b", a=16), "td")
        b.ld(tsb, tsb_in.rearrange("p (a b) -> p a b", a=16), "tsb")
        SC_D, SC_M, SC_S, SC_B = 64 ** -0.5, 192 ** -0.5, 64 ** -0.5, 128 ** -0.5
        for h in range(16):
            swa = 4 <= h < 12
            sc = SC_D if h < 4 else (SC_S if swa else SC_B)
            ccol = zcol[:, 0:1] if swa else c31[:, h:h + 1]
            b.ts(tmpb, td[:, h, :], ccol, 1.0 / sc, ALU.subtract, ALU.mult, r=["td", "c31", "zcol"], w=["tmpb"])
            b.tt(exd[:, h, :], tmpb, mdiag, ALU.add, r=["tmpb", "mdiag"], w=["exd"])
            b.ts(tmpb, tsb[:, h, :], ccol, 1.0 / sc, ALU.subtract, ALU.mult, r=["tsb", "c31", "zcol"], w=["tmpb"])
            if swa:
                b.tt(exs[:, h, :], tmpb, mswa, ALU.add, r=["tmpb", "mswa"], w=["exs"])
            else:
                b.cp(exs[:, h, :], tmpb, r=["tmpb"], w=["exs"])
        b.cp(exdm, mdiag, r=["mdiag"], w=["exdm"])
        P.barrier()
        b.release(m0)
        m0 = b.mark()
        posi = b.alloc([NT], I32)
        ang = b.alloc([NT], F32)
        kf = b.alloc([NT], F32)
        ki = b.alloc([NT], I32)
        TWO_PI = float(2 * np.pi)
        b.ld(posi[0:64], pos_in, "posi")
        for which in range(2):
            A, KF, KI = ang[0:64], kf[0:64], ki[0:64]
            b.cp(A, posi[0:64], r=["posi"], w=["ang"])
            b.ts(A, A, invf[0:64, 0:1], float(np.pi / 2) if which == 0 else 0.0, ALU.mult, ALU.add,
                 r=["ang", "invf"], w=["ang"])
            b.ts(KF, A, 1.0 / TWO_PI, None, ALU.mult, None, r=["ang"], w=["kf"])
            b.cp(KI, KF, r=["kf"], w=["ki"])
            b.cp(KF, KI, r=["ki"], w=["kf"])
            b.stt(A, KF, -TWO_PI, A, ALU.mult, ALU.add, r=["kf", "ang"], w=["ang"])
            b.ts(KF, A, float(np.pi), None, ALU.is_gt, None, r=["ang"], w=["kf"])
            b.stt(A, KF, -TWO_PI, A, ALU.mult, ALU.add, r=["kf", "ang"], w=["ang"])
            b.ts(KF, A, float(-np.pi), None, ALU.is_lt, None, r=["ang"], w=["kf"])
            b.stt(A, KF, TWO_PI, A, ALU.mult, ALU.add, r=["kf", "ang"], w=["ang"])
            if which == 0:
                b.act(cos2[0:64], A, AF.Sin, r=["ang"], w=["cos2"])
            else:
                b.act(KF, A, AF.Sin, r=["ang"], w=["kf"])
                b.ts(sinS[0:64], KF, invf[0:64, 1:2], None, ALU.mult, None, r=["kf", "invf"], w=["sinS"])
        b.release(m0)
        P.barrier()
        persist = b.mark()

        def norm_tile(xsrc_rows, gb, hT, col0, xt, hj, ssq, xkey, hkey, gkey):
            b.ld(xt, xsrc_rows, xkey)
            b.tt(hj, xt, xt, ALU.mult, r=[xkey], w=[hkey])
            b.red(ssq, hj, ALU.add, r=[hkey], w=["ssq"])
            b.rstd(ssq, ssq, 1.0 / D, r=["ssq"], w=["ssq"])
            b.stt(hj, xt, ssq[:, 0:1], gb, ALU.mult, ALU.mult, r=[xkey, "ssq", gkey], w=[hkey])
            for k4 in range(4):
                bk = b.bank()
                tp = b.pb(bk).bitcast(BF16)
                for j in range(4):
                    kc = k4 * 4 + j
                    b.tr(tp[:, j * 128:(j + 1) * 128], hj[:, kc * 128:(kc + 1) * 128], ident,
                         r=[hkey, "ident"], w=[f"B{bk}"])
                b.act(hT[:, k4 * 4:k4 * 4 + 4, col0:col0 + 128],
                      tp[:, 0:512].rearrange("p (a b) -> p a b", a=4), AF.Copy, r=[f"B{bk}"], w=["hT"])

        def _chk(k):
            if stop <= k:
                raise _Stop()

        try:
          _chk(0)
          for l in range(DEPTH):
            xsrc = x_in if l == 0 else XB
            lam_init = 0.8 - 0.6 * math.exp(-0.3 * l)
            b.release(persist)
            gq = b.alloc([4], F32)
            gkv = b.alloc([2], F32)
            subg = b.alloc([128], F32)
            lamt = b.alloc([256], F32)
            lamc = b.alloc([4], F32)
            esink = b.alloc([8], F32)
            cw = b.alloc([3, 44], F32)
            cbias = b.alloc([44], F32)
            wuq = b.alloc([4, 1024], BF16)
            wukv = b.alloc([2, 1024], BF16)
            b.ld(gq, gq_in[l], "gq")
            b.ld(gkv, gkv_in[l], "gkv")
            b.ld(subg, sub_in[l], "subg")
            b.ld(lamt, lam_in[l], "lamt")
            b.ld(esink, sink_in[l], "esink")
            b.ld(cw, cw_in[l].rearrange("p (a b) -> p a b", a=3), "cw")
            b.ld(cbias, cb_in[l], "cbias")
            P.dma("pool", lambda e, l=l: e.dma_start(out=wuq, in_=w_uq[l].rearrange("(k p) n -> p k n", p=128)),
                  "wuq", w=["wuq"])
            P.dma("pool", lambda e, l=l: e.dma_start(out=wukv, in_=w_ukv[l].rearrange("(k p) n -> p k n", p=128)),
                  "wukv", w=["wukv"])
            b.act(esink, esink, AF.Exp, r=["esink"], w=["esink"])
            b.ts(subg, subg, 1.0 - lam_init, None, ALU.mult, None, r=["subg"], w=["subg"])
            b.tt(lamt[:, 0:64], lamt[:, 0:64], lamt[:, 64:128], ALU.mult, r=["lamt"], w=["lamt"])
            b.tt(lamt[:, 128:192], lamt[:, 128:192], lamt[:, 192:256], ALU.mult, r=["lamt"], w=["lamt"])
            b.red(lamc[:, 0:1], lamt[:, 0:64], ALU.add, r=["lamt"], w=["lamc"])
            b.red(lamc[:, 1:2], lamt[:, 128:192], ALU.add, r=["lamt"], w=["lamc"])
            b.act(lamc[:, 0:2], lamc[:, 0:2], AF.Exp, r=["lamc"], w=["lamc"])
            b.tt(lamc[:, 2:3], lamc[:, 1:2], lamc[:, 0:1], ALU.subtract, r=["lamc"], w=["lamc"])
            b.ts(lamc[:, 2:3], lamc[:, 2:3], -lam_init, None, ALU.add, None, r=["lamc"], w=["lamc"])
            layer_mark = b.mark()

            for g in range(NT // G1):
                t0 = g * G1
                b.release(layer_mark)
                hT = b.alloc([16, G1], BF16)
                g1b = b.alloc([D], F32)
                b.ld(g1b, g1_in[l], "g1b")
                xt = b.alloc([D], F32)
                hj = b.alloc([D], BF16)
                ssq = b.alloc([1], F32)
                stg = [b.alloc([G1], BF16) for _ in range(2)]
                vst = [b.alloc([4, 132], BF16) for _ in range(2)]
                cqraw = b.alloc([4, G1], BF16)
                ckvraw = b.alloc([2, G1], BF16)
                krA = b.alloc([G1], F32)
                krB = b.alloc([G1], F32)
                for i in range(2):
                    b.memset(vst[i], 1.0, w=[f"vst{i}"])
                for t in range(G1 // 128):
                    norm_tile(xsrc[t0 + t * 128:t0 + (t + 1) * 128, :], g1b, hT, t * 128, xt, hj, ssq,
                              "xt", "hj", "g1b")
                sti = [0]

                def fm_block(wv, wkey, c0, nrows, evac):
                    for q in range(G1 // 512):
                        bk = b.bank()
                        for kc in range(16):
                            b.mm(b.pb(bk)[0:nrows], wv[:, kc, c0:c0 + nrows], hT[:, kc, q * 512:(q + 1) * 512],
                                 kc == 0, kc == 15, r=[wkey, "hT"], w=[f"B{bk}"])
                        evac(q, b.pb(bk)[0:nrows], f"B{bk}")

                def to_dram(dst, row0, func=AF.Copy):
                    s = sti[0] % 2
                    sti[0] += 1
                    key = f"stg{s}"

                    def ev(q, pap, bkey):
                        b.act(stg[s][0:pap.shape[0], q * 512:(q + 1) * 512], pap, func, r=[bkey], w=[key])
                        if q == G1 // 512 - 1:
                            n = pap.shape[0]
                            b.st(dst[row0:row0 + n, t0:t0 + G1], stg[s][0:n, :], key, "dram")
                    return ev

                def tm_chunk(wv, wkey, c0, H, dv, dst, padw):
                    for t in range(G1 // 128):
                        bk = b.bank()
                        for kc in range(16):
                            b.mm(b.pb(bk, H * dv), hT[:, kc, t * 128:(t + 1) * 128], wv[:, kc, c0:c0 + H * dv],
                                 kc == 0, kc == 15, r=[wkey, "hT"], w=[f"B{bk}"])
                        s = sti[0] % 2
                        sti[0] += 1
                        vv = vst[s][:, 0:H * padw // 132, :] if padw == 132 else \
                            vst[s].rearrange("p a b -> p (a b)")[:, 0:H * padw].rearrange("p (a b) -> p a b", a=H)
                        b.act(vv[:, :, 0:dv], b.pb(bk, H * dv).rearrange("p (a b) -> p a b", a=H), AF.Copy,
                              r=[f"B{bk}"], w=[f"vst{s}"])
                        b.st(dst[t0 + t * 128:t0 + (t + 1) * 128, :],
                             vv.rearrange("p a b -> p (a b)"), f"vst{s}", "dram")

                wl = w_in[l].rearrange("(k p) n -> p k n", p=128)
                for (c0, dst, r0) in ((AQ, QT, Q_DIFF), (AK, KT, K_DIFF), (SQ, QT, Q_SWA), (MQ, QT, Q_MOBA),
                                      (MK, KT, K_MOBA)):
                    wv, wk = b.wload(wl[:, :, c0:c0 + 512], [16, 512])
                    for blk in range(4):
                        fm_block(wv, wk, blk * 128, 128, to_dram(dst, r0 + blk * 128))
                for (c0, dst) in ((AV, VD), (MV, VB)):
                    wv, wk = b.wload(wl[:, :, c0:c0 + 512], [16, 512])
                    tm_chunk(wv, wk, 0, 4, 128, dst, 132)
                for i in range(2):
                    b.memset(vst[i], 1.0, w=[f"vst{i}"])
                wv, wk = b.wload(wl[:, :, SK:SK + 256], [16, 256])
                fm_block(wv, wk, 0, 128, to_dram(KT, K_SWA))
                tm_chunk(wv, wk, 128, 2, 64, VS, 68)
                for i in range(2):
                    b.memset(vst[i], 1.0, w=[f"vst{i}"])
                wv, wk = b.wload(wl[:, :, CQ:CQ + 512], [16, 512])
                for blk in range(4):
                    def ev(q, pap, bkey, blk=blk):
                        b.act(cqraw[:, blk, q * 512:(q + 1) * 512], pap, AF.Copy, r=[bkey], w=["cqraw"])
                    fm_block(wv, wk, blk * 128, 128, ev)
                wv, wk = b.wload(wl[:, :, CKV:CKV + 320], [16, 320])
                for blk in range(2):
                    def ev(q, pap, bkey, blk=blk):
                        b.act(ckvraw[:, blk, q * 512:(q + 1) * 512], pap, AF.Copy, r=[bkey], w=["ckvraw"])
                    fm_block(wv, wk, blk * 128, 128, ev)

                def evA(q, pap, bkey):
                    b.act(krA[0:64, q * 512:(q + 1) * 512], pap, AF.Copy, r=[bkey], w=["krA"])
                fm_block(wv, wk, 256, 64, evA)
                wv, wk = b.wload(w_krsw[l].rearrange("(k p) n -> p k n", p=128), [16, 64])

                def evB(q, pap, bkey):
                    b.act(krB[0:64, q * 512:(q + 1) * 512], pap, AF.Copy, r=[bkey], w=["krB"])
                fm_block(wv, wk, 0, 64, evB)
                for ch in range(16):
                    wv, wk = b.wload(wl[:, :, ZG + ch * 512:ZG + (ch + 1) * 512], [16, 512])
                    for blk in range(4):
                        fm_block(wv, wk, blk * 128, 128, to_dram(GT, ch * 512 + blk * 128, AF.Sigmoid))

                mm0 = b.mark()
                sq = b.alloc([4, G1], BF16)
                rq = b.alloc([G1], F32)
                rkv = b.alloc([G1], F32)
                rtok = b.alloc([G1 // 128], F32)
                t1 = b.alloc([G1], F32)
                t2 = b.alloc([G1], F32)
                csl = slice(t0, t0 + G1)

                def ssq_bc(raw, nblk, rout, rkey, mul):
                    b.tt(sq[:, 0:nblk, :], raw, raw, ALU.mult, r=[rkey], w=["sq"])
                    for q in range(G1 // 512):
                        bk = b.bank()
                        for kb in range(nblk):
                            b.mm(b.pb(bk), ones, sq[:, kb, q * 512:(q + 1) * 512], kb == 0, kb == nblk - 1,
                                 r=["sq", "ones"], w=[f"B{bk}"])
                        b.ts(rout[:, q * 512:(q + 1) * 512], b.pb(bk), mul, EPS, ALU.mult, ALU.add,
                             r=[f"B{bk}"], w=["rr"])
                    b.act(rout, rout, AF.Sqrt, r=["rr"], w=["rr"])
                    b.recip(rout, rout, r=["rr"], w=["rr"])

                ssq_bc(cqraw, 4, rq, "cqraw", 1.0 / 512)
                for kb in range(4):
                    b.ts(cqraw[:, kb, :], cqraw[:, kb, :], gq[:, kb:kb + 1], None, ALU.mult, None,
                         r=["cqraw", "gq", "sq"], w=["cqraw"])
                for h in range(4):
                    s = sti[0] % 2
                    sti[0] += 1
                    for q in range(G1 // 512):
                        bk = b.bank()
                        for kc in range(4):
                            b.mm(b.pb(bk), wuq[:, kc, h * 192:h * 192 + 128], cqraw[:, kc, q * 512:(q + 1) * 512],
                                 kc == 0, kc == 3, r=["wuq", "cqraw"], w=[f"B{bk}"])
                        b.tt(stg[s][:, q * 512:(q + 1) * 512], b.pb(bk), rq[:, q * 512:(q + 1) * 512], ALU.mult,
                             r=[f"B{bk}", "rr"], w=[f"stg{s}"])
                    b.st(QT[Q_MN + h * 128:Q_MN + (h + 1) * 128, csl], stg[s], f"stg{s}", "dram")
                    s = sti[0] % 2
                    sti[0] += 1
                    for q in range(G1 // 512):
                        qs = slice(q * 512, (q + 1) * 512)
                        bka, bkb = b.bank(), b.bank()
                        for kc in range(4):
                            b.mm(b.pb(bka)[0:64], wuq[:, kc, h * 192 + 128:h * 192 + 192], cqraw[:, kc, qs],
                                 kc == 0, kc == 3, r=["wuq", "cqraw"], w=[f"B{bka}"])
                        for kc in range(4):
                            b.mm(b.pb(bkb)[0:64], wuq[:, kc, 768 + h * 64:768 + (h + 1) * 64], cqraw[:, kc, qs],
                                 kc == 0, kc == 3, r=["wuq", "cqraw"], w=[f"B{bkb}"])
                        gs = slice(t0 + q * 512, t0 + (q + 1) * 512)
                        b.tt(t1[0:64, qs], b.pb(bka)[0:64], cos2[0:64, gs], ALU.mult, r=[f"B{bka}", "cos2"], w=["t1"])
                        b.tt(t2[0:64, qs], b.pb(bkb)[0:64], sinS[0:64, gs], ALU.mult, r=[f"B{bkb}", "sinS"], w=["t2"])
                        b.tt(t1[0:64, qs], t1[0:64, qs], t2[0:64, qs], ALU.add, r=["t1", "t2"], w=["t1"])
                        b.tt(stg[s][0:64, qs], t1[0:64, qs], rq[0:64, qs], ALU.mult, r=["t1", "rr"], w=[f"stg{s}"])
                    b.st(QT[Q_MR + h * 64:Q_MR + (h + 1) * 64, csl], stg[s][0:64, :], f"stg{s}", "dram")
                ssq_bc(ckvraw, 2, rkv, "ckvraw", 1.0 / 256)
                bk = b.bank()
                for t in range(G1 // 128):
                    for kb in range(2):
                        b.mm(b.pb(bk)[:, t:t + 1], sq[:, kb, t * 128:(t + 1) * 128], ones[:, 0:1], kb == 0, kb == 1,
                             r=["sq", "ones"], w=[f"B{bk}"])
                b.rstd(rtok, b.pb(bk)[:, 0:G1 // 128], 1.0 / 256, r=[f"B{bk}"], w=["rtok"])
                for kb in range(2):
                    b.ts(ckvraw[:, kb, :], ckvraw[:, kb, :], gkv[:, kb:kb + 1], None, ALU.mult, None,
                         r=["ckvraw", "gkv", "sq"], w=["ckvraw"])
                for h in range(4):
                    s = sti[0] % 2
                    sti[0] += 1
                    for q in range(G1 // 512):
                        bk = b.bank()
                        for kc in range(2):
                            b.mm(b.pb(bk), wukv[:, kc, h * 256:h * 256 + 128], ckvraw[:, kc, q * 512:(q + 1) * 512],
                                 kc == 0, kc == 1, r=["wukv", "ckvraw"], w=[f"B{bk}"])
                        b.tt(stg[s][:, q * 512:(q + 1) * 512], b.pb(bk), rkv[:, q * 512:(q + 1) * 512], ALU.mult,
                             r=[f"B{bk}", "rr"], w=[f"stg{s}"])
                    b.st(KT[K_MN + h * 128:K_MN + (h + 1) * 128, csl], stg[s], f"stg{s}", "dram")
                wv_v = wukv.rearrange("p k (h two c) -> p k h two c", h=4, two=2)
                for t in range(G1 // 128):
                    bk = b.bank()
                    for kc in range(2):
                        b.mm(b.pb(bk).rearrange("p (a b) -> p a b", a=4), ckvraw[:, kc, t * 128:(t + 1) * 128],
                             wv_v[:, kc, :, 1, :], kc == 0, kc == 1, r=["wukv", "ckvraw"], w=[f"B{bk}"])
                    s = sti[0] % 2
                    sti[0] += 1
                    b.ts(vst[s][:, :, 0:128], b.pb(bk).rearrange("p (a b) -> p a b", a=4), rtok[:, t:t + 1], None,
                         ALU.mult, None, r=[f"B{bk}", "rtok"], w=[f"vst{s}"])
                    b.st(VM[t0 + t * 128:t0 + (t + 1) * 128, :], vst[s].rearrange("p a b -> p (a b)"),
                         f"vst{s}", "dram")
                s = sti[0] % 2
                sti[0] += 1
                gsl = slice(t0, t0 + G1)
                b.tt(t1[0:64], krA[0:64], cos2[0:64, gsl], ALU.mult, r=["krA", "cos2"], w=["t1"])
                b.tt(t2[0:64], krB[0:64], sinS[0:64, gsl], ALU.mult, r=["krB", "sinS"], w=["t2"])
                b.tt(stg[s][0:64], t1[0:64], t2[0:64], ALU.add, r=["t1", "t2"], w=[f"stg{s}"])
                b.st(KT[K_MR:K_MR + 64, csl], stg[s][0:64, :], f"stg{s}", "dram")
                b.release(mm0)
                P.barrier()
            _chk(1)

            b.release(layer_mark)
            NQT = NT // 512
            pt = [b.alloc([512], BF16) for _ in range(3)]
            osb = b.alloc([4, 128], F32)
            obf = b.alloc([4, 128], BF16)
            obf2 = [b.alloc([4, 128], BF16) for _ in range(2)]
            rs = b.alloc([8], F32)
            brs = [b.alloc([512], BF16) for _ in range(2)]
            vsb = b.alloc([NKB, 4 * 132], BF16)
            kts = [b.alloc([NT], BF16) for _ in range(2)]
            qts = [b.alloc([NT], BF16) for _ in range(2)]
            ktr = b.alloc([NT], BF16)
            qtr = b.alloc([NT], BF16)
            negT = b.alloc([NT], BF16)
            brc = [0]
            b.ring_set = [0, 1, 2]
            ocnt = [0]

            def obanks():
                ocnt[0] += 1
                return (3, 4) if ocnt[0] % 2 else (5, 6)

            pending = []

            def flush_pending():
                while pending:
                    pending.pop(0)()

            def dense_pass(T, chunks, hb, sc, ccol, exd_t, exs_t, vcol, obank, moba=False):
                nk = 4 * T + 4

                def qk(j):
                    smin = max(0, j - 4 * T)
                    q0 = smin * 128
                    sb_ = b.bank()
                    S = b.pb(sb_)
                    skey = f"B{sb_}"
                    ext = []
                    for s in range(smin, 4):
                        d = 4 * T + s - j
                        if d == 0:
                            ext.append((s, exd_t))
                        elif d == 1 and exs_t is not None:
                            ext.append((s, exs_t))
                    nmm = len(chunks) + len(ext) + (1 if moba else 0)
                    i = 0
                    for (kt_, qt_, rows, kkey, qkey) in chunks:
                        b.mm(S[:, q0:512], kt_[0:rows, j * 128:(j + 1) * 128], qt_[0:rows, T * 512 + q0:T * 512 + 512],
                             i == 0, i == nmm - 1, r=[kkey, qkey], w=[skey])
                        i += 1
                    for (s, ex) in ext:
                        b.mm(S[:, s * 128:(s + 1) * 128], ident, ex, False, i == nmm - 1,
                             r=["ident", "exd", "exs", "exdm"], w=[skey])
                        i += 1
                    if moba:
                        n = j // 2
                        b.mm(S[:, q0:512], esel[0:16, n, :], negT[0:16, T * 512 + q0:T * 512 + 512], False, True,
                             r=["esel", "negT"], w=[skey])
                    return (S, skey, smin, q0)

                infos = {0: qk(0)}
                if nk > 1:
                    infos[1] = qk(1)
                for j in range(nk):
                    if j + 2 < nk:
                        infos[j + 2] = qk(j + 2)
                    S, skey, smin, q0 = infos.pop(j)
                    pi_ = j % 3
                    b.act(pt[pi_][:, q0:512], S[:, q0:512], AF.Exp, r=[skey, "c31", "zcol"], w=[f"pt{pi_}"],
                          bias=ccol, scale=float(sc))
                    for s in range(smin, 4):
                        ob = obank[s // 2]
                        b.mm(b.pb(ob, 129, (s % 2) * 256), pt[pi_][:, s * 128:(s + 1) * 128],
                             vsb[:, j, vcol:vcol + 129], j == 0 and s % 2 == 0, j == 4 * T + s,
                             r=[f"pt{pi_}", "vsb"], w=[f"B{ob}"])
                    if j == 1 or nk == 1:
                        flush_pending()

            def emit_branch(T, row0):
                k = brc[0] % 2
                brc[0] += 1
                b.cp(obf2[k], obf, r=["obf"], w=[f"obf2{k}"])

                def tail(k=k, T=T, row0=row0):
                    bk = 7
                    tp = b.pb(bk).bitcast(BF16)
                    for s in range(4):
                        b.tr(tp[:, s * 128:(s + 1) * 128], obf2[k][:, s, :], ident, r=[f"obf2{k}", "ident"], w=[f"B{bk}"])
                    b.act(brs[k], tp[:, 0:512], AF.Copy, r=[f"B{bk}"], w=[f"brs{k}"])
                    b.st(BRT[row0:row0 + 128, T * 512:(T + 1) * 512], brs[k], f"brs{k}", "dram")
                pending.append(tail)

            def load_rows(dst, src, row0, rows, key):
                b.ld(dst[0:rows, :], src[row0:row0 + rows, :], key)

            b.ld(vsb, VD.rearrange("(j p) c -> p j c", p=128), "vsb")
            for h in range(4):
                for m in range(2):
                    load_rows(kts[m], KT, K_DIFF + (2 * h + m) * 64, 64, f"kts{m}")
                    load_rows(qts[m], QT, Q_DIFF + (2 * h + m) * 64, 64, f"qts{m}")
                for T in range(NQT):
                    for m in range(2):
                        ob = obanks()
                        dense_pass(T, [(kts[m], qts[m], 64, f"kts{m}", f"qts{m}")], h, SC_D, c31[:, h:h + 1],
                                   exd[:, h, :], exs[:, h, :], h * 132, ob)
                        for s in range(4):
                            O = b.pb(ob[s // 2], 129, (s % 2) * 256)
                            okey = f"B{ob[s // 2]}"
                            b.recip(rs[:, s:s + 1], O[:, 128:129], r=[okey], w=["rs"])
                            if m == 0:
                                b.ts(osb[:, s, :], O[:, 0:128], rs[:, s:s + 1], None, ALU.mult, None,
                                     r=[okey, "rs", "obf"], w=["osb"])
                            else:
                                b.tt(rs[:, s:s + 1], rs[:, s:s + 1], lamc[:, 2:3], ALU.mult, r=["rs", "lamc"], w=["rs"])
                                b.stt(osb[:, s, :], O[:, 0:128], rs[:, s:s + 1], osb[:, s, :], ALU.mult, ALU.add,
                                      r=[okey, "rs", "osb"], w=["osb"])
                    b.tt(obf, osb, osb, ALU.mult, r=["osb"], w=["obf"])
                    b.red(rs[:, 4:8], obf, ALU.add, r=["obf"], w=["rs"])
                    b.rstd(rs[:, 4:8], rs[:, 4:8], 1.0 / 128, r=["rs"], w=["rs"])
                    for s in range(4):
                        b.stt(obf[:, s, :], osb[:, s, :], rs[:, 4 + s:5 + s], subg, ALU.mult, ALU.mult,
                              r=["osb", "rs", "subg"], w=["obf"])
                    emit_branch(T, 0 * 512 + h * 128)

            def simple_post(ob, extra_col=None):
                for s in range(4):
                    O = b.pb(ob[s // 2], 129, (s % 2) * 256)
                    okey = f"B{ob[s // 2]}"
                    b.recip(rs[:, s:s + 1], O[:, 128:129], r=[okey], w=["rs"])
                    b.ts(obf[:, s, :], O[:, 0:128], rs[:, s:s + 1], None, ALU.mult, None, r=[okey, "rs"], w=["obf"])

            _chk(1.25)
            b.ld(vsb, VM.rearrange("(j p) c -> p j c", p=128), "vsb")
            load_rows(ktr, KT, K_MR, 64, "ktr")
            for h in range(4):
                load_rows(kts[0], KT, K_MN + h * 128, 128, "kts0")
                load_rows(qts[0], QT, Q_MN + h * 128, 128, "qts0")
                load_rows(qtr, QT, Q_MR + h * 64, 64, "qtr")
                for T in range(NQT):
                    ob = obanks()
                    dense_pass(T, [(kts[0], qts[0], 128, "kts0", "qts0"), (ktr, qtr, 64, "ktr", "qtr")], 0, SC_M,
                               zcol[:, 0:1], exdm, None, h * 132, ob)
                    simple_post(ob)
                    emit_branch(T, 1 * 512 + h * 128)

            flush_pending()
            _chk(1.5)
            b.ld(vsb, VB.rearrange("(j p) c -> p j c", p=128), "vsb")
            kmf = b.alloc([NB], F32)
            kmT = b.alloc([NB], BF16)
            gm = b.alloc([NB], F32)
            gm2 = b.alloc([NB], F32)
            gmk = b.alloc([NB], F32)
            nbf = b.alloc([NB], BF16)
            mx = b.alloc([1], F32)
            for h in range(4):
                load_rows(kts[0], KT, K_MOBA + h * 128, 128, "kts0")
                load_rows(qts[0], QT, Q_MOBA + h * 128, 128, "qts0")
                b.red(kmf, kts[0].rearrange("p (n k) -> p n k", k=256), ALU.add, r=["kts0"], w=["kmf"])
                b.ts(kmT, kmf, 1.0 / 256, None, ALU.mult, None, r=["kmf"], w=["kmT"])
                for i in range(NKB):
                    blk = i // 2
                    bk = 7
                    b.mm(b.pb(bk, NB), qts[0][:, i * 128:(i + 1) * 128], kmT, True, True, r=["qts0", "kmT"],
                         w=[f"B{bk}"])
                    b.tt(gm, b.pb(bk, NB), elig[:, blk, :], ALU.add, r=[f"B{bk}", "elig"], w=["gm"])
                    b.cp(gm2, gm, r=["gm"], w=["gm2"])
                    for it in range(3):
                        b.red(mx, gm2, ALU.max, r=["gm2"], w=["mx"])
                        if it < 2:
                            b.ts(gmk, gm2, mx[:, 0:1], NEG * 1e20, ALU.is_ge, ALU.mult, r=["gm2", "mx"], w=["gmk"])
                            b.tt(gm2, gm2, gmk, ALU.add, r=["gm2", "gmk"], w=["gm2"])
                    b.ts(gmk, gm, mx[:, 0:1], NEG, ALU.is_lt, ALU.mult, r=["gm", "mx"], w=["gmk"])
                    b.tt(nbf, gmk, keep[:, blk, :], ALU.mult, r=["gmk", "keep"], w=["nbf"])
                    bk2 = 7
                    tp = b.pb(bk2).bitcast(BF16)
                    b.tr(tp[0:NB, 0:128], nbf, ident, r=["nbf", "ident"], w=[f"B{bk2}"])
                    b.act(negT[0:NB, i * 128:(i + 1) * 128], tp[0:NB, 0:128], AF.Copy, r=[f"B{bk2}"], w=["negT"])
                for T in range(NQT):
                    ob = obanks()
                    hb = 12 + h
                    dense_pass(T, [(kts[0], qts[0], 128, "kts0", "qts0")], hb, SC_B, c31[:, hb:hb + 1],
                               exd[:, hb, :], exs[:, hb, :], h * 132, ob, moba=True)
                    simple_post(ob)
                    emit_branch(T, 3 * 512 + h * 128)

            flush_pending()
            _chk(1.75)
            vss = vsb.rearrange("p j c -> p (j c)")[:, 0:NKB * 136].rearrange("p (j c) -> p j c", c=136)
            b.ld(vss, VS.rearrange("(j p) c -> p j c", p=128), "vsb")
            for pr in range(4):
                kvh = (2 * pr) // 4
                load_rows(kts[0], KT, K_SWA + kvh * 64, 64, "kts0")
                for hh in range(2):
                    load_rows(qts[hh], QT, Q_SWA + (2 * pr + hh) * 64, 64, f"qts{hh}")
                for i in range(NKB):
                    for hh in range(2):
                        hq = 2 * pr + hh
                        hb = 4 + hq
                        sb_ = b.bank()
                        S = b.pb(sb_)
                        skey = f"B{sb_}"
                        qsl = qts[hh][0:64, i * 128:(i + 1) * 128]
                        lo = 0 if i > 0 else 1
                        for kk in range(lo, 2):
                            kb = i - 1 + kk
                            b.mm(S[:, kk * 128:(kk + 1) * 128], kts[0][0:64, kb * 128:(kb + 1) * 128], qsl, True, False,
                                 r=["kts0", f"qts{hh}"], w=[skey])
                            b.mm(S[:, kk * 128:(kk + 1) * 128], ident, (exs if kk == 0 else exd)[:, hb, :], False, True,
                                 r=["ident", "exd", "exs"], w=[skey])
                        pi_ = (i * 2 + hh) % 3
                        b.act(pt[pi_][:, lo * 128:256], S[:, lo * 128:256], AF.Exp, r=[skey, "zcol"], w=[f"pt{pi_}"],
                              bias=zcol[:, 0:1], scale=float(SC_S))
                        ob = 3 + (i * 2 + hh) % 4
                        O = b.pb(ob, 65)
                        for kk in range(lo, 2):
                            kb = i - 1 + kk
                            b.mm(O, pt[pi_][:, kk * 128:(kk + 1) * 128], vss[:, kb, kvh * 68:kvh * 68 + 65], kk == lo,
                                 kk == 1, r=[f"pt{pi_}", "vsb"], w=[f"B{ob}"])
                        b.tt(rs[:, hh:hh + 1], O[:, 64:65], esink[:, hq:hq + 1], ALU.add, r=[f"B{ob}", "esink"], w=["rs"])
                        b.recip(rs[:, hh:hh + 1], rs[:, hh:hh + 1], r=["rs"], w=["rs"])
                        b.ts(obf[:, i % 4, hh * 64:(hh + 1) * 64], O[:, 0:64], rs[:, hh:hh + 1], None, ALU.mult, None,
                             r=[f"B{ob}", "rs"], w=["obf"])
                    if i % 4 == 3:
                        emit_branch(i // 4, 2 * 512 + pr * 128)
                        flush_pending()
            flush_pending()
            P.barrier()
            b.ring_set = list(range(8))
            _chk(2)

            b.release(layer_mark)
            brT = b.alloc([16, GM], BF16)
            mT = b.alloc([16, GM], BF16)
            gts = [b.alloc([4, GM], BF16) for _ in range(2)]
            acc = b.alloc([512], F32)
            tmp = b.alloc([512], F32)
            xts = [b.alloc([512], F32) for _ in range(2)]
            for g in range(NT // GM):
                t0 = g * GM
                b.ld(brT, BRT[:, t0:t0 + GM].rearrange("(k p) t -> p k t", p=128), "brT")
                gi = 0
                for cc in range(4):
                    wv, wk = b.wload(w_br[l].rearrange("i (k p) n -> p (i k) n", p=128)[:, :, cc * 512:(cc + 1) * 512],
                                     [16, 512])
                    for cb in range(4):
                        c = cc * 4 + cb
                        gsel = gi % 2
                        gi += 1
                        gt_ = gts[gsel]
                        gkey = f"gts{gsel}"
                        P.dma("sp", lambda e, gt_=gt_, c=c, t0=t0: e.dma_start(
                            out=gt_, in_=GT.rearrange("(i r) t -> r i t", i=4)[c * 128:(c + 1) * 128, :, t0:t0 + GM]),
                            gkey, w=[gkey])
                        for hf in range(GM // 512):
                            hs = slice(hf * 512, (hf + 1) * 512)
                            bks = [b.bank() for _ in range(4)]
                            for i in range(4):
                                for kc in range(4):
                                    b.mm(b.pb(bks[i]), wv[:, i * 4 + kc, cb * 128:(cb + 1) * 128], brT[:, i * 4 + kc, hs],
                                         kc == 0, kc == 3, r=[wk, "brT"], w=[f"B{bks[i]}"])
                            b.tt(acc, b.pb(bks[0]), gt_[:, 0, hs], ALU.mult, r=[f"B{bks[0]}", gkey], w=["acc"])
                            for i in range(1, 4):
                                b.tt(tmp, b.pb(bks[i]), gt_[:, i, hs], ALU.mult, r=[f"B{bks[i]}", gkey], w=["tmp"])
                                if i < 3:
                                    b.tt(acc, acc, tmp, ALU.add, r=["acc", "tmp"], w=["acc"])
                                else:
                                    b.tt(mT[:, c, hs], acc, tmp, ALU.add, r=["acc", "tmp"], w=["mT"])
                xi = 0
                for cc in range(4):
                    cs = slice(cc * 512, (cc + 1) * 512)
                    wv, wk = b.wload(w_out[l].rearrange("(k p) n -> p k n", p=128)[:, :, cs], [16, 512])
                    for t in range(GM // 128):
                        rows = slice(t0 + t * 128, t0 + (t + 1) * 128)
                        k = xi % 2
                        xi += 1
                        b.ld(xts[k], xsrc[rows, cs], f"xts{k}", r=["dramx"])
                        bk = b.bank()
                        for kc in range(16):
                            b.mm(b.pb(bk), mT[:, kc, t * 128:(t + 1) * 128], wv[:, kc, :], kc == 0, kc == 15,
                                 r=[wk, "mT"], w=[f"B{bk}"])
                        b.tt(xts[k], xts[k], b.pb(bk), ALU.add, r=[f"xts{k}", f"B{bk}"], w=[f"xts{k}"])
                        b.st(XA[rows, cs], xts[k], f"xts{k}", "dramxa")
            P.barrier()
            _chk(3)

            b.release(layer_mark)
            h2T = b.alloc([16, GF + 2], BF16)
            g2b = b.alloc([D], F32)
            b.ld(g2b, g2_in[l], "g2b")
            gT = b.alloc([44, GF], BF16)
            xt = b.alloc([D], F32)
            hj = b.alloc([D], BF16)
            ssq = b.alloc([1], F32)
            asb = b.alloc([GF + 2], F32)
            c0t = b.alloc([GF], F32)
            c1t = b.alloc([GF], F32)
            xts = [b.alloc([256], F32) for _ in range(2)]
            for g in range(NT // GF):
                t0 = g * GF
                if g == 0:
                    b.memset(h2T[:, :, 0:2], 0.0, w=["hT"])
                else:
                    b.cp(h2T[:, :, 0:2], h2T[:, :, GF:GF + 2], r=["hT"], w=["hT"])
                for t in range(GF // 128):
                    norm_tile(XA[t0 + t * 128:t0 + (t + 1) * 128, :], g2b, h2T, 2 + t * 128, xt, hj, ssq,
                              "xt", "hj", "g2b")
                wu = w_up[l].rearrange("(k p) (two f) -> p k two f", p=128, two=2)
                for cp_ in range(22):
                    wv, wk = b.wload([wu[:, :, 0, cp_ * 256:(cp_ + 1) * 256], wu[:, :, 1, cp_ * 256:(cp_ + 1) * 256]],
                                     [16, 2, 256])
                    for c2 in range(2):
                        cb = cp_ * 2 + c2
                        cs = slice(c2 * 128, (c2 + 1) * 128)
                        ba, bh, bv = b.bank(), b.bank(), b.bank()
                        for kc in range(16):
                            b.mm(b.pb(ba), wv[:, kc, 0, cs], h2T[:, kc, 2:GF + 2], kc == 0, kc == 15, r=[wk, "hT"],
                                 w=[f"B{ba}"])
                        for kc in range(16):
                            b.mm(b.pb(bh, 2), wv[:, kc, 0, cs], h2T[:, kc, 0:2], kc == 0, kc == 15, r=[wk, "hT"],
                                 w=[f"B{bh}"])
                        for kc in range(16):
                            b.mm(b.pb(bv), wv[:, kc, 1, cs], h2T[:, kc, 2:GF + 2], kc == 0, kc == 15, r=[wk, "hT"],
                                 w=[f"B{bv}"])
                        b.act(asb[:, 0:2], b.pb(bh, 2), AF.Copy, r=[f"B{bh}"], w=["asb"])
                        b.act(asb[:, 2:GF + 2], b.pb(ba), AF.Copy, r=[f"B{ba}"], w=["asb"])
                        b.act(c0t, asb[:, 2:GF + 2], AF.Identity, r=["asb", "cw", "cbias"], w=["c0t"],
                              bias=cbias[:, cb:cb + 1], scale=cw[:, 2, cb:cb + 1])
                        b.stt(c1t, asb[:, 1:GF + 1], cw[:, 1, cb:cb + 1], c0t, ALU.mult, ALU.add,
                              r=["asb", "cw", "c0t"], w=["c1t"])
                        b.stt(c0t, asb[:, 0:GF], cw[:, 0, cb:cb + 1], c1t, ALU.mult, ALU.add,
                              r=["asb", "cw", "c1t"], w=["c0t"])
                        b.act(c1t, c0t, AF.Gelu, r=["c0t"], w=["c1t"])
                        b.tt(gT[:, cb, :], c1t, b.pb(bv), ALU.mult, r=["c1t", f"B{bv}"], w=["gT"])
                xi = 0
                for cc in range(8):
                    cs = slice(cc * 256, (cc + 1) * 256)
                    wv, wk = b.wload(w_dn[l].rearrange("(k p) n -> p k n", p=128)[:, :, cs], [44, 256])
                    for t in range(GF // 128):
                        rows = slice(t0 + t * 128, t0 + (t + 1) * 128)
                        k = xi % 2
                        xi += 1
                        b.ld(xts[k], XA[rows, cs], f"xts{k}")
                        bk = b.bank()
                        for kc in range(44):
                            b.mm(b.pb(bk, 256), gT[:, kc, t * 128:(t + 1) * 128], wv[:, kc, :], kc == 0, kc == 43,
                                 r=[wk, "gT"], w=[f"B{bk}"])
                        b.tt(xts[k], xts[k], b.pb(bk, 256), ALU.add, r=[f"xts{k}", f"B{bk}"], w=[f"xts{k}"])
                        b.st(XB[rows, cs], xts[k], f"xts{k}", "dramxb")
            P.barrier()

        except _Stop:
            pass
        b.release(persist)
        gfb = b.alloc([D], F32)
        xtf = [b.alloc([D], F32) for _ in range(2)]
        hjf = b.alloc([D], F32)
        ssq = b.alloc([1], F32)
        b.ld(gfb, gf_in, "gfb")
        finals = []
        for t in range(NT // 128):
            k = t % 2
            rows = slice(t * 128, (t + 1) * 128)
            b.ld(xtf[k], XB[rows, :], f"xtf{k}")
            b.tt(hjf, xtf[k], xtf[k], ALU.mult, r=[f"xtf{k}"], w=["hjf"])
            b.red(ssq, hjf, ALU.add, r=["hjf"], w=["ssq"])
            b.rstd(ssq, ssq, 1.0 / D, r=["ssq"], w=["ssq"])
            b.stt(xtf[k], xtf[k], ssq[:, 0:1], gfb, ALU.mult, ALU.mult, r=[f"xtf{k}", "ssq", "gfb"], w=[f"xtf{k}"])
            finals.append(b.st(out_d[rows, :], xtf[k], f"xtf{k}", "dramout"))
        P.emit(finals[-2:] if stop >= 99 else list(P.last_dma.values()))
    return nc


def _t5_bucket(n):
    n = np.maximum(n, 0)
    nf = np.maximum(n, 1).astype(np.float32)
    large = 16 + (np.log(nf / np.float32(16)) / np.float32(math.log(8)) * np.float32(16)).astype(np.int32)
    return np.where(n < 16, n, np.minimum(large, 31))


def host_consts(NT):
    NB = NT // 256
    kk = np.arange(128)[:, None]
    qq = np.arange(128)[None, :]
    c = {}
    c["ident"] = np.eye(128, dtype=np.float32).astype(ml_dtypes.bfloat16)
    c["ones"] = np.ones((128, 128), dtype=np.float32).astype(ml_dtypes.bfloat16)
    c["mdiag"] = np.where(qq >= kk, 0.0, NEG * 8).astype(np.float32)
    c["mswa"] = np.where(qq < kk, 0.0, NEG * 8).astype(np.float32)
    es = np.zeros((16, 16, 128), dtype=np.float32)
    for n in range(16):
        es[n, n, :] = 1.0
    c["esel"] = es.reshape(16, 16 * 128).astype(ml_dtypes.bfloat16)
    el = np.zeros((128, NB, NB), dtype=np.float32)
    kp = np.zeros((128, NB, NB), dtype=np.float32)
    for bq in range(NB):
        el[:, bq, bq:] = -1e30
        kp[:, bq, :bq] = 1.0
    c["elig"] = el.reshape(128, NB * NB)
    c["keep"] = kp.reshape(128, NB * NB)
    inv = (10000.0 ** (-np.arange(0, 64, 2, dtype=np.float32) / np.float32(64))).astype(np.float32)
    invf = np.zeros((64, 2), dtype=np.float32)
    invf[:, 0] = np.concatenate([inv, inv])
    invf[:, 1] = np.concatenate([-np.ones(32), np.ones(32)])
    c["invf"] = invf
    c["_bd"] = _t5_bucket(qq - kk)
    c["_bs"] = _t5_bucket(128 + qq - kk)
    return c


def prep_inputs(inp, NT, DEPTH, b):
    c = host_consts(NT)
    rb = np.asarray(inp["rel_bias"], dtype=np.float32)
    m = {}
    m["x"] = np.ascontiguousarray(inp["x"][b])
    m["pos"] = np.ascontiguousarray(np.broadcast_to(np.asarray(inp["positions"], dtype=np.int32)[None, :NT], (64, NT)))
    m["td"] = np.ascontiguousarray(rb[c["_bd"]].transpose(0, 2, 1)).reshape(128, 16 * 128)
    m["tsb"] = np.ascontiguousarray(rb[c["_bs"]].transpose(0, 2, 1)).reshape(128, 16 * 128)
    m["c31"] = np.ascontiguousarray(np.broadcast_to(rb[31:32, :], (128, 16)))
    for k in ("ident", "ones", "mdiag", "mswa", "esel", "elig", "keep", "invf"):
        m[k] = c[k]
    return m


_SHARED = {}


def rep128(a):
    a = np.asarray(a, dtype=np.float32)
    return np.ascontiguousarray(np.broadcast_to(a[:, None, :], (a.shape[0], 128, a.shape[1])))


def shared_inputs(inp, DEPTH):
    w_in = np.asarray(inp["w_in"], dtype=np.float32)
    s = {}
    s["w_in"] = w_in
    s["w_krsw"] = np.ascontiguousarray(np.concatenate([w_in[:, :, KR + 32:KR + 64], w_in[:, :, KR:KR + 32]], axis=2))
    wuq = np.asarray(inp["mla_w_uq"], dtype=np.float32)
    sw = [np.concatenate([wuq[:, :, h * 192 + 160:h * 192 + 192], wuq[:, :, h * 192 + 128:h * 192 + 160]], axis=2)
          for h in range(4)]
    s["w_uq"] = np.ascontiguousarray(np.concatenate([wuq] + sw, axis=2))
    s["w_ukv"] = np.asarray(inp["mla_w_ukv"], dtype=np.float32)
    s["w_br"] = np.asarray(inp["w_branch"], dtype=np.float32)
    s["w_out"] = np.asarray(inp["w_out"], dtype=np.float32)
    s["w_up"] = np.asarray(inp["ffn_w_up"], dtype=np.float32)
    s["w_dn"] = np.asarray(inp["ffn_w_down"], dtype=np.float32)
    s["g1"] = rep128(inp["norm1_g"])
    s["g2"] = rep128(inp["norm2_g"])
    s["gf"] = np.ascontiguousarray(np.broadcast_to(np.asarray(inp["final_norm_g"], dtype=np.float32)[None, :], (128, D)))
    s["gq"] = np.ascontiguousarray(np.asarray(inp["mla_q_norm_g"], dtype=np.float32).reshape(DEPTH, 4, 128).transpose(0, 2, 1))
    s["gkv"] = np.ascontiguousarray(np.asarray(inp["mla_kv_norm_g"], dtype=np.float32).reshape(DEPTH, 2, 128).transpose(0, 2, 1))
    s["subg"] = rep128(inp["diff_subln_g"])
    s["lam"] = rep128(np.asarray(inp["diff_lambda"], dtype=np.float32).reshape(DEPTH, 256))
    s["sinks"] = rep128(inp["swa_sinks"])
    cw = np.asarray(inp["ffn_conv_w"], dtype=np.float32).reshape(DEPTH, 3, 44, 128)
    s["convw"] = np.ascontiguousarray(cw.transpose(0, 3, 1, 2)).reshape(DEPTH, 128, 3 * 44)
    cb = np.asarray(inp["ffn_conv_b"], dtype=np.float32).reshape(DEPTH, 44, 128)
    s["convb"] = np.ascontiguousarray(cb.transpose(0, 2, 1))
    return s


def run(inp, NT, DEPTH, BATCH, n_cores=8, stop=99):
    nc = build(NT, DEPTH, stop)
    sh = shared_inputs(inp, DEPTH)
    in_maps = []
    if n_cores >= 2 * BATCH:
        active = {2 * bi: bi for bi in range(BATCH)}
    else:
        active = {bi: bi for bi in range(BATCH)}
    zmap = None
    for c in range(n_cores):
        if c in active:
            m = dict(sh)
            m.update(prep_inputs(inp, NT, DEPTH, active[c]))
        else:
            if zmap is None:
                ref = dict(sh)
                ref.update(prep_inputs(inp, NT, DEPTH, 0))
                zmap = {k: np.zeros_like(v) for k, v in ref.items()}
            m = zmap
        in_maps.append(m)
    res = run_bass_kernel_spmd(nc, in_maps, core_ids=list(range(n_cores)))
    global LAST
    LAST = res.results
    inv = {bi: c for c, bi in active.items()}
    return np.stack([res.results[inv[bi]]["out"] for bi in range(BATCH)], axis=0)


def kernel(**inputs):
    inp = {k: np.asarray(v) for k, v in inputs.items()}
    return run(inp, 4096, 4, 4).astype(np.float32)
```

```python
import contextlib
import math
import numpy as np
import ml_dtypes
import concourse.bass as bass
import concourse.mybir as mybir
from concourse.bass_utils import run_bass_kernel_spmd

F32 = mybir.dt.float32
BF16 = mybir.dt.bfloat16
I32 = mybir.dt.int32
AF = mybir.ActivationFunctionType
ALU = mybir.AluOpType
AX = mybir.AxisListType
ENGS = ("pe", "act", "dve", "pool", "sp")

D = 2048
INC = 12864
FF = 5632
EPS = 1e-6
NEG = -30000.0
AQ, AK, AV, CQ, CKV, KR, SQ, SK, SV, MQ, MK, MV, ZG = (0, 512, 1024, 1536, 2048, 2304, 2368, 2880, 3008,
                                                        3136, 3648, 4160, 4672)
Q_DIFF, Q_MN, Q_MR, Q_SWA, Q_MOBA, QROWS = 0, 512, 1024, 1280, 1792, 2304
K_DIFF, K_MN, K_MR, K_SWA, K_MOBA, KROWS = 0, 512, 1024, 1088, 1216, 1728


class Op:
    __slots__ = ("eng", "fn", "deps", "signal", "tok", "is_dma", "semkey", "ninc", "idx", "epoch")

    def __init__(self, eng, fn, is_dma=False, semkey=None):
        self.eng, self.fn, self.is_dma, self.semkey = eng, fn, is_dma, semkey
        self.deps, self.signal, self.tok, self.ninc, self.idx, self.epoch = [], False, None, 0, -1, 0


class Prog:
    def __init__(self, nc):
        self.nc = nc
        self.ops = {e: [] for e in ENGS}
        self.last_w = {}
        self.readers = {}
        self.n = 0
        self.epoch = 0
        self.bar = []
        self.last_eng = {}
        self.last_dma = {}

    def barrier(self):
        b = [o for o in self.last_eng.values()] + [o for o in self.last_dma.values()]
        for o in b:
            o.signal = True
        self.bar = b
        self.last_w.clear()
        self.readers.clear()
        self.epoch += 1

    def _add(self, op, reads, writes):
        deps = set(self.bar)
        for k in reads:
            w = self.last_w.get(k)
            if w is not None:
                deps.add(w)
        for k in writes:
            w = self.last_w.get(k)
            if w is not None:
                deps.add(w)
            deps.update(self.readers.get(k, ()))
        deps.discard(op)
        keep = []
        for d in deps:
            if d.eng == op.eng and not d.is_dma and not op.is_dma and op.eng == "pe":
                continue
            keep.append(d)
            d.signal = True
        op.deps = keep
        op.idx = self.n
        op.epoch = self.epoch
        self.n += 1
        self.ops[op.eng].append(op)
        if op.is_dma:
            self.last_dma[op.semkey] = op
        else:
            self.last_eng[op.eng] = op
        for k in reads:
            lst = self.readers.setdefault(k, [])
            if not op.is_dma:
                for i_, o_ in enumerate(lst):
                    if o_.eng == op.eng and not o_.is_dma:
                        lst[i_] = op
                        break
                else:
                    lst.append(op)
            else:
                lst.append(op)
        for k in writes:
            self.last_w[k] = op
            self.readers[k] = []
        return op

    def op(self, eng, fn, r=(), w=()):
        return self._add(Op(eng, fn), r, w)

    def dma(self, eng, fn, semkey, r=(), w=(), n=1):
        o = Op(eng, fn, True, semkey)
        o.signal = True
        o.ninc = n
        return self._add(o, r, w)

    def emit(self, finals):
        nc = self.nc
        with contextlib.ExitStack() as es:
            eng_sems, eng_cnt, dma_sems, dma_cnt = {}, {}, {}, {}
            for e in ENGS:
                for o in self.ops[e]:
                    if o.is_dma or not o.signal:
                        continue
                    key = (e, o.epoch // 7)
                    if key not in eng_sems:
                        eng_sems[key] = es.enter_context(nc.semaphore(f"s_{e}_{key[1]}"))
                        eng_cnt[key] = 0
                    eng_cnt[key] += 1
                    o.tok = (eng_sems[key], eng_cnt[key])
            alld = sorted((o for e in ENGS for o in self.ops[e] if o.is_dma), key=lambda o: o.idx)
            for o in alld:
                k = o.semkey
                if k not in dma_sems:
                    dma_sems[k] = es.enter_context(nc.semaphore(f"d_{k}"))
                    dma_cnt[k] = 0
                dma_cnt[k] += 16 * o.ninc
                o.tok = (dma_sems[k], dma_cnt[k])
            self.nsems = len(eng_sems) + len(dma_sems)
            with nc.Block() as block:
                def run(engname, eh):
                    waited = {}
                    for o in self.ops[engname]:
                        for d in o.deps:
                            sem, cnt = d.tok
                            if waited.get(id(sem), 0) >= cnt:
                                continue
                            waited[id(sem)] = cnt
                            eh.wait_ge(sem, cnt)
                        res = o.fn(eh)
                        if o.is_dma:
                            if not isinstance(res, (list, tuple)):
                                res = [res]
                            assert len(res) == o.ninc
                            for ins in res:
                                ins.then_inc(o.tok[0], 16)
                        elif o.signal:
                            res.then_inc(o.tok[0], 1)
                    if engname == "sp":
                        for d in finals:
                            eh.wait_ge(d.tok[0], d.tok[1])

                @block.tensor
                def _(e):
                    run("pe", e)

                @block.scalar
                def _(e):
                    run("act", e)

                @block.vector
                def _(e):
                    run("dve", e)

                @block.gpsimd
                def _(e):
                    run("pool", e)

                @block.sync
                def _(e):
                    run("sp", e)


class B:
    def __init__(self, nc, P, arena, ps):
        self.nc, self.P, self.arena, self.ps = nc, P, arena, ps
        self.off = 0
        self.ring = 0
        self.wslot = 0

    def mark(self):
        return self.off

    def release(self, m):
        self.off = m

    def alloc(self, free, dt, parts=128):
        n = int(np.prod(free))
        sz = n * (4 if dt in (F32, I32) else 2)
        off = (self.off + 63) // 64 * 64
        self.off = off + sz
        assert self.off <= self.arena_bytes, ("SBUF arena overflow", self.off)
        v = self.arena[:, off // 2:(off + sz) // 2]
        if dt != BF16:
            v = v.bitcast(dt)
        if len(free) == 2:
            v = v.rearrange("p (a b) -> p a b", a=free[0])
        elif len(free) == 3:
            v = v.rearrange("p (a b c) -> p a b c", a=free[0], b=free[1])
        return v

    ring_set = list(range(8))

    def bank(self):
        self.ring = (self.ring + 1) % len(self.ring_set)
        return self.ring_set[self.ring]

    def pb(self, b, n=512, off=0):
        return self.ps[:, b * 512 + off:b * 512 + off + n]

    def mm(self, out, lhsT, rhs, start, stop, r, w):
        return self.P.op("pe", lambda e: e.matmul(out, lhsT=lhsT, rhs=rhs, start=start, stop=stop,
                                                  skip_group_check=True), r=r, w=w)

    def tr(self, out, in_, ident, r, w):
        return self.P.op("pe", lambda e: e.transpose(out, in_, ident), r=r, w=w)

    def act(self, out, in_, func, r, w, bias=None, scale=None):
        kw = {}
        if bias is not None:
            kw["bias"] = bias
        if scale is not None:
            kw["scale"] = scale
        return self.P.op("act", lambda e: e.activation(out=out, in_=in_, func=func, **kw), r=r, w=w)

    def ts(self, out, in0, s1, s2, op0, op1, r, w, eng="dve"):
        if op1 is None:
            return self.P.op(eng, lambda e: e.tensor_scalar(out=out, in0=in0, scalar1=s1, scalar2=None, op0=op0),
                             r=r, w=w)
        return self.P.op(eng, lambda e: e.tensor_scalar(out=out, in0=in0, scalar1=s1, scalar2=s2, op0=op0,
                                                        op1=op1), r=r, w=w)

    def tt(self, out, in0, in1, op, r, w, eng="dve"):
        return self.P.op(eng, lambda e: e.tensor_tensor(out=out, in0=in0, in1=in1, op=op), r=r, w=w)

    def stt(self, out, in0, scalar, in1, op0, op1, r, w, eng="dve"):
        return self.P.op(eng, lambda e: e.scalar_tensor_tensor(out=out, in0=in0, scalar=scalar, in1=in1,
                                                               op0=op0, op1=op1), r=r, w=w)

    def cp(self, out, in_, r, w, eng="dve"):
        return self.P.op(eng, lambda e: e.tensor_copy(out=out, in_=in_), r=r, w=w)

    def red(self, out, in_, op, r, w):
        return self.P.op("dve", lambda e: e.tensor_reduce(out=out, in_=in_, axis=AX.X, op=op), r=r, w=w)

    def recip(self, out, in_, r, w):
        return self.P.op("dve", lambda e: e.reciprocal(out=out, in_=in_), r=r, w=w)

    def memset(self, ap, val, w):
        return self.P.op("dve", lambda e: e.memset(ap, val), r=(), w=w)

    def ld(self, out, in_, key, r=(), eng="sp"):
        return self.P.dma(eng, lambda e: e.dma_start(out=out, in_=in_), key, r=r, w=[key])

    def st(self, out, in_, key, dkey, eng="sp"):
        return self.P.dma(eng, lambda e: e.dma_start(out=out, in_=in_), key, r=[key], w=[dkey])

    _pend = None
    wc = None

    def wload(self, src, shape, cache=None):
        s = self.wslot
        self.wslot ^= 1
        key = f"wbuf{s}"
        n = int(np.prod(shape))
        assert n * 2 <= self.wbytes
        flat = self.wbufs[s][:, 0:n]
        v = flat
        if len(shape) == 2:
            v = v.rearrange("p (a b) -> p a b", a=shape[0])
        elif len(shape) == 3:
            v = v.rearrange("p (a b c) -> p a b c", a=shape[0], b=shape[1])
        cap = None
        if cache is not None:
            if self.wc is None:
                self.wc = {}
            name, g = cache
            if name not in self.wc:
                self.wc[name] = self.nc.dram_tensor(f"wc_{name}", [128, n], BF16).ap()
            cap = self.wc[name]
            ckey = f"wc_{name}"
        if cap is not None and cache[1] > 0:
            self.P.dma("pool", lambda e: e.dma_start(out=flat, in_=cap), key, r=[ckey], w=[key])
        elif isinstance(src, (list, tuple)):
            self.P.dma("pool", lambda e: [e.dma_start(out=v[:, :, i, :], in_=sr) for i, sr in enumerate(src)],
                       key, r=(), w=[key], n=len(src))
        else:
            self.P.dma("pool", lambda e: e.dma_start(out=v, in_=src), key, r=(), w=[key])
        pend = self._pend
        self._pend = None
        if pend is not None:
            pend()
        if cap is not None and cache[1] == 0:
            def store(cap=cap, flat=flat, key=key, ckey=ckey):
                self.P.dma("pool", lambda e: e.dma_start(out=cap, in_=flat), key, r=[key], w=[ckey])
            self._pend = store
        return v, key

    def rstd(self, out, in_, mul, r, w):
        self.ts(out, in_, float(mul), EPS, ALU.mult, ALU.add, r=r, w=w)
        self.act(out, out, AF.Sqrt, r=w, w=w)
        self.recip(out, out, r=w, w=w)


class _Stop(Exception):
    pass


def build(NT, DEPTH, stop=99):
    NB = NT // 256
    NKB = NT // 128
    G1 = 1024
    GM = 1024
    GF = 512
    nc = bass.Bass("TRN2", target_bir_lowering=False)

    def din(name, shape, dt=F32):
        return nc.dram_tensor(name, list(shape), dt, kind="ExternalInput").ap()

    x_in = din("x", [NT, D])
    pos_in = din("pos", [64, NT], I32)
    w_in = din("w_in", [DEPTH, D, INC])
    w_krsw = din("w_krsw", [DEPTH, D, 64])
    w_uq = din("w_uq", [DEPTH, 512, 1024])
    w_ukv = din("w_ukv", [DEPTH, 256, 1024])
    w_br = din("w_br", [DEPTH, 4, 512, D])
    w_out = din("w_out", [DEPTH, D, D])
    w_up = din("w_up", [DEPTH, D, 2 * FF])
    w_dn = din("w_dn", [DEPTH, FF, D])
    g1_in = din("g1", [DEPTH, 128, D])
    g2_in = din("g2", [DEPTH, 128, D])
    gf_in = din("gf", [128, D])
    gq_in = din("gq", [DEPTH, 128, 4])
    gkv_in = din("gkv", [DEPTH, 128, 2])
    sub_in = din("subg", [DEPTH, 128, 128])
    lam_in = din("lam", [DEPTH, 128, 256])
    sink_in = din("sinks", [DEPTH, 128, 8])
    cw_in = din("convw", [DEPTH, 128, 3 * 44])
    cb_in = din("convb", [DEPTH, 128, 44])
    td_in = din("td", [128, 16 * 128])
    tsb_in = din("tsb", [128, 16 * 128])
    c31_in = din("c31", [128, 16])
    ident_in = din("ident", [128, 128], BF16)
    ones_in = din("ones", [128, 128], BF16)
    mdiag_in = din("mdiag", [128, 128])
    mswa_in = din("mswa", [128, 128])
    esel_in = din("esel", [16, 16 * 128], BF16)
    elig_in = din("elig", [128, NB * NB])
    keep_in = din("keep", [128, NB * NB])
    invf_in = din("invf", [64, 2])
    out_d = nc.dram_tensor("out", [NT, D], F32, kind="ExternalOutput").ap()

    import os as _os
    _dbg = {"kind": "ExternalOutput"} if _os.environ.get("MKDBG") else {}
    XA = nc.dram_tensor("XA", [NT, D], F32, **_dbg).ap()
    XB = nc.dram_tensor("XB", [NT, D], F32).ap()
    QT = nc.dram_tensor("QT", [QROWS, NT], BF16).ap()
    KT = nc.dram_tensor("KT", [KROWS, NT], BF16).ap()
    VD = nc.dram_tensor("VD", [NT, 4 * 132], BF16).ap()
    VM = nc.dram_tensor("VM", [NT, 4 * 132], BF16).ap()
    VB = nc.dram_tensor("VB", [NT, 4 * 132], BF16).ap()
    VS = nc.dram_tensor("VS", [NT, 2 * 68], BF16).ap()
    GT = nc.dram_tensor("GT", [4 * D, NT], BF16).ap()
    BRT = nc.dram_tensor("BRT", [D, NT], BF16, **_dbg).ap()

    P = Prog(nc)
    with contextlib.ExitStack() as es:
        ARENA = 211000
        arena = es.enter_context(nc.sbuf_tensor("arena", [128, ARENA // 2], BF16))
        ps = es.enter_context(nc.psum_tensor("ps", [128, 4096], F32))
        b = B(nc, P, arena, ps)
        b.arena_bytes = ARENA
        b.wbytes = 22528
        b.wbufs = [b.alloc([b.wbytes // 2], BF16) for _ in range(2)]

        ident = b.alloc([128], BF16)
        ones = b.alloc([128], BF16)
        mdiag = b.alloc([128], F32)
        mswa = b.alloc([128], F32)
        esel = b.alloc([16, 128], BF16)
        elig = b.alloc([NB, NB], F32)
        keep = b.alloc([NB, NB], F32)
        invf = b.alloc([2], F32)
        c31 = b.alloc([16], F32)
        zcol = b.alloc([1], F32)
        exd = b.alloc([16, 128], BF16)
        exs = b.alloc([16, 128], BF16)
        exdm = b.alloc([128], BF16)
        cos2 = b.alloc([NT], BF16)
        sinS = b.alloc([NT], BF16)
        b.ld(ident, ident_in, "ident")
        b.ld(ones, ones_in, "ones")
        b.ld(mdiag, mdiag_in, "mdiag")
        b.ld(mswa, mswa_in, "mswa")
        b.ld(esel[0:16], esel_in.rearrange("p (a b) -> p a b", a=16), "esel")
        b.ld(elig, elig_in.rearrange("p (a b) -> p a b", a=NB), "elig")
        b.ld(keep, keep_in.rearrange("p (a b) -> p a b", a=NB), "keep")
        b.ld(invf[0:64], invf_in, "invf")
        b.ld(c31, c31_in, "c31")
        b.memset(zcol, 0.0, w=["zcol"])
        m0 = b.mark()
        td = b.alloc([16, 128], F32)
        tsb = b.alloc([16, 128], F32)
        tmpb = b.alloc([128], F32)
        b.ld(td, td_in.rearrange("p (a b) -> p a b", a=16), "td")
        b.ld(tsb, tsb_in.rearrange("p (a b) -> p a b", a=16), "tsb")
        SC_D, SC_M, SC_S, SC_B = 64 ** -0.5, 192 ** -0.5, 64 ** -0.5, 128 ** -0.5
        for h in range(16):
            swa = 4 <= h < 12
            sc = SC_D if h < 4 else (SC_S if swa else SC_B)
            ccol = zcol[:, 0:1] if swa else c31[:, h:h + 1]
            b.ts(tmpb, td[:, h, :], ccol, 1.0 / sc, ALU.subtract, ALU.mult, r=["td", "c31", "zcol"], w=["tmpb"])
            b.tt(exd[:, h, :], tmpb, mdiag, ALU.add, r=["tmpb", "mdiag"], w=["exd"])
            b.ts(tmpb, tsb[:, h, :], ccol, 1.0 / sc, ALU.subtract, ALU.mult, r=["tsb", "c31", "zcol"], w=["tmpb"])
            if swa:
                b.tt(exs[:, h, :], tmpb, mswa, ALU.add, r=["tmpb", "mswa"], w=["exs"])
            else:
                b.cp(exs[:, h, :], tmpb, r=["tmpb"], w=["exs"])
        b.cp(exdm, mdiag, r=["mdiag"], w=["exdm"])
        P.barrier()
        b.release(m0)
        m0 = b.mark()
        posi = b.alloc([NT], I32)
        ang = b.alloc([NT], F32)
        kf = b.alloc([NT], F32)
        ki = b.alloc([NT], I32)
        TWO_PI = float(2 * np.pi)
        b.ld(posi[0:64], pos_in, "posi")
        for which in range(2):
            A, KF, KI = ang[0:64], kf[0:64], ki[0:64]
            b.cp(A, posi[0:64], r=["posi"], w=["ang"])
            b.ts(A, A, invf[0:64, 0:1], float(np.pi / 2) if which == 0 else 0.0, ALU.mult, ALU.add,
                 r=["ang", "invf"], w=["ang"])
            b.ts(KF, A, 1.0 / TWO_PI, None, ALU.mult, None, r=["ang"], w=["kf"])
            b.cp(KI, KF, r=["kf"], w=["ki"])
            b.cp(KF, KI, r=["ki"], w=["kf"])
            b.stt(A, KF, -TWO_PI, A, ALU.mult, ALU.add, r=["kf", "ang"], w=["ang"])
            b.ts(KF, A, float(np.pi), None, ALU.is_gt, None, r=["ang"], w=["kf"])
            b.stt(A, KF, -TWO_PI, A, ALU.mult, ALU.add, r=["kf", "ang"], w=["ang"])
            b.ts(KF, A, float(-np.pi), None, ALU.is_lt, None, r=["ang"], w=["kf"])
            b.stt(A, KF, TWO_PI, A, ALU.mult, ALU.add, r=["kf", "ang"], w=["ang"])
            if which == 0:
                b.act(cos2[0:64], A, AF.Sin, r=["ang"], w=["cos2"])
            else:
                b.act(KF, A, AF.Sin, r=["ang"], w=["kf"])
                b.ts(sinS[0:64], KF, invf[0:64, 1:2], None, ALU.mult, None, r=["kf", "invf"], w=["sinS"])
        b.release(m0)
        P.barrier()
        persist = b.mark()

        def norm_tile(xsrc_rows, gb, hT, col0, xt, hj, ssq, xkey, hkey, gkey):
            b.ld(xt, xsrc_rows, xkey)
            b.tt(hj, xt, xt, ALU.mult, r=[xkey], w=[hkey])
            b.red(ssq, hj, ALU.add, r=[hkey], w=["ssq"])
            b.rstd(ssq, ssq, 1.0 / D, r=["ssq"], w=["ssq"])
            b.stt(hj, xt, ssq[:, 0:1], gb, ALU.mult, ALU.mult, r=[xkey, "ssq", gkey], w=[hkey])
            for k4 in range(4):
                bk = b.bank()
                tp = b.pb(bk).bitcast(BF16)
                for j in range(4):
                    kc = k4 * 4 + j
                    b.tr(tp[:, j * 128:(j + 1) * 128], hj[:, kc * 128:(kc + 1) * 128], ident,
                         r=[hkey, "ident"], w=[f"B{bk}"])
                b.act(hT[:, k4 * 4:k4 * 4 + 4, col0:col0 + 128],
                      tp[:, 0:512].rearrange("p (a b) -> p a b", a=4), AF.Copy, r=[f"B{bk}"], w=["hT"])

        def _chk(k):
            if stop <= k:
                raise _Stop()

        try:
          _chk(0)
          for l in range(DEPTH):
            xsrc = x_in if l == 0 else XB
            lam_init = 0.8 - 0.6 * math.exp(-0.3 * l)
            b.release(persist)
            gq = b.alloc([4], F32)
            gkv = b.alloc([2], F32)
            subg = b.alloc([128], F32)
            lamt = b.alloc([256], F32)
            lamc = b.alloc([4], F32)
            esink = b.alloc([8], F32)
            cw = b.alloc([3, 44], F32)
            cbias = b.alloc([44], F32)
            wuq = b.alloc([4, 1024], BF16)
            wukv = b.alloc([2, 1024], BF16)
            b.ld(gq, gq_in[l], "gq")
            b.ld(gkv, gkv_in[l], "gkv")
            b.ld(subg, sub_in[l], "subg")
            b.ld(lamt, lam_in[l], "lamt")
            b.ld(esink, sink_in[l], "esink")
            b.ld(cw, cw_in[l].rearrange("p (a b) -> p a b", a=3), "cw")
            b.ld(cbias, cb_in[l], "cbias")
            P.dma("pool", lambda e, l=l: e.dma_start(out=wuq, in_=w_uq[l].rearrange("(k p) n -> p k n", p=128)),
                  "wuq", w=["wuq"])
            P.dma("pool", lambda e, l=l: e.dma_start(out=wukv, in_=w_ukv[l].rearrange("(k p) n -> p k n", p=128)),
                  "wukv", w=["wukv"])
            b.act(esink, esink, AF.Exp, r=["esink"], w=["esink"])
            b.ts(subg, subg, 1.0 - lam_init, None, ALU.mult, None, r=["subg"], w=["subg"])
            b.tt(lamt[:, 0:64], lamt[:, 0:64], lamt[:, 64:128], ALU.mult, r=["lamt"], w=["lamt"])
            b.tt(lamt[:, 128:192], lamt[:, 128:192], lamt[:, 192:256], ALU.mult, r=["lamt"], w=["lamt"])
            b.red(lamc[:, 0:1], lamt[:, 0:64], ALU.add, r=["lamt"], w=["lamc"])
            b.red(lamc[:, 1:2], lamt[:, 128:192], ALU.add, r=["lamt"], w=["lamc"])
            b.act(lamc[:, 0:2], lamc[:, 0:2], AF.Exp, r=["lamc"], w=["lamc"])
            b.tt(lamc[:, 2:3], lamc[:, 1:2], lamc[:, 0:1], ALU.subtract, r=["lamc"], w=["lamc"])
            b.ts(lamc[:, 2:3], lamc[:, 2:3], -lam_init, None, ALU.add, None, r=["lamc"], w=["lamc"])
            layer_mark = b.mark()

            for g in range(NT // G1):
                t0 = g * G1
                b.release(layer_mark)
                hT = b.alloc([16, G1], BF16)
                g1b = b.alloc([D], F32)
                b.ld(g1b, g1_in[l], "g1b")
                xt = b.alloc([D], F32)
                hj = b.alloc([D], BF16)
                ssq = b.alloc([1], F32)
                stg = [b.alloc([G1], BF16) for _ in range(2)]
                vst = [b.alloc([4, 132], BF16) for _ in range(2)]
                cqraw = b.alloc([4, G1], BF16)
                ckvraw = b.alloc([2, G1], BF16)
                krA = b.alloc([G1], F32)
                krB = b.alloc([G1], F32)
                for i in range(2):
                    b.memset(vst[i], 1.0, w=[f"vst{i}"])
                for t in range(G1 // 128):
                    norm_tile(xsrc[t0 + t * 128:t0 + (t + 1) * 128, :], g1b, hT, t * 128, xt, hj, ssq,
                              "xt", "hj", "g1b")
                sti = [0]

                def fm_block(wv, wkey, c0, nrows, evac):
                    for q in range(G1 // 512):
                        bk = b.bank()
                        for kc in range(16):
                            b.mm(b.pb(bk)[0:nrows], wv[:, kc, c0:c0 + nrows], hT[:, kc, q * 512:(q + 1) * 512],
                                 kc == 0, kc == 15, r=[wkey, "hT"], w=[f"B{bk}"])
                        evac(q, b.pb(bk)[0:nrows], f"B{bk}")

                def to_dram(dst, row0, func=AF.Copy):
                    s = sti[0] % 2
                    sti[0] += 1
                    key = f"stg{s}"

                    def ev(q, pap, bkey):
                        b.act(stg[s][0:pap.shape[0], q * 512:(q + 1) * 512], pap, func, r=[bkey], w=[key])
                        if q == G1 // 512 - 1:
                            n = pap.shape[0]
                            b.st(dst[row0:row0 + n, t0:t0 + G1], stg[s][0:n, :], key, "dram")
                    return ev

                def tm_chunk(wv, wkey, c0, H, dv, dst, padw):
                    for t in range(G1 // 128):
                        bk = b.bank()
                        for kc in range(16):
                            b.mm(b.pb(bk, H * dv), hT[:, kc, t * 128:(t + 1) * 128], wv[:, kc, c0:c0 + H * dv],
                                 kc == 0, kc == 15, r=[wkey, "hT"], w=[f"B{bk}"])
                        s = sti[0] % 2
                        sti[0] += 1
                        vv = vst[s][:, 0:H * padw // 132, :] if padw == 132 else \
                            vst[s].rearrange("p a b -> p (a b)")[:, 0:H * padw].rearrange("p (a b) -> p a b", a=H)
                        b.act(vv[:, :, 0:dv], b.pb(bk, H * dv).rearrange("p (a b) -> p a b", a=H), AF.Copy,
                              r=[f"B{bk}"], w=[f"vst{s}"])
                        b.st(dst[t0 + t * 128:t0 + (t + 1) * 128, :],
                             vv.rearrange("p a b -> p (a b)"), f"vst{s}", "dram")

                wl = w_in[l].rearrange("(k p) n -> p k n", p=128)
                for (c0, dst, r0) in ((AQ, QT, Q_DIFF), (AK, KT, K_DIFF), (SQ, QT, Q_SWA), (MQ, QT, Q_MOBA),
                                      (MK, KT, K_MOBA)):
                    wv, wk = b.wload(wl[:, :, c0:c0 + 512], [16, 512], cache=(f"win{c0}", g))
                    for blk in range(4):
                        fm_block(wv, wk, blk * 128, 128, to_dram(dst, r0 + blk * 128))
                for (c0, dst) in ((AV, VD), (MV, VB)):
                    wv, wk = b.wload(wl[:, :, c0:c0 + 512], [16, 512], cache=(f"win{c0}", g))
                    tm_chunk(wv, wk, 0, 4, 128, dst, 132)
                for i in range(2):
                    b.memset(vst[i], 1.0, w=[f"vst{i}"])
                wv, wk = b.wload(wl[:, :, SK:SK + 256], [16, 256], cache=("winsk", g))
                fm_block(wv, wk, 0, 128, to_dram(KT, K_SWA))
                tm_chunk(wv, wk, 128, 2, 64, VS, 68)
                for i in range(2):
                    b.memset(vst[i], 1.0, w=[f"vst{i}"])
                wv, wk = b.wload(wl[:, :, CQ:CQ + 512], [16, 512], cache=("wincq", g))
                for blk in range(4):
                    def ev(q, pap, bkey, blk=blk):
                        b.act(cqraw[:, blk, q * 512:(q + 1) * 512], pap, AF.Copy, r=[bkey], w=["cqraw"])
                    fm_block(wv, wk, blk * 128, 128, ev)
                wv, wk = b.wload(wl[:, :, CKV:CKV + 320], [16, 320], cache=("winckv", g))
                for blk in range(2):
                    def ev(q, pap, bkey, blk=blk):
                        b.act(ckvraw[:, blk, q * 512:(q + 1) * 512], pap, AF.Copy, r=[bkey], w=["ckvraw"])
                    fm_block(wv, wk, blk * 128, 128, ev)

                def evA(q, pap, bkey):
                    b.act(krA[0:64, q * 512:(q + 1) * 512], pap, AF.Copy, r=[bkey], w=["krA"])
                fm_block(wv, wk, 256, 64, evA)
                wv, wk = b.wload(w_krsw[l].rearrange("(k p) n -> p k n", p=128), [16, 64], cache=("winkrsw", g))

                def evB(q, pap, bkey):
                    b.act(krB[0:64, q * 512:(q + 1) * 512], pap, AF.Copy, r=[bkey], w=["krB"])
                fm_block(wv, wk, 0, 64, evB)
                for ch in range(16):
                    wv, wk = b.wload(wl[:, :, ZG + ch * 512:ZG + (ch + 1) * 512], [16, 512], cache=(f"winzg{ch}", g))
                    for blk in range(4):
                        fm_block(wv, wk, blk * 128, 128, to_dram(GT, ch * 512 + blk * 128, AF.Sigmoid))

                mm0 = b.mark()
                sq = b.alloc([4, G1], BF16)
                rq = b.alloc([G1], F32)
                rkv = b.alloc([G1], F32)
                rtok = b.alloc([G1 // 128], F32)
                t1 = b.alloc([G1], F32)
                t2 = b.alloc([G1], F32)
                csl = slice(t0, t0 + G1)

                def ssq_bc(raw, nblk, rout, rkey, mul):
                    b.tt(sq[:, 0:nblk, :], raw, raw, ALU.mult, r=[rkey], w=["sq"])
                    for q in range(G1 // 512):
                        bk = b.bank()
                        for kb in range(nblk):
                            b.mm(b.pb(bk), ones, sq[:, kb, q * 512:(q + 1) * 512], kb == 0, kb == nblk - 1,
                                 r=["sq", "ones"], w=[f"B{bk}"])
                        b.ts(rout[:, q * 512:(q + 1) * 512], b.pb(bk), mul, EPS, ALU.mult, ALU.add,
                             r=[f"B{bk}"], w=["rr"])
                    b.act(rout, rout, AF.Sqrt, r=["rr"], w=["rr"])
                    b.recip(rout, rout, r=["rr"], w=["rr"])

                ssq_bc(cqraw, 4, rq, "cqraw", 1.0 / 512)
                for kb in range(4):
                    b.ts(cqraw[:, kb, :], cqraw[:, kb, :], gq[:, kb:kb + 1], None, ALU.mult, None,
                         r=["cqraw", "gq", "sq"], w=["cqraw"])
                for h in range(4):
                    s = sti[0] % 2
                    sti[0] += 1
                    for q in range(G1 // 512):
                        bk = b.bank()
                        for kc in range(4):
                            b.mm(b.pb(bk), wuq[:, kc, h * 192:h * 192 + 128], cqraw[:, kc, q * 512:(q + 1) * 512],
                                 kc == 0, kc == 3, r=["wuq", "cqraw"], w=[f"B{bk}"])
                        b.tt(stg[s][:, q * 512:(q + 1) * 512], b.pb(bk), rq[:, q * 512:(q + 1) * 512], ALU.mult,
                             r=[f"B{bk}", "rr"], w=[f"stg{s}"])
                    b.st(QT[Q_MN + h * 128:Q_MN + (h + 1) * 128, csl], stg[s], f"stg{s}", "dram")
                    s = sti[0] % 2
                    sti[0] += 1
                    for q in range(G1 // 512):
                        qs = slice(q * 512, (q + 1) * 512)
                        bka, bkb = b.bank(), b.bank()
                        for kc in range(4):
                            b.mm(b.pb(bka)[0:64], wuq[:, kc, h * 192 + 128:h * 192 + 192], cqraw[:, kc, qs],
                                 kc == 0, kc == 3, r=["wuq", "cqraw"], w=[f"B{bka}"])
                        for kc in range(4):
                            b.mm(b.pb(bkb)[0:64], wuq[:, kc, 768 + h * 64:768 + (h + 1) * 64], cqraw[:, kc, qs],
                                 kc == 0, kc == 3, r=["wuq", "cqraw"], w=[f"B{bkb}"])
                        gs = slice(t0 + q * 512, t0 + (q + 1) * 512)
                        b.tt(t1[0:64, qs], b.pb(bka)[0:64], cos2[0:64, gs], ALU.mult, r=[f"B{bka}", "cos2"], w=["t1"])
                        b.tt(t2[0:64, qs], b.pb(bkb)[0:64], sinS[0:64, gs], ALU.mult, r=[f"B{bkb}", "sinS"], w=["t2"])
                        b.tt(t1[0:64, qs], t1[0:64, qs], t2[0:64, qs], ALU.add, r=["t1", "t2"], w=["t1"])
                        b.tt(stg[s][0:64, qs], t1[0:64, qs], rq[0:64, qs], ALU.mult, r=["t1", "rr"], w=[f"stg{s}"])
                    b.st(QT[Q_MR + h * 64:Q_MR + (h + 1) * 64, csl], stg[s][0:64, :], f"stg{s}", "dram")
                ssq_bc(ckvraw, 2, rkv, "ckvraw", 1.0 / 256)
                bk = b.bank()
                for t in range(G1 // 128):
                    for kb in range(2):
                        b.mm(b.pb(bk)[:, t:t + 1], sq[:, kb, t * 128:(t + 1) * 128], ones[:, 0:1], kb == 0, kb == 1,
                             r=["sq", "ones"], w=[f"B{bk}"])
                b.rstd(rtok, b.pb(bk)[:, 0:G1 // 128], 1.0 / 256, r=[f"B{bk}"], w=["rtok"])
                for kb in range(2):
                    b.ts(ckvraw[:, kb, :], ckvraw[:, kb, :], gkv[:, kb:kb + 1], None, ALU.mult, None,
                         r=["ckvraw", "gkv", "sq"], w=["ckvraw"])
                for h in range(4):
                    s = sti[0] % 2
                    sti[0] += 1
                    for q in range(G1 // 512):
                        bk = b.bank()
                        for kc in range(2):
                            b.mm(b.pb(bk), wukv[:, kc, h * 256:h * 256 + 128], ckvraw[:, kc, q * 512:(q + 1) * 512],
                                 kc == 0, kc == 1, r=["wukv", "ckvraw"], w=[f"B{bk}"])
                        b.tt(stg[s][:, q * 512:(q + 1) * 512], b.pb(bk), rkv[:, q * 512:(q + 1) * 512], ALU.mult,
                             r=[f"B{bk}", "rr"], w=[f"stg{s}"])
                    b.st(KT[K_MN + h * 128:K_MN + (h + 1) * 128, csl], stg[s], f"stg{s}", "dram")
                wv_v = wukv.rearrange("p k (h two c) -> p k h two c", h=4, two=2)
                for t in range(G1 // 128):
                    bk = b.bank()
                    for kc in range(2):
                        b.mm(b.pb(bk).rearrange("p (a b) -> p a b", a=4), ckvraw[:, kc, t * 128:(t + 1) * 128],
                             wv_v[:, kc, :, 1, :], kc == 0, kc == 1, r=["wukv", "ckvraw"], w=[f"B{bk}"])
                    s = sti[0] % 2
                    sti[0] += 1
                    b.ts(vst[s][:, :, 0:128], b.pb(bk).rearrange("p (a b) -> p a b", a=4), rtok[:, t:t + 1], None,
                         ALU.mult, None, r=[f"B{bk}", "rtok"], w=[f"vst{s}"])
                    b.st(VM[t0 + t * 128:t0 + (t + 1) * 128, :], vst[s].rearrange("p a b -> p (a b)"),
                         f"vst{s}", "dram")
                s = sti[0] % 2
                sti[0] += 1
                gsl = slice(t0, t0 + G1)
                b.tt(t1[0:64], krA[0:64], cos2[0:64, gsl], ALU.mult, r=["krA", "cos2"], w=["t1"])
                b.tt(t2[0:64], krB[0:64], sinS[0:64, gsl], ALU.mult, r=["krB", "sinS"], w=["t2"])
                b.tt(stg[s][0:64], t1[0:64], t2[0:64], ALU.add, r=["t1", "t2"], w=[f"stg{s}"])
                b.st(KT[K_MR:K_MR + 64, csl], stg[s][0:64, :], f"stg{s}", "dram")
                b.release(mm0)
                P.barrier()
            _chk(1)

            b.release(layer_mark)
            NQT = NT // 512
            pt = [b.alloc([512], BF16) for _ in range(3)]
            osb = b.alloc([4, 128], F32)
            obf = b.alloc([4, 128], BF16)
            obf2 = [b.alloc([4, 128], BF16) for _ in range(2)]
            rs = b.alloc([8], F32)
            brs = [b.alloc([512], BF16) for _ in range(2)]
            vsb = b.alloc([NKB, 4 * 132], BF16)
            kts = [b.alloc([NT], BF16) for _ in range(2)]
            qts = [b.alloc([NT], BF16) for _ in range(2)]
            ktr = b.alloc([NT], BF16)
            qtr = b.alloc([NT], BF16)
            negT = b.alloc([NT], BF16)
            brc = [0]
            b.ring_set = [0, 1, 2]
            ocnt = [0]

            def obanks():
                ocnt[0] += 1
                return (3, 4) if ocnt[0] % 2 else (5, 6)

            pending = []

            def flush_pending():
                while pending:
                    pending.pop(0)()

            def dense_pass(T, chunks, hb, sc, ccol, exd_t, exs_t, vcol, obank, moba=False):
                nk = 4 * T + 4

                def qk(j):
                    smin = max(0, j - 4 * T)
                    q0 = smin * 128
                    sb_ = b.bank()
                    S = b.pb(sb_)
                    skey = f"B{sb_}"
                    ext = []
                    for s in range(smin, 4):
                        d = 4 * T + s - j
                        if d == 0:
                            ext.append((s, exd_t))
                        elif d == 1 and exs_t is not None:
                            ext.append((s, exs_t))
                    nmm = len(chunks) + len(ext) + (1 if moba else 0)
                    i = 0
                    for (kt_, qt_, rows, kkey, qkey) in chunks:
                        b.mm(S[:, q0:512], kt_[0:rows, j * 128:(j + 1) * 128], qt_[0:rows, T * 512 + q0:T * 512 + 512],
                             i == 0, i == nmm - 1, r=[kkey, qkey], w=[skey])
                        i += 1
                    for (s, ex) in ext:
                        b.mm(S[:, s * 128:(s + 1) * 128], ident, ex, False, i == nmm - 1,
                             r=["ident", "exd", "exs", "exdm"], w=[skey])
                        i += 1
                    if moba:
                        n = j // 2
                        b.mm(S[:, q0:512], esel[0:16, n, :], negT[0:16, T * 512 + q0:T * 512 + 512], False, True,
                             r=["esel", "negT"], w=[skey])
                    return (S, skey, smin, q0)

                infos = {0: qk(0)}
                if nk > 1:
                    infos[1] = qk(1)
                for j in range(nk):
                    if j + 2 < nk:
                        infos[j + 2] = qk(j + 2)
                    S, skey, smin, q0 = infos.pop(j)
                    pi_ = j % 3
                    b.act(pt[pi_][:, q0:512], S[:, q0:512], AF.Exp, r=[skey, "c31", "zcol"], w=[f"pt{pi_}"],
                          bias=ccol, scale=float(sc))
                    for s in range(smin, 4):
                        ob = obank[s // 2]
                        b.mm(b.pb(ob, 129, (s % 2) * 256), pt[pi_][:, s * 128:(s + 1) * 128],
                             vsb[:, j, vcol:vcol + 129], j == 0 and s % 2 == 0, j == 4 * T + s,
                             r=[f"pt{pi_}", "vsb"], w=[f"B{ob}"])
                    if j == 1 or nk == 1:
                        flush_pending()

            def emit_branch(T, row0):
                k = brc[0] % 2
                brc[0] += 1
                b.cp(obf2[k], obf, r=["obf"], w=[f"obf2{k}"])

                def tail(k=k, T=T, row0=row0):
                    bk = 7
                    tp = b.pb(bk).bitcast(BF16)
                    for s in range(4):
                        b.tr(tp[:, s * 128:(s + 1) * 128], obf2[k][:, s, :], ident, r=[f"obf2{k}", "ident"], w=[f"B{bk}"])
                    b.act(brs[k], tp[:, 0:512], AF.Copy, r=[f"B{bk}"], w=[f"brs{k}"])
                    b.st(BRT[row0:row0 + 128, T * 512:(T + 1) * 512], brs[k], f"brs{k}", "dram")
                pending.append(tail)

            def load_rows(dst, src, row0, rows, key):
                b.ld(dst[0:rows, :], src[row0:row0 + rows, :], key)

            b.ld(vsb, VD.rearrange("(j p) c -> p j c", p=128), "vsb")
            for h in range(4):
                for m in range(2):
                    load_rows(kts[m], KT, K_DIFF + (2 * h + m) * 64, 64, f"kts{m}")
                    load_rows(qts[m], QT, Q_DIFF + (2 * h + m) * 64, 64, f"qts{m}")
                for T in range(NQT):
                    for m in range(2):
                        ob = obanks()
                        dense_pass(T, [(kts[m], qts[m], 64, f"kts{m}", f"qts{m}")], h, SC_D, c31[:, h:h + 1],
                                   exd[:, h, :], exs[:, h, :], h * 132, ob)
                        for s in range(4):
                            O = b.pb(ob[s // 2], 129, (s % 2) * 256)
                            okey = f"B{ob[s // 2]}"
                            b.recip(rs[:, s:s + 1], O[:, 128:129], r=[okey], w=["rs"])
                            if m == 0:
                                b.ts(osb[:, s, :], O[:, 0:128], rs[:, s:s + 1], None, ALU.mult, None,
                                     r=[okey, "rs", "obf"], w=["osb"])
                            else:
                                b.tt(rs[:, s:s + 1], rs[:, s:s + 1], lamc[:, 2:3], ALU.mult, r=["rs", "lamc"], w=["rs"])
                                b.stt(osb[:, s, :], O[:, 0:128], rs[:, s:s + 1], osb[:, s, :], ALU.mult, ALU.add,
                                      r=[okey, "rs", "osb"], w=["osb"])
                    b.tt(obf, osb, osb, ALU.mult, r=["osb"], w=["obf"])
                    b.red(rs[:, 4:8], obf, ALU.add, r=["obf"], w=["rs"])
                    b.rstd(rs[:, 4:8], rs[:, 4:8], 1.0 / 128, r=["rs"], w=["rs"])
                    for s in range(4):
                        b.stt(obf[:, s, :], osb[:, s, :], rs[:, 4 + s:5 + s], subg, ALU.mult, ALU.mult,
                              r=["osb", "rs", "subg"], w=["obf"])
                    emit_branch(T, 0 * 512 + h * 128)

            def simple_post(ob, extra_col=None):
                for s in range(4):
                    O = b.pb(ob[s // 2], 129, (s % 2) * 256)
                    okey = f"B{ob[s // 2]}"
                    b.recip(rs[:, s:s + 1], O[:, 128:129], r=[okey], w=["rs"])
                    b.ts(obf[:, s, :], O[:, 0:128], rs[:, s:s + 1], None, ALU.mult, None, r=[okey, "rs"], w=["obf"])

            _chk(1.25)
            b.ld(vsb, VM.rearrange("(j p) c -> p j c", p=128), "vsb")
            load_rows(ktr, KT, K_MR, 64, "ktr")
            for h in range(4):
                load_rows(kts[0], KT, K_MN + h * 128, 128, "kts0")
                load_rows(qts[0], QT, Q_MN + h * 128, 128, "qts0")
                load_rows(qtr, QT, Q_MR + h * 64, 64, "qtr")
                for T in range(NQT):
                    ob = obanks()
                    dense_pass(T, [(kts[0], qts[0], 128, "kts0", "qts0"), (ktr, qtr, 64, "ktr", "qtr")], 0, SC_M,
                               zcol[:, 0:1], exdm, None, h * 132, ob)
                    simple_post(ob)
                    emit_branch(T, 1 * 512 + h * 128)

            flush_pending()
            _chk(1.5)
            b.ld(vsb, VB.rearrange("(j p) c -> p j c", p=128), "vsb")
            kmf = b.alloc([NB], F32)
            kmT = b.alloc([NB], BF16)
            gm = b.alloc([NB], F32)
            gm2 = b.alloc([NB], F32)
            gmk = b.alloc([NB], F32)
            nbf = b.alloc([NB], BF16)
            mx = b.alloc([1], F32)
            for h in range(4):
                load_rows(kts[0], KT, K_MOBA + h * 128, 128, "kts0")
                load_rows(qts[0], QT, Q_MOBA + h * 128, 128, "qts0")
                b.red(kmf, kts[0].rearrange("p (n k) -> p n k", k=256), ALU.add, r=["kts0"], w=["kmf"])
                b.ts(kmT, kmf, 1.0 / 256, None, ALU.mult, None, r=["kmf"], w=["kmT"])
                for i in range(NKB):
                    blk = i // 2
                    bk = 7
                    b.mm(b.pb(bk, NB), qts[0][:, i * 128:(i + 1) * 128], kmT, True, True, r=["qts0", "kmT"],
                         w=[f"B{bk}"])
                    b.tt(gm, b.pb(bk, NB), elig[:, blk, :], ALU.add, r=[f"B{bk}", "elig"], w=["gm"])
                    b.cp(gm2, gm, r=["gm"], w=["gm2"])
                    for it in range(3):
                        b.red(mx, gm2, ALU.max, r=["gm2"], w=["mx"])
                        if it < 2:
                            b.ts(gmk, gm2, mx[:, 0:1], NEG * 1e20, ALU.is_ge, ALU.mult, r=["gm2", "mx"], w=["gmk"])
                            b.tt(gm2, gm2, gmk, ALU.add, r=["gm2", "gmk"], w=["gm2"])
                    b.ts(gmk, gm, mx[:, 0:1], NEG, ALU.is_lt, ALU.mult, r=["gm", "mx"], w=["gmk"])
                    b.tt(nbf, gmk, keep[:, blk, :], ALU.mult, r=["gmk", "keep"], w=["nbf"])
                    bk2 = 7
                    tp = b.pb(bk2).bitcast(BF16)
                    b.tr(tp[0:NB, 0:128], nbf, ident, r=["nbf", "ident"], w=[f"B{bk2}"])
                    b.act(negT[0:NB, i * 128:(i + 1) * 128], tp[0:NB, 0:128], AF.Copy, r=[f"B{bk2}"], w=["negT"])
                for T in range(NQT):
                    ob = obanks()
                    hb = 12 + h
                    dense_pass(T, [(kts[0], qts[0], 128, "kts0", "qts0")], hb, SC_B, c31[:, hb:hb + 1],
                               exd[:, hb, :], exs[:, hb, :], h * 132, ob, moba=True)
                    simple_post(ob)
                    emit_branch(T, 3 * 512 + h * 128)

            flush_pending()
            _chk(1.75)
            vss = vsb.rearrange("p j c -> p (j c)")[:, 0:NKB * 136].rearrange("p (j c) -> p j c", c=136)
            b.ld(vss, VS.rearrange("(j p) c -> p j c", p=128), "vsb")
            for pr in range(4):
                kvh = (2 * pr) // 4
                load_rows(kts[0], KT, K_SWA + kvh * 64, 64, "kts0")
                for hh in range(2):
                    load_rows(qts[hh], QT, Q_SWA + (2 * pr + hh) * 64, 64, f"qts{hh}")
                for i in range(NKB):
                    for hh in range(2):
                        hq = 2 * pr + hh
                        hb = 4 + hq
                        sb_ = b.bank()
                        S = b.pb(sb_)
                        skey = f"B{sb_}"
                        qsl = qts[hh][0:64, i * 128:(i + 1) * 128]
                        lo = 0 if i > 0 else 1
                        for kk in range(lo, 2):
                            kb = i - 1 + kk
                            b.mm(S[:, kk * 128:(kk + 1) * 128], kts[0][0:64, kb * 128:(kb + 1) * 128], qsl, True, False,
                                 r=["kts0", f"qts{hh}"], w=[skey])
                            b.mm(S[:, kk * 128:(kk + 1) * 128], ident, (exs if kk == 0 else exd)[:, hb, :], False, True,
                                 r=["ident", "exd", "exs"], w=[skey])
                        pi_ = (i * 2 + hh) % 3
                        b.act(pt[pi_][:, lo * 128:256], S[:, lo * 128:256], AF.Exp, r=[skey, "zcol"], w=[f"pt{pi_}"],
                              bias=zcol[:, 0:1], scale=float(SC_S))
                        ob = 3 + (i * 2 + hh) % 4
                        O = b.pb(ob, 65)
                        for kk in range(lo, 2):
                            kb = i - 1 + kk
                            b.mm(O, pt[pi_][:, kk * 128:(kk + 1) * 128], vss[:, kb, kvh * 68:kvh * 68 + 65], kk == lo,
                                 kk == 1, r=[f"pt{pi_}", "vsb"], w=[f"B{ob}"])
                        b.tt(rs[:, hh:hh + 1], O[:, 64:65], esink[:, hq:hq + 1], ALU.add, r=[f"B{ob}", "esink"], w=["rs"])
                        b.recip(rs[:, hh:hh + 1], rs[:, hh:hh + 1], r=["rs"], w=["rs"])
                        b.ts(obf[:, i % 4, hh * 64:(hh + 1) * 64], O[:, 0:64], rs[:, hh:hh + 1], None, ALU.mult, None,
                             r=[f"B{ob}", "rs"], w=["obf"])
                    if i % 4 == 3:
                        emit_branch(i // 4, 2 * 512 + pr * 128)
                        flush_pending()
            flush_pending()
            P.barrier()
            b.ring_set = list(range(8))
            _chk(2)

            b.release(layer_mark)
            brT = b.alloc([16, GM], BF16)
            mT = b.alloc([16, GM], BF16)
            gts = [b.alloc([4, GM], BF16) for _ in range(2)]
            acc = b.alloc([512], F32)
            tmp = b.alloc([512], F32)
            xts = [b.alloc([512], F32) for _ in range(2)]
            for g in range(NT // GM):
                t0 = g * GM
                b.ld(brT, BRT[:, t0:t0 + GM].rearrange("(k p) t -> p k t", p=128), "brT")
                gi = 0
                for cc in range(4):
                    wv, wk = b.wload(w_br[l].rearrange("i (k p) n -> p (i k) n", p=128)[:, :, cc * 512:(cc + 1) * 512],
                                     [16, 512], cache=(f"wbr{cc}", g))
                    for cb in range(4):
                        c = cc * 4 + cb
                        gsel = gi % 2
                        gi += 1
                        gt_ = gts[gsel]
                        gkey = f"gts{gsel}"
                        P.dma("sp", lambda e, gt_=gt_, c=c, t0=t0: e.dma_start(
                            out=gt_, in_=GT.rearrange("(i r) t -> r i t", i=4)[c * 128:(c + 1) * 128, :, t0:t0 + GM]),
                            gkey, w=[gkey])
                        for hf in range(GM // 512):
                            hs = slice(hf * 512, (hf + 1) * 512)
                            bks = [b.bank() for _ in range(4)]
                            for i in range(4):
                                for kc in range(4):
                                    b.mm(b.pb(bks[i]), wv[:, i * 4 + kc, cb * 128:(cb + 1) * 128], brT[:, i * 4 + kc, hs],
                                         kc == 0, kc == 3, r=[wk, "brT"], w=[f"B{bks[i]}"])
                            b.tt(acc, b.pb(bks[0]), gt_[:, 0, hs], ALU.mult, r=[f"B{bks[0]}", gkey], w=["acc"])
                            for i in range(1, 4):
                                b.tt(tmp, b.pb(bks[i]), gt_[:, i, hs], ALU.mult, r=[f"B{bks[i]}", gkey], w=["tmp"])
                                if i < 3:
                                    b.tt(acc, acc, tmp, ALU.add, r=["acc", "tmp"], w=["acc"])
                                else:
                                    b.tt(mT[:, c, hs], acc, tmp, ALU.add, r=["acc", "tmp"], w=["mT"])
                xi = 0
                for cc in range(4):
                    cs = slice(cc * 512, (cc + 1) * 512)
                    wv, wk = b.wload(w_out[l].rearrange("(k p) n -> p k n", p=128)[:, :, cs], [16, 512], cache=(f"wout{cc}", g))
                    for t in range(GM // 128):
                        rows = slice(t0 + t * 128, t0 + (t + 1) * 128)
                        k = xi % 2
                        xi += 1
                        b.ld(xts[k], xsrc[rows, cs], f"xts{k}", r=["dramx"])
                        bk = b.bank()
                        for kc in range(16):
                            b.mm(b.pb(bk), mT[:, kc, t * 128:(t + 1) * 128], wv[:, kc, :], kc == 0, kc == 15,
                                 r=[wk, "mT"], w=[f"B{bk}"])
                        b.tt(xts[k], xts[k], b.pb(bk), ALU.add, r=[f"xts{k}", f"B{bk}"], w=[f"xts{k}"])
                        b.st(XA[rows, cs], xts[k], f"xts{k}", "dramxa")
            P.barrier()
            _chk(3)

            b.release(layer_mark)
            h2T = b.alloc([16, GF + 2], BF16)
            g2b = b.alloc([D], F32)
            b.ld(g2b, g2_in[l], "g2b")
            gT = b.alloc([44, GF], BF16)
            xt = b.alloc([D], F32)
            hj = b.alloc([D], BF16)
            ssq = b.alloc([1], F32)
            asb = b.alloc([GF + 2], F32)
            c0t = b.alloc([GF], F32)
            c1t = b.alloc([GF], F32)
            xts = [b.alloc([256], F32) for _ in range(2)]
            for g in range(NT // GF):
                t0 = g * GF
                if g == 0:
                    b.memset(h2T[:, :, 0:2], 0.0, w=["hT"])
                else:
                    b.cp(h2T[:, :, 0:2], h2T[:, :, GF:GF + 2], r=["hT"], w=["hT"])
                for t in range(GF // 128):
                    norm_tile(XA[t0 + t * 128:t0 + (t + 1) * 128, :], g2b, h2T, 2 + t * 128, xt, hj, ssq,
                              "xt", "hj", "g2b")
                wu = w_up[l].rearrange("(k p) (two f) -> p k two f", p=128, two=2)
                for cp_ in range(22):
                    wv, wk = b.wload([wu[:, :, 0, cp_ * 256:(cp_ + 1) * 256], wu[:, :, 1, cp_ * 256:(cp_ + 1) * 256]],
                                     [16, 2, 256], cache=(f"wup{cp_}", g))
                    for c2 in range(2):
                        cb = cp_ * 2 + c2
                        cs = slice(c2 * 128, (c2 + 1) * 128)
                        ba, bh, bv = b.bank(), b.bank(), b.bank()
                        for kc in range(16):
                            b.mm(b.pb(ba), wv[:, kc, 0, cs], h2T[:, kc, 2:GF + 2], kc == 0, kc == 15, r=[wk, "hT"],
                                 w=[f"B{ba}"])
                        for kc in range(16):
                            b.mm(b.pb(bh, 2), wv[:, kc, 0, cs], h2T[:, kc, 0:2], kc == 0, kc == 15, r=[wk, "hT"],
                                 w=[f"B{bh}"])
                        for kc in range(16):
                            b.mm(b.pb(bv), wv[:, kc, 1, cs], h2T[:, kc, 2:GF + 2], kc == 0, kc == 15, r=[wk, "hT"],
                                 w=[f"B{bv}"])
                        b.act(asb[:, 0:2], b.pb(bh, 2), AF.Copy, r=[f"B{bh}"], w=["asb"])
                        b.act(asb[:, 2:GF + 2], b.pb(ba), AF.Copy, r=[f"B{ba}"], w=["asb"])
                        b.act(c0t, asb[:, 2:GF + 2], AF.Identity, r=["asb", "cw", "cbias"], w=["c0t"],
                              bias=cbias[:, cb:cb + 1], scale=cw[:, 2, cb:cb + 1])
                        b.stt(c1t, asb[:, 1:GF + 1], cw[:, 1, cb:cb + 1], c0t, ALU.mult, ALU.add,
                              r=["asb", "cw", "c0t"], w=["c1t"])
                        b.stt(c0t, asb[:, 0:GF], cw[:, 0, cb:cb + 1], c1t, ALU.mult, ALU.add,
                              r=["asb", "cw", "c1t"], w=["c0t"])
                        b.act(c1t, c0t, AF.Gelu, r=["c0t"], w=["c1t"])
                        b.tt(gT[:, cb, :], c1t, b.pb(bv), ALU.mult, r=["c1t", f"B{bv}"], w=["gT"])
                xi = 0
                for cc in range(8):
                    cs = slice(cc * 256, (cc + 1) * 256)
                    wv, wk = b.wload(w_dn[l].rearrange("(k p) n -> p k n", p=128)[:, :, cs], [44, 256], cache=(f"wdn{cc}", g))
                    for t in range(GF // 128):
                        rows = slice(t0 + t * 128, t0 + (t + 1) * 128)
                        k = xi % 2
                        xi += 1
                        b.ld(xts[k], XA[rows, cs], f"xts{k}")
                        bk = b.bank()
                        for kc in range(44):
                            b.mm(b.pb(bk, 256), gT[:, kc, t * 128:(t + 1) * 128], wv[:, kc, :], kc == 0, kc == 43,
                                 r=[wk, "gT"], w=[f"B{bk}"])
                        b.tt(xts[k], xts[k], b.pb(bk, 256), ALU.add, r=[f"xts{k}", f"B{bk}"], w=[f"xts{k}"])
                        b.st(XB[rows, cs], xts[k], f"xts{k}", "dramxb")
            P.barrier()

        except _Stop:
            pass
        b.release(persist)
        gfb = b.alloc([D], F32)
        xtf = [b.alloc([D], F32) for _ in range(2)]
        hjf = b.alloc([D], F32)
        ssq = b.alloc([1], F32)
        b.ld(gfb, gf_in, "gfb")
        finals = []
        for t in range(NT // 128):
            k = t % 2
            rows = slice(t * 128, (t + 1) * 128)
            b.ld(xtf[k], XB[rows, :], f"xtf{k}")
            b.tt(hjf, xtf[k], xtf[k], ALU.mult, r=[f"xtf{k}"], w=["hjf"])
            b.red(ssq, hjf, ALU.add, r=["hjf"], w=["ssq"])
            b.rstd(ssq, ssq, 1.0 / D, r=["ssq"], w=["ssq"])
            b.stt(xtf[k], xtf[k], ssq[:, 0:1], gfb, ALU.mult, ALU.mult, r=[f"xtf{k}", "ssq", "gfb"], w=[f"xtf{k}"])
            finals.append(b.st(out_d[rows, :], xtf[k], f"xtf{k}", "dramout"))
        P.emit(finals[-2:] if stop >= 99 else list(P.last_dma.values()))
    return nc


def _t5_bucket(n):
    n = np.maximum(n, 0)
    nf = np.maximum(n, 1).astype(np.float32)
    large = 16 + (np.log(nf / np.float32(16)) / np.float32(math.log(8)) * np.float32(16)).astype(np.int32)
    return np.where(n < 16, n, np.minimum(large, 31))


def host_consts(NT):
    NB = NT // 256
    kk = np.arange(128)[:, None]
    qq = np.arange(128)[None, :]
    c = {}
    c["ident"] = np.eye(128, dtype=np.float32).astype(ml_dtypes.bfloat16)
    c["ones"] = np.ones((128, 128), dtype=np.float32).astype(ml_dtypes.bfloat16)
    c["mdiag"] = np.where(qq >= kk, 0.0, NEG * 8).astype(np.float32)
    c["mswa"] = np.where(qq < kk, 0.0, NEG * 8).astype(np.float32)
    es = np.zeros((16, 16, 128), dtype=np.float32)
    for n in range(16):
        es[n, n, :] = 1.0
    c["esel"] = es.reshape(16, 16 * 128).astype(ml_dtypes.bfloat16)
    el = np.zeros((128, NB, NB), dtype=np.float32)
    kp = np.zeros((128, NB, NB), dtype=np.float32)
    for bq in range(NB):
        el[:, bq, bq:] = -1e30
        kp[:, bq, :bq] = 1.0
    c["elig"] = el.reshape(128, NB * NB)
    c["keep"] = kp.reshape(128, NB * NB)
    inv = (10000.0 ** (-np.arange(0, 64, 2, dtype=np.float32) / np.float32(64))).astype(np.float32)
    invf = np.zeros((64, 2), dtype=np.float32)
    invf[:, 0] = np.concatenate([inv, inv])
    invf[:, 1] = np.concatenate([-np.ones(32), np.ones(32)])
    c["invf"] = invf
    c["_bd"] = _t5_bucket(qq - kk)
    c["_bs"] = _t5_bucket(128 + qq - kk)
    return c


def prep_inputs(inp, NT, DEPTH, b):
    c = host_consts(NT)
    rb = np.asarray(inp["rel_bias"], dtype=np.float32)
    m = {}
    m["x"] = np.ascontiguousarray(inp["x"][b])
    m["pos"] = np.ascontiguousarray(np.broadcast_to(np.asarray(inp["positions"], dtype=np.int32)[None, :NT], (64, NT)))
    m["td"] = np.ascontiguousarray(rb[c["_bd"]].transpose(0, 2, 1)).reshape(128, 16 * 128)
    m["tsb"] = np.ascontiguousarray(rb[c["_bs"]].transpose(0, 2, 1)).reshape(128, 16 * 128)
    m["c31"] = np.ascontiguousarray(np.broadcast_to(rb[31:32, :], (128, 16)))
    for k in ("ident", "ones", "mdiag", "mswa", "esel", "elig", "keep", "invf"):
        m[k] = c[k]
    return m


_SHARED = {}


def rep128(a):
    a = np.asarray(a, dtype=np.float32)
    return np.ascontiguousarray(np.broadcast_to(a[:, None, :], (a.shape[0], 128, a.shape[1])))


def shared_inputs(inp, DEPTH):
    w_in = np.asarray(inp["w_in"], dtype=np.float32)
    s = {}
    s["w_in"] = w_in
    s["w_krsw"] = np.ascontiguousarray(np.concatenate([w_in[:, :, KR + 32:KR + 64], w_in[:, :, KR:KR + 32]], axis=2))
    wuq = np.asarray(inp["mla_w_uq"], dtype=np.float32)
    sw = [np.concatenate([wuq[:, :, h * 192 + 160:h * 192 + 192], wuq[:, :, h * 192 + 128:h * 192 + 160]], axis=2)
          for h in range(4)]
    s["w_uq"] = np.ascontiguousarray(np.concatenate([wuq] + sw, axis=2))
    s["w_ukv"] = np.asarray(inp["mla_w_ukv"], dtype=np.float32)
    s["w_br"] = np.asarray(inp["w_branch"], dtype=np.float32)
    s["w_out"] = np.asarray(inp["w_out"], dtype=np.float32)
    s["w_up"] = np.asarray(inp["ffn_w_up"], dtype=np.float32)
    s["w_dn"] = np.asarray(inp["ffn_w_down"], dtype=np.float32)
    s["g1"] = rep128(inp["norm1_g"])
    s["g2"] = rep128(inp["norm2_g"])
    s["gf"] = np.ascontiguousarray(np.broadcast_to(np.asarray(inp["final_norm_g"], dtype=np.float32)[None, :], (128, D)))
    s["gq"] = np.ascontiguousarray(np.asarray(inp["mla_q_norm_g"], dtype=np.float32).reshape(DEPTH, 4, 128).transpose(0, 2, 1))
    s["gkv"] = np.ascontiguousarray(np.asarray(inp["mla_kv_norm_g"], dtype=np.float32).reshape(DEPTH, 2, 128).transpose(0, 2, 1))
    s["subg"] = rep128(inp["diff_subln_g"])
    s["lam"] = rep128(np.asarray(inp["diff_lambda"], dtype=np.float32).reshape(DEPTH, 256))
    s["sinks"] = rep128(inp["swa_sinks"])
    cw = np.asarray(inp["ffn_conv_w"], dtype=np.float32).reshape(DEPTH, 3, 44, 128)
    s["convw"] = np.ascontiguousarray(cw.transpose(0, 3, 1, 2)).reshape(DEPTH, 128, 3 * 44)
    cb = np.asarray(inp["ffn_conv_b"], dtype=np.float32).reshape(DEPTH, 44, 128)
    s["convb"] = np.ascontiguousarray(cb.transpose(0, 2, 1))
    return s


def run(inp, NT, DEPTH, BATCH, n_cores=8, stop=99):
    nc = build(NT, DEPTH, stop)
    sh = shared_inputs(inp, DEPTH)
    in_maps = []
    if n_cores >= 2 * BATCH:
        active = {2 * bi: bi for bi in range(BATCH)}
    else:
        active = {bi: bi for bi in range(BATCH)}
    zmap = None
    for c in range(n_cores):
        if c in active:
            m = dict(sh)
            m.update(prep_inputs(inp, NT, DEPTH, active[c]))
        else:
            if zmap is None:
                ref = dict(sh)
                ref.update(prep_inputs(inp, NT, DEPTH, 0))
                zmap = {k: np.zeros_like(v) for k, v in ref.items()}
            m = zmap
        in_maps.append(m)
    res = run_bass_kernel_spmd(nc, in_maps, core_ids=list(range(n_cores)))
    global LAST
    LAST = res.results
    inv = {bi: c for c, bi in active.items()}
    return np.stack([res.results[inv[bi]]["out"] for bi in range(BATCH)], axis=0)


def kernel(**inputs):
    inp = {k: np.asarray(v) for k, v in inputs.items()}
    return run(inp, 4096, 4, 4).astype(np.float32)
```
